# Optimizing a Trainium2 kernel written in Bass

```python
import math
import jax, jax.numpy as jnp
from jax import lax
import numpy as np

D_MODEL = 1024
BATCH = 8
SEQ = 8192
DEPTH = 4

N_MIXERS = 4
MIX_WIDTH = D_MODEL
GROUP_WIDTH = MIX_WIDTH // N_MIXERS
SC_HEAD_DIM = 64
SC_TAPS = 3
POOL_WINDOWS = (2, 4, 8, 16)
POOL_GROUP = GROUP_WIDTH // len(POOL_WINDOWS)
CF_TAPS = 31
SSM_CH = 16
SSM_GROUPS = GROUP_WIDTH // SSM_CH
SSM_STATE = 64
SSM_DT_MIN = 1e-3
SSM_DT_MAX = 1e-1
D_FF = -(-(8 * D_MODEL) // (3 * 256)) * 256
IN_SIZES = (GROUP_WIDTH, GROUP_WIDTH, GROUP_WIDTH, GROUP_WIDTH, GROUP_WIDTH, GROUP_WIDTH, GROUP_WIDTH)
IN_WIDTH = sum(IN_SIZES)
IN_SPLITS = tuple(int(v) for v in np.cumsum(IN_SIZES)[:-1])
ALPHA = (2 * DEPTH) ** 0.25
BETA = (8 * DEPTH) ** -0.25
LN_EPS = 1e-5

kernel_name = "hybrid_parallel_mixer_trunk"


def layer_norm(x, g, b):
    xf = x.astype(jnp.float32)
    mu = jnp.mean(xf, axis=-1, keepdims=True)
    var = jnp.mean(jnp.square(xf - mu), axis=-1, keepdims=True)
    y = (xf - mu) * lax.rsqrt(var + LN_EPS) * g.astype(jnp.float32) + b.astype(jnp.float32)
    return y.astype(x.dtype)


def causal_dwconv(x, w):
    k, ch = w.shape
    return lax.conv_general_dilated(
        x, w[:, None, :].astype(x.dtype), window_strides=(1,), padding=[(k - 1, 0)],
        dimension_numbers=('NWC', 'WIO', 'NWC'), feature_group_count=ch)


def pool_mixer(z, w_pool, scale):
    zf = z.astype(jnp.float32)
    s = z.shape[1]
    pos = jnp.arange(1, s + 1, dtype=jnp.float32)[None, :, None]
    outs = []
    for i, w in enumerate(POOL_WINDOWS):
        zg = zf[..., i * POOL_GROUP:(i + 1) * POOL_GROUP]
        cs = jnp.cumsum(zg, axis=1)
        lagged = jnp.pad(cs, ((0, 0), (w, 0), (0, 0)))[:, :s]
        mean = (cs - lagged) / jnp.minimum(pos, float(w))
        outs.append(jnp.einsum('bsc,cd->bsd', mean - zg, w_pool[i].astype(jnp.float32)))
    return (jnp.concatenate(outs, axis=-1) * scale.astype(jnp.float32)).astype(z.dtype)


def conformer_conv(z, dw_w, dw_b, ln_g, ln_b):
    val, gate = jnp.split(z, 2, axis=-1)
    h = val * jax.nn.sigmoid(gate)
    h = causal_dwconv(h, dw_w) + dw_b
    h = layer_norm(h, ln_g, ln_b)
    return jax.nn.silu(h)


def ssm_mixer(u, lam_re, lam_im, log_dt, b_re, b_im, c_re, c_im, d, w_glu, b_glu):
    bsz, s, _ = u.shape
    f32 = jnp.float32
    uf = u.astype(f32).reshape(bsz, s, SSM_GROUPS, SSM_CH)
    lam = lax.complex(lam_re.astype(f32), lam_im.astype(f32))
    dt = jnp.exp(log_dt.astype(f32))[:, None]
    lam_bar = jnp.exp(lam * dt)
    bmat = lax.complex(b_re.astype(f32), b_im.astype(f32))
    b_bar = ((lam_bar - 1.0) / lam)[..., None] * bmat
    bu = jnp.einsum('gph,bsgh->bsgp', b_bar, uf)
    a = jnp.broadcast_to(lam_bar, (1, s) + lam_bar.shape)

    def combine(e1, e2):
        a1, b1 = e1
        a2, b2 = e2
        return a1 * a2, a2 * b1 + b2

    _, states = lax.associative_scan(combine, (a, bu), axis=1)
    cmat = lax.complex(c_re.astype(f32), c_im.astype(f32))
    y = jnp.real(jnp.einsum('ghp,bsgp->bsgh', cmat, states)) + d.astype(f32).reshape(SSM_GROUPS, SSM_CH) * uf
    y = y.reshape(bsz, s, GROUP_WIDTH)
    yg = jax.nn.gelu(y)
    out = yg * jax.nn.sigmoid(yg @ w_glu.astype(f32) + b_glu.astype(f32))
    return out.astype(u.dtype)


def setup_inputs(seed: int = 0) -> dict:
    key = jax.random.key(seed)
    ks = jax.random.split(key, 32)
    L = DEPTH
    f32 = jnp.float32

    def nrm(k, shape, scale):
        return jax.random.normal(k, shape, f32) * scale

    lam_im_base = jnp.pi * jnp.arange(SSM_STATE, dtype=f32)
    return {
        "x": nrm(ks[0], (BATCH, SEQ, D_MODEL), 1.0),
        "c": nrm(ks[1], (BATCH, D_MODEL), 1.0),
        "w_ada": nrm(ks[2], (L, D_MODEL, 6 * D_MODEL), 0.1 * D_MODEL ** -0.5),
        "b_ada": nrm(ks[3], (L, 6 * D_MODEL), 0.01),
        "w_in": nrm(ks[4], (L, D_MODEL, IN_WIDTH), D_MODEL ** -0.5),
        "b_in": nrm(ks[5], (L, IN_WIDTH), 0.01),
        "sc_w": nrm(ks[6], (L, SC_TAPS, GROUP_WIDTH), SC_TAPS ** -0.5),
        "pool_w": nrm(ks[7], (L, len(POOL_WINDOWS), POOL_GROUP, POOL_GROUP), POOL_GROUP ** -0.5),
        "pool_scale": 1.0 + nrm(ks[8], (L, GROUP_WIDTH), 0.02),
        "cf_dw_w": nrm(ks[9], (L, CF_TAPS, GROUP_WIDTH), CF_TAPS ** -0.5),
        "cf_dw_b": nrm(ks[10], (L, GROUP_WIDTH), 0.01),
        "cf_ln_g": 1.0 + nrm(ks[11], (L, GROUP_WIDTH), 0.02),
        "cf_ln_b": nrm(ks[12], (L, GROUP_WIDTH), 0.01),
        "ssm_lam_re": -0.5 + nrm(ks[13], (L, SSM_GROUPS, SSM_STATE), 0.01),
        "ssm_lam_im": lam_im_base + nrm(ks[14], (L, SSM_GROUPS, SSM_STATE), 0.01),
        "ssm_log_dt": jax.random.uniform(ks[15], (L, SSM_GROUPS), f32, math.log(SSM_DT_MIN), math.log(SSM_DT_MAX)),
        "ssm_b_re": nrm(ks[16], (L, SSM_GROUPS, SSM_STATE, SSM_CH), (2 * SSM_CH) ** -0.5),
        "ssm_b_im": nrm(ks[17], (L, SSM_GROUPS, SSM_STATE, SSM_CH), (2 * SSM_CH) ** -0.5),
        "ssm_c_re": nrm(ks[18], (L, SSM_GROUPS, SSM_CH, SSM_STATE), SSM_STATE ** -0.5),
        "ssm_c_im": nrm(ks[19], (L, SSM_GROUPS, SSM_CH, SSM_STATE), SSM_STATE ** -0.5),
        "ssm_d": nrm(ks[20], (L, GROUP_WIDTH), 1.0),
        "ssm_w_glu": nrm(ks[21], (L, GROUP_WIDTH, GROUP_WIDTH), GROUP_WIDTH ** -0.5),
        "ssm_b_glu": nrm(ks[22], (L, GROUP_WIDTH), 0.01),
        "w_o": nrm(ks[23], (L, MIX_WIDTH, D_MODEL), BETA * MIX_WIDTH ** -0.5),
        "ln1_g": 1.0 + nrm(ks[24], (L, D_MODEL), 0.02),
        "ln1_b": nrm(ks[25], (L, D_MODEL), 0.01),
        "w_gate": nrm(ks[26], (L, D_MODEL, D_FF), D_MODEL ** -0.5),
        "w_up": nrm(ks[27], (L, D_MODEL, D_FF), D_MODEL ** -0.5),
        "w_down": nrm(ks[28], (L, D_FF, D_MODEL), BETA * D_FF ** -0.5),
        "ln2_g": 1.0 + nrm(ks[29], (L, D_MODEL), 0.02),
        "ln2_b": nrm(ks[30], (L, D_MODEL), 0.01),
    }


def reference(x, c, w_ada, b_ada, w_in, b_in, sc_w, pool_w, pool_scale, cf_dw_w, cf_dw_b, cf_ln_g, cf_ln_b,
              ssm_lam_re, ssm_lam_im, ssm_log_dt, ssm_b_re, ssm_b_im, ssm_c_re, ssm_c_im, ssm_d,
              ssm_w_glu, ssm_b_glu, w_o, ln1_g, ln1_b, w_gate, w_up, w_down, ln2_g, ln2_b):
    cond = jax.nn.silu(c)
    for l in range(DEPTH):
        mod = cond @ w_ada[l] + b_ada[l]
        sh1, sc1, g1, sh2, sc2, g2 = [m[:, None, :] for m in jnp.split(mod, 6, axis=-1)]

        h = x * (1 + sc1) + sh1
        z = h @ w_in[l] + b_in[l]
        z_h, z_b, z_c, z_p, z_cv, z_cg, z_s = jnp.split(z, IN_SPLITS, axis=-1)
        y_a = z_b * causal_dwconv(z_c * z_h, sc_w[l])
        y_b = pool_mixer(z_p, pool_w[l], pool_scale[l])
        y_c = conformer_conv(jnp.concatenate([z_cv, z_cg], axis=-1),
                             cf_dw_w[l], cf_dw_b[l], cf_ln_g[l], cf_ln_b[l])
        y_d = ssm_mixer(z_s, ssm_lam_re[l], ssm_lam_im[l], ssm_log_dt[l], ssm_b_re[l], ssm_b_im[l],
                        ssm_c_re[l], ssm_c_im[l], ssm_d[l], ssm_w_glu[l], ssm_b_glu[l])
        y = jnp.concatenate([y_a, y_b, y_c, y_d], axis=-1) @ w_o[l]
        x = layer_norm(ALPHA * x + (1 + g1) * y, ln1_g[l], ln1_b[l])

        h = x * (1 + sc2) + sh2
        f = (jax.nn.silu(h @ w_gate[l]) * (h @ w_up[l])) @ w_down[l]
        x = layer_norm(ALPHA * x + (1 + g2) * f, ln2_g[l], ln2_b[l])
    return x
```

```python
import numpy as np
from contextlib import ExitStack
import concourse.bass as bass
import concourse.mybir as mybir
from concourse.bass_utils import run_bass_kernel_spmd

F32 = mybir.dt.float32
BF16 = mybir.dt.bfloat16
I32 = mybir.dt.int32
AF = mybir.ActivationFunctionType
ALU = mybir.AluOpType

D = 1024
KC = 8
INW = 1792
ZC = 14
DFF = 2816
FC = 22
T = 512
LN_EPS = 1e-5
SL = 8
NJ = T // SL
NRS = 6
PI = float(np.pi)
ENGS = ("tensor", "vector", "scalar", "gpsimd", "sync")
ARENA_WORDS = 53200


class Tracker:
    def __init__(self, nc, es):
        self.nc = nc
        self.es = es
        self.streams = {e: [] for e in ENGS}
        self.sems = {}
        self.cnt = {}
        self.last_w = {}
        self.readers = {}
        self.waited = {e: {} for e in ENGS}
        for e in ENGS:
            self._sem("E_" + e)

    def _sem(self, key):
        if key not in self.sems:
            self.sems[key] = self.es.enter_context(self.nc.semaphore("s_" + key))
            self.cnt[key] = 0
        return key

    def op(self, eng, fn, reads=(), writes=(), dma=None, sig=True):
        deps = {}

        def add(d):
            if d is not None and deps.get(d[0], 0) < d[1]:
                deps[d[0]] = d[1]

        for b in reads:
            add(self.last_w.get(b))
        for b in writes:
            add(self.last_w.get(b))
            for r in self.readers.get(b, ()):
                add(r)
        if dma is not None:
            if dma == "c0":
                self.misc_rr = (getattr(self, "misc_rr", -1) + 1) % 8
                dma = f"c{self.misc_rr}"
            skey = self._sem("D_" + dma)
            inc = 16
            if self.cnt[skey] > 0:
                add((skey, self.cnt[skey]))
        else:
            skey = "E_" + eng
            inc = 1
        own = "E_" + eng
        waits = []
        for s, v in deps.items():
            if eng == "tensor" and s == own and dma is None:
                continue
            if self.waited[eng].get(s, 0) >= v:
                continue
            self.waited[eng][s] = v
            waits.append((s, v))
        if sig:
            self.cnt[skey] += inc
            me = (skey, self.cnt[skey])
        else:
            me = (skey, self.cnt[skey] + inc)
        self.streams[eng].append((waits, fn, skey if sig else None, inc))
        for b in reads:
            self.readers.setdefault(b, []).append(me)
        for b in writes:
            self.last_w[b] = me
            self.readers[b] = []
        return me

    def barrier(self):
        for e in ENGS:
            waits = []
            for s, v in self.cnt.items():
                if v > 0 and self.waited[e].get(s, 0) < v and not (s == "E_" + e and e in ("tensor",)):
                    self.waited[e][s] = v
                    waits.append((s, v))
            if waits:
                self.streams[e].append((waits, None, None, 0))
        self.last_w = {}
        self.readers = {}

    def emit(self):
        nc = self.nc
        sems = self.sems
        streams = self.streams
        finals = [(k, v) for k, v in self.cnt.items() if v > 0]

        def runner(ename):
            def run(eng):
                for waits, fn, skey, inc in streams[ename]:
                    for s, v in waits:
                        eng.wait_ge(sems[s], v)
                    if fn is None:
                        continue
                    ins = fn(eng)
                    if skey is not None:
                        ins.then_inc(sems[skey], inc)
                if ename == "sync":
                    for k, v in finals:
                        eng.wait_ge(sems[k], v)
            return run

        with nc.Block() as block:
            block.sync(runner("sync"))
            block.tensor(runner("tensor"))
            block.vector(runner("vector"))
            block.scalar(runner("scalar"))
            block.gpsimd(runner("gpsimd"))


class Arena:
    def __init__(self, t, words):
        self.t = t
        self.words = words
        self.ptr = 0
        self.hi = 0

    @staticmethod
    def _shape(ap, shape):
        if len(shape) == 1:
            return ap
        names = "abcd"[: len(shape)]
        kw = {names[i]: int(shape[i]) for i in range(len(shape) - 1)}
        return ap.rearrange("p (" + " ".join(names) + ") -> p " + " ".join(names), **kw)

    def f32(self, *shape):
        n = int(np.prod(shape))
        ap = self.t[:, self.ptr:self.ptr + n]
        self.ptr += n
        self.hi = max(self.hi, self.ptr)
        assert self.ptr <= self.words, ("arena overflow", self.ptr)
        return self._shape(ap, shape)

    def i32(self, *shape):
        return self.f32(*shape).bitcast(I32)

    def bf16(self, *shape):
        n = int(np.prod(shape))
        nw = (n + 1) // 2
        ap = self.t[:, self.ptr:self.ptr + nw].bitcast(BF16)
        if 2 * nw != n:
            ap = ap[:, 0:n]
        self.ptr += nw
        self.hi = max(self.hi, self.ptr)
        assert self.ptr <= self.words, ("arena overflow", self.ptr)
        return self._shape(ap, shape)


class Builder:
    def __init__(self, S, L):
        assert S % T == 0
        self.S = S
        self.L = L
        self.NT = S // T
        self.alpha = float((2 * L) ** 0.25)
        self.nc = bass.Bass("TRN2", target_bir_lowering=False)
        self.es = ExitStack()
        self.tr = Tracker(self.nc, self.es)
        self.din = {}
        self.uid = 0

    def dram_in(self, name, shape):
        self.din[name] = self.nc.dram_tensor(name, list(shape), F32, kind="ExternalInput").ap()
        return self.din[name]

    def mm(self, out, lhsT, rhs, start=True, stop=True, r=(), w=(), tp=None):
        if tp is None:
            fn = lambda e: e.matmul(out, lhsT=lhsT, rhs=rhs, start=start, stop=stop)
        else:
            fn = lambda e: e.matmul(out, lhsT=lhsT, rhs=rhs, start=start, stop=stop, tile_position=tp)
        self.tr.op("tensor", fn, reads=r, writes=w, sig=stop)

    def act(self, out, in_, func, r=(), w=(), bias=0.0, scale=1.0):
        self.tr.op("scalar", lambda e: e.activation(out=out, in_=in_, func=func, bias=bias, scale=scale),
                   reads=r, writes=w)

    def tt(self, out, in0, in1, op, r=(), w=(), eng="vector"):
        self.tr.op(eng, lambda e: e.tensor_tensor(out=out, in0=in0, in1=in1, op=op), reads=r, writes=w)

    def ts(self, out, in0, s1, op0, s2=None, op1=None, r=(), w=(), eng="vector"):
        if op1 is None:
            fn = lambda e: e.tensor_scalar(out=out, in0=in0, scalar1=s1, scalar2=None, op0=op0)
        else:
            fn = lambda e: e.tensor_scalar(out=out, in0=in0, scalar1=s1, scalar2=s2, op0=op0, op1=op1)
        self.tr.op(eng, fn, reads=r, writes=w)

    def stt(self, out, in0, scalar, in1, op0, op1, r=(), w=(), eng="vector"):
        self.tr.op(eng, lambda e: e.scalar_tensor_tensor(out=out, in0=in0, scalar=scalar, in1=in1, op0=op0, op1=op1),
                   reads=r, writes=w)

    def cp(self, out, in_, r=(), w=(), eng="vector"):
        self.tr.op(eng, lambda e: e.tensor_copy(out=out, in_=in_), reads=r, writes=w)

    def memset(self, ap, val, w=(), eng="gpsimd"):
        self.tr.op(eng, lambda e: e.memset(ap, val), writes=w)

    def recip(self, out, in_, r=(), w=()):
        self.tr.op("vector", lambda e: e.reciprocal(out=out, in_=in_), reads=r, writes=w)

    def dma(self, out, in_, stream, r=(), w=(), q="sync"):
        self.tr.op(q, lambda e: e.dma_start(out=out, in_=in_), reads=r, writes=w, dma=stream)

    def key(self, base):
        self.uid += 1
        return f"{base}#{self.uid}"

    def build(self):
        nc, es, S, L = self.nc, self.es, self.S, self.L
        d = self.dram_in
        d("x_fm", [D, S]); d("cvec", [128, KC])
        d("w_ada", [L, D, 6 * D]); d("b_ada", [L, 128, 48])
        d("w_in", [L, D, INW]); d("b_in", [L, 128, ZC])
        d("sc_w", [L, 128, 6]); d("pool_bd", [L, 2, 128, 128]); d("pool_scale", [L, 128, 2])
        d("cf_w", [L, 128, 62]); d("cf_b", [L, 128, 2]); d("cf_g", [L, 128, 2]); d("cf_beta", [L, 128, 2])
        d("lam_i", [L, 3, 128, 128]); d("bt_i", [L, 2, 128, 128])
        d("lam_ii", [L, 3, 128, 256]); d("c_a", [L, 128, 256]); d("c_sw", [L, 128, 256]); d("b_a", [L, 128, 256]); d("b_sw", [L, 128, 256])
        d("ssm_d", [L, 128, 2]); d("w_glu", [L, 256, 256]); d("b_glu", [L, 128, 2])
        d("w_o", [L, D, D]); d("ln1", [L, 128, 16])
        d("w_gate", [L, D, DFF]); d("w_up", [L, D, DFF]); d("w_down", [L, DFF, D]); d("ln2", [L, 128, 16])
        d("cmat", [3, 128, 128]); d("cvecs", [128, 40])
        self.y = nc.dram_tensor("y_fm", [D, S], F32, kind="ExternalOutput").ap()
        self.xa = nc.dram_tensor("xa_scr", [D, S], F32, kind="Internal").ap()
        self.xb = nc.dram_tensor("xb_scr", [D, S], F32, kind="Internal").ap()

        arena_t = es.enter_context(nc.sbuf_tensor("arena", [128, ARENA_WORDS], F32))
        self.A = A = Arena(arena_t, ARENA_WORDS)
        self.ps = [es.enter_context(nc.psum_tensor(f"psb{i}", [128, 512], F32)) for i in range(8)]

        self.ident = A.f32(128); self.jmat = A.f32(128); self.ones_cf = A.f32(128)
        self.ones_ln = A.bf16(128); self.ones_cfb = A.bf16(128)
        self.cv = A.f32(40)
        self.cond = A.f32(KC)
        self.mods = A.f32(L, 48)
        self.dma(self.ident, self.din["cmat"][0], "c0", w=["ident"])
        self.dma(self.jmat, self.din["cmat"][1], "c0", w=["jmat"])
        self.dma(self.ones_cf, self.din["cmat"][2], "c0", w=["ones_cf"])
        self.dma(self.cv, self.din["cvecs"], "c0", w=["cv"])
        self.dma(self.cond, self.din["cvec"], "c0", w=["cond"])
        self.ts(self.ones_ln, self.ones_cf, 256.0 / D, ALU.mult, r=["ones_cf"], w=["ones_ln"])
        self.cp(self.ones_cfb, self.ones_cf, r=["ones_cf"], w=["ones_cfb"])
        self.act(self.cond, self.cond, AF.Silu, r=["cond"], w=["cond"])
        mark = A.ptr
        self.mods_alloc()
        for j6 in range(6):
            self.mods_chunk(0, j6)
        self.mods_finish(0)
        self.tr.barrier()
        self.ha = nc.dram_tensor("ha_scr", [D, S], BF16, kind="Internal").ap()
        self.hbs = nc.dram_tensor("hb_scr", [D, S], BF16, kind="Internal").ap()
        pre = self.prepass_thunks()
        later = [(l, j6) for l in range(1, L) for j6 in range(6)]
        for i in range(max(len(pre), len(later))):
            if i < len(later):
                l_, j6_ = later[i]
                self.mods_chunk(l_, j6_)
                if j6_ == 5:
                    self.mods_finish(l_)
            if i < len(pre):
                pre[i]()
        self.tr.barrier()
        for l in range(L):
            A.ptr = mark
            src = self.din["x_fm"] if l == 0 else self.xb
            self.phase_m(l, src, self.xa, self.ha, self.hbs)
            self.tr.barrier()
            A.ptr = mark
            dst = self.y if l == L - 1 else self.xb
            self.phase_f(l, self.xa, dst, self.hbs, self.ha)
            self.tr.barrier()
        self.tr.emit()
        es.close()
        return nc

    def emit_h(self, hn, hkeys, xt, sc_col, sh_col, mods, hdst, t0):
        for c in range(KC):
            self.act(hn[:, c, :], xt[:, c, :], AF.Identity, r=[("x", c), "mods"], w=[hkeys[c]],
                     bias=mods[:, sh_col + c:sh_col + c + 1], scale=mods[:, sc_col + c:sc_col + c + 1])
        self.dma(hdst.rearrange("(c p) t -> p c t", p=128)[:, :, t0:t0 + T], hn, "hout", r=list(hkeys))

    def prepass_thunks(self):
        A = self.A
        xt2 = [A.f32(KC, T), A.f32(KC, T)]
        hn2 = [A.bf16(KC, T), A.bf16(KC, T)]
        mods = self.mods[:, 0, :]
        srcv = self.din["x_fm"].rearrange("(c p) t -> p c t", p=128)
        dstv = self.ha.rearrange("(c p) t -> p c t", p=128)

        def tile(it):
            b = it % 2
            t0 = it * T
            self.dma(xt2[b], srcv[:, :, t0:t0 + T], f"pin{b}", w=[("px", b)])
            for c in range(KC):
                self.act(hn2[b][:, c, :], xt2[b][:, c, :], AF.Identity, r=[("px", b), "mods0"], w=[("ph", b)],
                         bias=mods[:, c:c + 1], scale=mods[:, 8 + c:9 + c])
            self.dma(dstv[:, :, t0:t0 + T], hn2[b], f"pout{b}", r=[("ph", b)])
        return [lambda it=it: tile(it) for it in range(self.NT)]

    def mods_alloc(self):
        A, L = self.A, self.L
        self.m_stg = [A.f32(KC, 1024), A.f32(KC, 1024)]
        self.m_bada = A.f32(L, 48)
        self.m_n = 0
        self.dma(self.m_bada, self.din["b_ada"].rearrange("l p j -> p l j"), "c0", w=["bada"])

    def mods_chunk(self, l, j6):
        pm = self.ps[l % 2]
        n = self.m_n
        self.m_n += 1
        sb = self.m_stg[n % 2]
        sk = f"stg{n % 2}"
        self.dma(sb, self.din["w_ada"][l, :, 1024 * j6:1024 * (j6 + 1)].rearrange("(kc p) n -> p kc n", p=128), sk, w=[sk])
        for jj in range(8):
            col = 8 * j6 + jj
            for kc in range(KC):
                self.mm(pm[:, col:col + 1], sb[:, kc, 128 * jj:128 * (jj + 1)], self.cond[:, kc:kc + 1],
                        start=(kc == 0), stop=(kc == KC - 1), r=[sk, "cond"], w=[("pm", l % 2)])

    def mods_finish(self, l):
        pm = self.ps[l % 2]
        mk = f"mods{l}"
        self.tt(self.mods[:, l, :], pm[:, 0:48], self.m_bada[:, l, :], ALU.add, r=[("pm", l % 2), "bada"], w=[mk])
        al = 1.0 / self.alpha
        m = self.mods[:, l, :]
        self.ts(m[:, 8:16], m[:, 8:16], 1.0, ALU.add, r=[mk], w=[mk])
        self.ts(m[:, 32:40], m[:, 32:40], 1.0, ALU.add, r=[mk], w=[mk])
        self.ts(m[:, 16:24], m[:, 16:24], 1.0, ALU.add, al, ALU.mult, r=[mk], w=[mk])
        self.ts(m[:, 40:48], m[:, 40:48], 1.0, ALU.add, al, ALU.mult, r=[mk], w=[mk])

    def load_cast(self, jobs, stgs, engs=("gpsimd", "vector", "scalar")):
        for i, (dst, src, key) in enumerate(jobs):
            n = dst.shape[-1]
            k = i % len(stgs)
            sk = f"wstg{k}"
            self.dma(stgs[k][:, 0:n], src, sk, w=[sk], q="sync")
            e = engs[i % len(engs)]
            if e == "scalar":
                self.act(dst, stgs[k][:, 0:n], AF.Copy, r=[sk], w=[key])
            else:
                self.cp(dst, stgs[k][:, 0:n], r=[sk], w=[key], eng=e)

    def stats(self, pmean, kmean, pmsq, kmsq, st, eps):
        mean_sb, mm_, rstd = st
        self.act(mean_sb, pmean[:, :], AF.Copy, r=[kmean], w=["st_mean"])
        self.tt(mm_, mean_sb, mean_sb, ALU.mult, r=["st_mean"], w=["st_mm"])
        self.tt(mm_, pmsq[:, :], mm_, ALU.subtract, r=[kmsq, "st_mm"], w=["st_mm"])
        self.ts(mm_, mm_, 0.0, ALU.max, float(eps), ALU.add, r=["st_mm"], w=["st_mm"])
        self.act(mm_, mm_, AF.Sqrt, r=["st_mm"], w=["st_mm"])
        self.recip(rstd, mm_, r=["st_mm"], w=["st_rstd"])

    def sin_red(self, out, th, shift, tmp, tmpi, r, w):
        kt = self.key("sr")
        self.ts(out, th, float(shift), ALU.add, r=r, w=w)
        self.ts(tmp, out, float(1.0 / (2 * PI)), ALU.mult, r=w, w=[kt])
        self.cp(tmpi, tmp, r=[kt], w=[kt + "i"])
        self.cp(tmp, tmpi, r=[kt + "i"], w=[kt])
        self.stt(out, tmp, -2 * PI, out, ALU.mult, ALU.add, r=[kt] + list(w), w=w)
        self.ts(tmp, out, PI, ALU.is_gt, r=w, w=[kt])
        self.stt(out, tmp, -2 * PI, out, ALU.mult, ALU.add, r=[kt] + list(w), w=w)
        self.ts(tmp, out, -PI, ALU.is_lt, r=w, w=[kt])
        self.stt(out, tmp, 2 * PI, out, ALU.mult, ALU.add, r=[kt] + list(w), w=w)
        self.act(out, out, AF.Sin, r=w, w=w)

    def lam_bar(self, lam, F, pre):
        A = self.A
        k = lambda n: f"{pre}_{n}"
        dt = A.f32(F); rho = A.f32(F); th = A.f32(F); sn = A.f32(F); cs = A.f32(F)
        tmp = A.f32(F); tmpi = A.i32(F)
        lbre = A.f32(F); lbim = A.f32(F)
        self.act(dt, lam[:, 2, :], AF.Exp, r=[k("lam")], w=[k("dt")])
        self.tt(rho, lam[:, 0, :], dt, ALU.mult, r=[k("lam"), k("dt")], w=[k("rho")])
        self.tt(th, lam[:, 1, :], dt, ALU.mult, r=[k("lam"), k("dt")], w=[k("th")])
        self.act(rho, rho, AF.Exp, r=[k("rho")], w=[k("rho")])
        self.sin_red(sn, th, 0.0, tmp, tmpi, r=[k("th")], w=[k("sn")])
        self.sin_red(cs, th, PI / 2, tmp, tmpi, r=[k("th")], w=[k("cs")])
        self.tt(lbre, rho, cs, ALU.mult, r=[k("rho"), k("cs")], w=[k("lbre")])
        self.tt(lbim, rho, sn, ALU.mult, r=[k("rho"), k("sn")], w=[k("lbim")])
        lr, li = lam[:, 0, :], lam[:, 1, :]
        nr = A.f32(F); den = A.f32(F); qre = A.f32(F); qim = A.f32(F); ta = A.f32(F); tb = A.f32(F)
        self.ts(nr, lbre, -1.0, ALU.add, r=[k("lbre")], w=[k("nr")])
        self.tt(den, lr, lr, ALU.mult, r=[k("lam")], w=[k("den")])
        self.tt(ta, li, li, ALU.mult, r=[k("lam")], w=[k("ta")])
        self.tt(den, den, ta, ALU.add, r=[k("den"), k("ta")], w=[k("den")])
        self.recip(den, den, r=[k("den")], w=[k("den")])
        self.tt(qre, nr, lr, ALU.mult, r=[k("nr"), k("lam")], w=[k("qre")])
        self.tt(ta, lbim, li, ALU.mult, r=[k("lbim"), k("lam")], w=[k("ta")])
        self.tt(qre, qre, ta, ALU.add, r=[k("qre"), k("ta")], w=[k("qre")])
        self.tt(qre, qre, den, ALU.mult, r=[k("qre"), k("den")], w=[k("qre")])
        self.tt(qim, lbim, lr, ALU.mult, r=[k("lbim"), k("lam")], w=[k("qim")])
        self.tt(tb, nr, li, ALU.mult, r=[k("nr"), k("lam")], w=[k("tb")])
        self.tt(qim, qim, tb, ALU.subtract, r=[k("qim"), k("tb")], w=[k("qim")])
        self.tt(qim, qim, den, ALU.mult, r=[k("qim"), k("den")], w=[k("qim")])
        return (lbre, k("lbre")), (lbim, k("lbim")), (qre, k("qre")), (qim, k("qim"))

    def cmul(self, ore, oim, a_re, a_im, b_re, b_im, t1, t2):
        self.tt(t1[0], a_re[0], b_re[0], ALU.mult, r=[a_re[1], b_re[1]], w=[t1[1]])
        self.tt(t2[0], a_im[0], b_im[0], ALU.mult, r=[a_im[1], b_im[1]], w=[t2[1]])
        self.tt(ore[0], t1[0], t2[0], ALU.subtract, r=[t1[1], t2[1]], w=[ore[1]])
        self.tt(t1[0], a_re[0], b_im[0], ALU.mult, r=[a_re[1], b_im[1]], w=[t1[1]])
        self.tt(t2[0], a_im[0], b_re[0], ALU.mult, r=[a_im[1], b_re[1]], w=[t2[1]])
        self.tt(oim[0], t1[0], t2[0], ALU.add, r=[t1[1], t2[1]], w=[oim[1]])

    def ssm_prep(self, l, BinT, CoutT, AT, ToepT, ssd):
        A = self.A
        cv = self.cv
        ps = self.ps
        lam = A.f32(3, 128); bt = A.f32(2, 128)
        self.dma(lam, self.din["lam_i"][l].rearrange("k p f -> p k f"), "c0", w=["i_lam"])
        self.dma(bt, self.din["bt_i"][l].rearrange("k p f -> p k f"), "c0", w=["bt"])
        lbre, lbim, qre, qim = self.lam_bar(lam, 128, "i")
        t1 = (A.f32(128), "i_t1"); t2 = (A.f32(128), "i_t2")
        bbre = (A.f32(128), "bbre"); bbim = (A.f32(128), "bbim")
        self.cmul(bbre, bbim, qre, qim, (bt[:, 0, :], "bt"), (bt[:, 1, :], "bt"), t1, t2)
        bbp = []
        for par in range(2):
            pr = (A.f32(128), f"bbpr{par}"); pi_ = (A.f32(128), f"bbpi{par}")
            self.ts(pr[0], bbre[0], cv[:, 1 + par:2 + par], ALU.mult, r=["bbre", "cv"], w=[pr[1]])
            self.ts(pi_[0], bbim[0], cv[:, 1 + par:2 + par], ALU.mult, r=["bbim", "cv"], w=[pi_[1]])
            bbp.append((pr, pi_))
        v3 = lambda ap: ap.rearrange("p (c f) -> p c f", c=2)
        pw_re, pw_im = lbre, lbim
        for m in range(SL):
            sp = SL - 1 - m
            for par in range(2):
                o_re = (BinT[:, :, par, sp, 0:64], "BinT"); o_im = (BinT[:, :, par, sp, 64:128], "BinT")
                if m == 0:
                    self.cp(o_re[0], v3(bbp[par][0][0]), r=[bbp[par][0][1]], w=["BinT"])
                    self.cp(o_im[0], v3(bbp[par][1][0]), r=[bbp[par][1][1]], w=["BinT"])
                else:
                    a_re, a_im = pw_re, pw_im
                    b_re, b_im = bbp[par]
                    self.tt(t1[0], a_re[0], b_re[0], ALU.mult, r=[a_re[1], b_re[1]], w=[t1[1]])
                    self.tt(t2[0], a_im[0], b_im[0], ALU.mult, r=[a_im[1], b_im[1]], w=[t2[1]])
                    self.tt(o_re[0], v3(t1[0]), v3(t2[0]), ALU.subtract, r=[t1[1], t2[1]], w=["BinT"])
                    self.tt(t1[0], a_re[0], b_im[0], ALU.mult, r=[a_re[1], b_im[1]], w=[t1[1]])
                    self.tt(t2[0], a_im[0], b_re[0], ALU.mult, r=[a_im[1], b_re[1]], w=[t2[1]])
                    self.tt(o_im[0], v3(t1[0]), v3(t2[0]), ALU.add, r=[t1[1], t2[1]], w=["BinT"])
            if 1 <= m < SL - 1:
                n_re = (A.f32(128), f"ipw_re{m + 1}"); n_im = (A.f32(128), f"ipw_im{m + 1}")
                self.cmul(n_re, n_im, pw_re, pw_im, lbre, lbim, t1, t2)
                pw_re, pw_im = n_re, n_im
        FW = 256
        lam2 = A.f32(3, FW); ca = A.f32(FW); csw = A.f32(FW); ba = A.f32(FW); bsw = A.f32(FW)
        self.dma(lam2, self.din["lam_ii"][l].rearrange("k p f -> p k f"), "c0", w=["ii_lam"])
        self.dma(ca, self.din["c_a"][l], "c0", w=["ca"])
        self.dma(csw, self.din["c_sw"][l], "c0", w=["csw"])
        self.dma(ba, self.din["b_a"][l], "c0", w=["ba"])
        self.dma(bsw, self.din["b_sw"][l], "c0", w=["bsw"])
        l2re, l2im, q2re, q2im = self.lam_bar(lam2, FW, "ii")
        u1 = (A.f32(FW), "ii_t1"); u2 = (A.f32(FW), "ii_t2")
        bb2 = A.f32(FW)
        self.ts(u1[0], q2im[0], cv[:, 0:1], ALU.mult, -1.0, ALU.mult, r=[q2im[1], "cv"], w=[u1[1]])
        self.tt(u2[0], bsw, u1[0], ALU.mult, r=["bsw", u1[1]], w=[u2[1]])
        self.tt(bb2, ba, q2re[0], ALU.mult, r=["ba", q2re[1]], w=["bb2"])
        self.tt(bb2, bb2, u2[0], ALU.add, r=["bb2", u2[1]], w=["bb2"])
        bbz = A.f32(16, 128)
        self.memset(bbz, 0.0, w=["bbz"])
        bb2v = bb2.rearrange("p (g h) -> p g h", g=16)
        for g in range(16):
            gl = g % 8
            self.cp(bbz[:, g, 16 * gl:16 * gl + 16], bb2v[:, g, :], r=["bb2"], w=["bbz"], eng=("gpsimd" if g % 2 else "vector"))
        x1 = A.f32(FW); x2 = A.f32(FW)
        self.ts(x1, ca, cv[:, 0:1], ALU.mult, r=["ca", "cv"], w=["x1"])
        self.ts(x2, csw, -1.0, ALU.mult, r=["csw"], w=["x2"])
        cl = A.f32(SL + 1, FW)
        self.cp(cl[:, 0, :], x1, r=["x1"], w=[("cl", 0)])
        q_re, q_im = l2re, l2im
        for m in range(1, SL + 1):
            if m > 1:
                n_re = (A.f32(FW), f"q_re{m}"); n_im = (A.f32(FW), f"q_im{m}")
                self.cmul(n_re, n_im, q_re, q_im, l2re, l2im, u1, u2)
                q_re, q_im = n_re, n_im
            self.tt(u1[0], x1, q_re[0], ALU.mult, r=["x1", q_re[1]], w=[u1[1]])
            self.tt(u2[0], x2, q_im[0], ALU.mult, r=["x2", q_im[1]], w=[u2[1]])
            self.tt(cl[:, m, :], u1[0], u2[0], ALU.add, r=[u1[1], u2[1]], w=[("cl", m)])
        are = A.f32(NRS, 16); aim = A.f32(NRS, 16); aims = A.f32(NRS, 16)
        self.cp(are[:, 0, :], q_re[0].rearrange("p (g h) -> p g h", h=16)[:, :, 0], r=[q_re[1]], w=[("are", 0)])
        self.cp(aim[:, 0, :], q_im[0].rearrange("p (g h) -> p g h", h=16)[:, :, 0], r=[q_im[1]], w=[("aim", 0)])
        for k in range(1, NRS):
            pr, pi_ = are[:, k - 1, :], aim[:, k - 1, :]
            self.tt(u1[0][:, 0:16], pr, pr, ALU.mult, r=[("are", k - 1)], w=[u1[1]])
            self.tt(u2[0][:, 0:16], pi_, pi_, ALU.mult, r=[("aim", k - 1)], w=[u2[1]])
            self.tt(are[:, k, :], u1[0][:, 0:16], u2[0][:, 0:16], ALU.subtract, r=[u1[1], u2[1]], w=[("are", k)])
            self.stt(aim[:, k, :], pr, 2.0, pi_, ALU.mult, ALU.mult, r=[("are", k - 1), ("aim", k - 1)], w=[("aim", k)])
        for k in range(NRS):
            self.ts(aims[:, k, :], aim[:, k, :], cv[:, 0:1], ALU.mult, r=[("aim", k), "cv"], w=[("aims", k)])
        for k in range(NRS):
            for g in range(16):
                kk = ("AT", k, g)
                self.act(AT[:, k, g, :], self.ident, AF.Identity, r=["ident", ("are", k)], w=[kk],
                         scale=are[:, k, g:g + 1])
                self.stt(AT[:, k, g, :], self.jmat, aims[:, k, g:g + 1], AT[:, k, g, :], ALU.mult, ALU.add,
                         r=["jmat", ("aims", k), kk], w=[kk])
        cov = CoutT.rearrange("p (q t) s m -> p q t s m", t=2)
        for s_ in range(SL):
            clv = cl[:, s_ + 1, :].rearrange("p (q t h) -> p q t h", t=2, h=16)
            for par in range(2):
                self.cp(cov[:, :, par, s_, 16 * par:16 * par + 16], clv[:, :, par, :], r=[("cl", s_ + 1)], w=["CoutT"],
                        eng=("gpsimd" if par else "vector"))
        clg = cl.rearrange("p m (g h) -> p m g h", g=16)
        for bank in range(2 * SL // 4):
            for i4 in range(4):
                b_ = bank * 4 + i4
                c, tau = b_ // SL, b_ % SL
                col = i4 * 128
                for gl in range(8):
                    g = 8 * c + gl
                    self.mm(ps[bank][:, col + 16 * gl:col + 16 * gl + 16], bbz[:, g, :], clg[:, tau, g, :],
                            r=["bbz", ("cl", tau)], w=[("ps", bank)])
            for i4 in range(4):
                b_ = bank * 4 + i4
                c, tau = b_ // SL, b_ % SL
                col = i4 * 128
                if tau == 0:
                    self.stt(ToepT[:, c, 0, :], self.ident, ssd[:, c:c + 1], ps[bank][:, col:col + 128], ALU.mult, ALU.add,
                             r=["ident", "par_ssm_d", ("ps", bank)], w=["ToepT"])
                else:
                    self.act(ToepT[:, c, tau, :], ps[bank][:, col:col + 128], AF.Copy, r=[("ps", bank)], w=["ToepT"])

    def phase_m(self, l, src, dst, hsrc, hdst):
        A, NT = self.A, self.NT
        ps = self.ps
        din = self.din
        mods = self.mods[:, l, :]
        w_in = A.bf16(KC, INW); w_o = A.bf16(KC, D); glu = A.bf16(2, 256); poolbd = A.bf16(2, 128)
        BinT = A.bf16(2, 2, SL, 128); CoutT = A.bf16(16, SL, 32); AT = A.bf16(NRS, 16, 128); ToepT = A.bf16(2, SL, 128)
        cdiag = A.bf16(2, 31, 128)
        b_in = A.f32(ZC); scw = A.f32(2, 3); pscale = A.f32(2); cfw = A.f32(2, 31)
        cfb = A.f32(2); cfg = A.f32(2); cfbeta = A.f32(2); ssd = A.f32(2); bglu = A.f32(2); ln1 = A.f32(16)
        for (ap, nm, re) in ((b_in, "b_in", None), (scw, "sc_w", "p (c k) -> p c k"), (pscale, "pool_scale", None),
                             (cfw, "cf_w", "p (c k) -> p c k"), (cfb, "cf_b", None), (cfg, "cf_g", None),
                             (cfbeta, "cf_beta", None), (ssd, "ssm_d", None), (bglu, "b_glu", None), (ln1, "ln1", None)):
            s_ = din[nm][l]
            if re is not None:
                s_ = s_.rearrange(re, c=2)
            self.dma(ap, s_, "c0", w=["par_" + nm])
        self.memset(CoutT, 0.0, w=["CoutT"])
        for j in range(2):
            for k in range(31):
                self.act(cdiag[:, j, k, :], self.ident, AF.Identity, r=["ident", "par_cf_w"], w=["cdiag"], scale=cfw[:, j, k:k + 1])
        m0 = A.ptr
        stgs = [A.f32(INW), A.f32(INW)]
        self.ssm_prep(l, BinT, CoutT, AT, ToepT, ssd)
        jobs = []
        for kc in range(KC):
            jobs.append((w_in[:, kc, :], din["w_in"][l, 128 * kc:128 * kc + 128, :], "w_in"))
        for kc in range(KC):
            jobs.append((w_o[:, kc, :], din["w_o"][l, 128 * kc:128 * kc + 128, :], "w_o"))
        for kc in range(2):
            jobs.append((glu[:, kc, :], din["w_glu"][l, 128 * kc:128 * kc + 128, :], "glu"))
            jobs.append((poolbd[:, kc, :], din["pool_bd"][l, kc], "poolbd"))
        self.load_cast(jobs, stgs, engs=("gpsimd",))
        self.tr.barrier()
        A.ptr = m0
        xt = A.f32(KC, T); hb2 = [A.bf16(KC, T), A.bf16(KC, T)]; ycat = A.bf16(KC, T)
        zh = A.f32(2, T); zb = A.f32(2, T); zc_ = A.f32(2, T); zv = A.f32(2, T + 16); sg = A.f32(2, T + 16)
        zp = A.f32(2, T + 15); pbuf = A.f32(2, T + 2); hbuf = A.bf16(2, T + 30)
        u = A.bf16(2, T); mb = A.bf16(2, T); yg = A.bf16(2, T); sig = A.bf16(T)
        scr0 = A.f32(2, T + 16); accb = A.bf16(2, T); sqb = A.bf16(2, T); scr2 = A.f32(2, T)
        Sx = A.bf16(16, NJ + 2)
        st = (A.f32(T), A.f32(T), A.f32(T))
        self.memset(zp, 0.0, w=[("zp", 0), ("zp", 1), ("z", 6), ("z", 7)])
        self.memset(pbuf, 0.0, w=[("pbuf", 0), ("pbuf", 1)])
        self.memset(hbuf, 0.0, w=[("hbuf", 0), ("hbuf", 1)])
        self.memset(Sx, 0.0, w=[("Sx", q_) for q_ in range(4)])
        srcv = src.rearrange("(c p) t -> p c t", p=128)
        dstv = dst.rearrange("(c p) t -> p c t", p=128)
        hsrcv = hsrc.rearrange("(c p) t -> p c t", p=128)
        xk = [("x", c) for c in range(KC)]
        eps1 = LN_EPS / (self.alpha ** 2)
        zdest = {0: (zh, 0, 0), 1: (zh, 1, 0), 2: (zb, 0, 0), 3: (zb, 1, 0), 4: (zc_, 0, 0), 5: (zc_, 1, 0),
                 6: (zp, 0, 15), 7: (zp, 1, 15), 8: (zv, 0, 0), 9: (zv, 1, 0), 10: (sg, 0, 0), 11: (sg, 1, 0)}
        zorder = [8, 10, 9, 11, 12, 13, 4, 0, 2, 5, 1, 3, 6, 7]
        uv = [u[:, c, :].rearrange("p (s j) -> p s j", s=SL) for c in range(2)]
        u_tok = [u[:, c, :].rearrange("p (s j) -> p j s", s=SL) for c in range(2)]
        ygv = [yg[:, c, :].rearrange("p (j s) -> p s j", s=SL) for c in range(2)]
        VB = [3, 4, 6, 7]
        psVq = [ps[VB[q]][:, 0:4 * NJ].rearrange("p (c t j) -> p c t j", c=2, t=2) for q in range(4)]
        Sxv = Sx.rearrange("p (c q t) j -> p c q t j", c=2, q=4)
        psY2 = [ps[5][:, :].rearrange("p (s j) -> p s j", s=SL), ps[2][:, :].rearrange("p (s j) -> p s j", s=SL)]
        YB = [5, 2]
        self.zrot = 0

        def head_load(i):
            b = i % 2
            self.dma(hb2[b], hsrcv[:, :, i * T:(i + 1) * T], f"hin{b}", w=[("hb", b, c) for c in range(KC)])

        def xload(i):
            self.dma(xt, srcv[:, :, i * T:(i + 1) * T], "xin", w=xk)

        def inproj(i):
            b = i % 2
            for zi in zorder:
                bk = self.zrot % 2
                self.zrot += 1
                for kc in range(KC):
                    self.mm(ps[bk][:, :], w_in[:, kc, 128 * zi:128 * zi + 128], hb2[b][:, kc, :],
                            start=(kc == 0), stop=(kc == KC - 1), r=["w_in", ("hb", b, kc)], w=[("ps", bk)])
                if zi >= 12:
                    c = zi - 12
                    self.act(u_tok[c], ps[bk][:, :].rearrange("p (j s) -> p j s", s=SL), AF.Identity,
                             r=[("ps", bk), "par_b_in"], w=[("z", zi)], bias=b_in[:, zi:zi + 1])
                    continue
                tl, j, off = zdest[zi]
                fn = AF.Sigmoid if zi in (10, 11) else AF.Identity
                self.act(tl[:, j, off:off + T], ps[bk][:, :], fn, r=[("ps", bk), "par_b_in"], w=[("z", zi)],
                         bias=b_in[:, zi:zi + 1])

        def conv_h():
            for j in range(2):
                self.tt(hbuf[:, j, 30:T + 30], zv[:, j, 0:T], sg[:, j, 0:T], ALU.mult, r=[("z", 8 + j), ("z", 10 + j)], w=[("hbuf", j)])

        def conv_thunks():
            th = []
            for j in range(2):
                hk, ak = ("hbuf", j), ("scr0", j)
                for k in range(31):
                    th.append(lambda j=j, k=k, hk=hk: self.mm(ps[2][:, :], cdiag[:, j, k, :], hbuf[:, j, k:k + T], start=(k == 0), stop=(k == 30),
                                                               r=["cdiag", hk], w=[("ps", 2)]))

                def fin(j=j, hk=hk, ak=ak):
                    self.act(scr0[:, j, 0:T], ps[2][:, :], AF.Identity, r=[("ps", 2), "par_cf_b"], w=[ak], bias=cfb[:, j:j + 1])
                    self.cp(hbuf[:, j, 0:30], hbuf[:, j, T:T + 30], r=[hk], w=[hk], eng="gpsimd")
                    self.act(sqb[:, j, :], scr0[:, j, 0:T], AF.Square, r=[ak], w=[("sqb", j)])
                    self.act(accb[:, j, :], scr0[:, j, 0:T], AF.Copy, r=[ak], w=[("accb", j)])
                th.append(fin)
            return th

        def pool(it):
            for j in range(2):
                zk, ka, kb = ("z", 6 + j), ("z", 8 + j), ("z", 10 + j)
                sa, sb_, zz = zv[:, j, :], sg[:, j, :], zp[:, j, :]
                self.tt(sa[:, 0:T + 14], zz[:, 1:T + 15], zz[:, 0:T + 14], ALU.add, r=[zk, ("zp", j)], w=[ka])
                if j == 0:
                    self.tt(sb_[64:128, 0:T + 12], sa[64:128, 2:T + 14], sa[64:128, 0:T + 12], ALU.add, r=[ka], w=[kb])
                    lo, lo_off, hi, hi_off = sa, 14, sb_, 12
                    wl, wh = 2.0, 4.0
                else:
                    self.tt(sb_[:, 0:T + 12], sa[:, 2:T + 14], sa[:, 0:T + 12], ALU.add, r=[ka], w=[kb])
                    self.tt(sa[:, 0:T + 8], sb_[:, 4:T + 12], sb_[:, 0:T + 8], ALU.add, r=[kb, ka], w=[ka])
                    self.tt(sb_[64:128, 0:T], sa[64:128, 8:T + 8], sa[64:128, 0:T], ALU.add, r=[ka, kb], w=[kb])
                    lo, lo_off, hi, hi_off = sa, 8, sb_, 0
                    wl, wh = 8.0, 16.0
                self.stt(mb[0:64, j, :], lo[0:64, lo_off:lo_off + T], 1.0 / wl, zz[0:64, 15:T + 15], ALU.mult, ALU.subtract,
                         r=[ka, kb, zk], w=[("mb", j)])
                self.stt(mb[64:128, j, :], hi[64:128, hi_off:hi_off + T], 1.0 / wh, zz[64:128, 15:T + 15], ALU.mult, ALU.subtract,
                         r=[ka, kb, zk], w=[("mb", j)])
                if it == 0:
                    rc = self.cv[:, 8 + 16 * j:24 + 16 * j]
                    tmp = scr2[:, j, 0:16]
                    for (pl, ph, sbuf_, off) in ((0, 64, lo, lo_off), (64, 128, hi, hi_off)):
                        self.tt(tmp[pl:ph, :], sbuf_[pl:ph, off:off + 16], rc[pl:ph, :], ALU.mult, r=[ka, kb, "cv"], w=[("scr2", j)])
                        self.tt(mb[pl:ph, j, 0:16], tmp[pl:ph, :], zz[pl:ph, 15:31], ALU.subtract,
                                r=[("scr2", j), zk], w=[("mb", j)])
                self.cp(zz[:, 0:15], zz[:, T:T + 15], r=[zk, ka, kb, ("mb", j)], w=[("zp", j)], eng="gpsimd")

        def pool_mm():
            for j in range(2):
                bk = self.zrot % 2
                self.zrot += 1
                self.mm(ps[bk][:, :], poolbd[:, j, :], mb[:, j, :], r=["poolbd", ("mb", j)], w=[("ps", bk)])
                self.act(ycat[:, 2 + j, :], ps[bk][:, :], AF.Identity, r=[("ps", bk), "par_pool_scale"], w=[("ycat", 2 + j)],
                         scale=pscale[:, j:j + 1])

        def sconv():
            for j in range(2):
                pk_, ak = ("pbuf", j), ("scr2", j)
                acc = scr2[:, j, :]
                self.tt(pbuf[:, j, 2:T + 2], zc_[:, j, :], zh[:, j, :], ALU.mult, r=[("z", 4 + j), ("z", j)], w=[pk_])
                self.ts(acc, pbuf[:, j, 2:T + 2], scw[:, j, 2:3], ALU.mult, r=[pk_, "par_sc_w"], w=[ak])
                self.stt(acc, pbuf[:, j, 1:T + 1], scw[:, j, 1:2], acc, ALU.mult, ALU.add, r=[pk_, "par_sc_w", ak], w=[ak])
                self.stt(acc, pbuf[:, j, 0:T], scw[:, j, 0:1], acc, ALU.mult, ALU.add, r=[pk_, "par_sc_w", ak], w=[ak])
                self.tt(ycat[:, j, :], acc, zb[:, j, :], ALU.mult, r=[ak, ("z", 2 + j)], w=[("ycat", j)])
                self.cp(pbuf[:, j, 0:2], pbuf[:, j, T:T + 2], r=[pk_], w=[pk_], eng="gpsimd")

        def body(it, conv, prev):
            ssm = []

            def st_v():
                for q in range(4):
                    for c in range(2):
                        for par in range(2):
                            g = 8 * c + 2 * q + par
                            for sp in range(SL):
                                self.mm(psVq[q][:, c, par, :], BinT[32 * q:32 * q + 32, c, par, sp, :], uv[c][32 * q:32 * q + 32, sp, :],
                                        start=(sp == 0), stop=(sp == SL - 1 and it == 0),
                                        r=["BinT", ("z", 12 + c)], w=[("ps", VB[q])], tp=(32 * q, 0))
                            if it > 0:
                                self.mm(psVq[q][:, c, par, 0:1], AT[:, 0, g, :], Sx[:, g, 0:1], start=False, stop=True,
                                        r=[("AT", 0, g), ("Sx", q)], w=[("ps", VB[q])])
                    self.cp(Sxv[:, :, q, :, 1:NJ + 1], psVq[q], r=[("ps", VB[q])], w=[("Sx", q)])
            ssm.append(st_v)

            def st_round(k):
                sh = 1 << k
                for q in range(4):
                    for c in range(2):
                        for par in range(2):
                            g = 8 * c + 2 * q + par
                            self.mm(psVq[q][:, c, par, sh:NJ], AT[:, k, g, :], Sx[:, g, 1:NJ + 1 - sh],
                                    r=[("AT", k, g), ("Sx", q)], w=[("ps", VB[q])])
                    self.tt(Sxv[:, :, q, :, 1 + sh:NJ + 1], psVq[q][:, :, :, sh:NJ], Sxv[:, :, q, :, 1 + sh:NJ + 1],
                            ALU.add, r=[("ps", VB[q]), ("Sx", q)], w=[("Sx", q)])
            for k in range(NRS):
                ssm.append(lambda k=k: st_round(k))

            def st_out(c):
                psY, yk = psY2[c], ("ps", YB[c])
                for s_ in range(SL):
                    for sp in range(s_ + 1):
                        self.mm(psY[:, s_, :], ToepT[:, c, s_ - sp, :], uv[c][:, sp, :], start=(sp == 0), stop=False,
                                r=["ToepT", ("z", 12 + c)], w=[yk])
                    for q in range(4):
                        for par in range(2):
                            g = 8 * c + 2 * q + par
                            self.mm(psY[32 * q:32 * q + 32, s_, :], CoutT[:, g, s_, :], Sx[:, g, 0:NJ], start=False,
                                    stop=(par == 1), r=["CoutT", ("Sx", q)], w=[yk], tp=(0, 32 * q))
                self.act(ygv[c], psY, AF.Gelu_apprx_tanh, r=[yk], w=[("yg", c)])
                if c == 1:
                    for q in range(4):
                        self.cp(Sxv[:, :, q, :, 0:1], Sxv[:, :, q, :, NJ:NJ + 1], r=[("Sx", q)], w=[("Sx", q)], eng="gpsimd")
            ssm.append(lambda: st_out(0))
            ssm.append(lambda: st_out(1))

            def st_glu():
                for mo in range(2):
                    bk = 6 + mo
                    for kc in range(2):
                        self.mm(ps[bk][:, :], glu[:, kc, 128 * mo:128 * mo + 128], yg[:, kc, :], start=(kc == 0), stop=(kc == 1),
                                r=["glu", ("yg", kc)], w=[("ps", bk)])
                    self.act(sig, ps[bk][:, :], AF.Sigmoid, r=[("ps", bk), "par_b_glu"], w=["sig"], bias=bglu[:, mo:mo + 1])
                    self.tt(ycat[:, 6 + mo, :], yg[:, mo, :], sig, ALU.mult, r=[("yg", mo), "sig"], w=[("ycat", 6 + mo)])
            ssm.append(st_glu)
            st_v_, rounds, out0_, out1_, glu_ = ssm[0], ssm[1:1 + NRS], ssm[1 + NRS], ssm[2 + NRS], ssm[3 + NRS]
            st_v_()
            pool_mm()
            for _ in range(32):
                conv.pop(0)()
            for k in range(3):
                rounds[k]()
                for _ in range(11):
                    if conv:
                        conv.pop(0)()
            while conv:
                conv.pop(0)()
            conf_stats_pe()
            for k in range(3, NRS):
                rounds[k]()
            conf_norm_a()
            out0_()
            out1_()
            conf_norm_b()
            glu_()

        def conf_stats_pe():
            for j in range(2):
                self.mm(ps[0][:, :], self.ones_cfb, accb[:, j, :], start=(j == 0), stop=(j == 1),
                        r=["ones_cfb", ("accb", j)], w=[("ps", 0)])
            for j in range(2):
                self.mm(ps[1][:, :], self.ones_cfb, sqb[:, j, :], start=(j == 0), stop=(j == 1),
                        r=["ones_cfb", ("sqb", j)], w=[("ps", 1)])

        def conf_norm_a():
            self.stats(ps[0], ("ps", 0), ps[1], ("ps", 1), st, LN_EPS)
            for j in range(2):
                ak = ("scr0", j)
                self.tt(scr0[:, j, 0:T], scr0[:, j, 0:T], st[0], ALU.subtract, r=[ak, "st_mean"], w=[ak])
                self.tt(scr0[:, j, 0:T], scr0[:, j, 0:T], st[2], ALU.mult, r=[ak, "st_rstd"], w=[ak])

        def conf_norm_b():
            for j in range(2):
                ak = ("scr0", j)
                self.act(ycat[:, 4 + j, :], scr0[:, j, 0:T], AF.Silu, r=[ak, "par_cf_g", "par_cf_beta"], w=[("ycat", 4 + j)],
                         bias=cfbeta[:, j:j + 1], scale=cfg[:, j:j + 1])

        def tail_wo(i):
            for mp in range(0, KC, 2):
                for mo in (mp, mp + 1):
                    bk = mo % 2
                    for kc in range(6):
                        self.mm(ps[bk][:, :], w_o[:, kc, 128 * mo:128 * mo + 128], ycat[:, kc, :],
                                start=(kc == 0), stop=False, r=["w_o", ("ycat", kc)], w=[("ps", bk)])
                for mo in (mp, mp + 1):
                    bk = mo % 2
                    for kc in (6, 7):
                        self.mm(ps[bk][:, :], w_o[:, kc, 128 * mo:128 * mo + 128], ycat[:, kc, :],
                                start=False, stop=(kc == 7), r=["w_o", ("ycat", kc)], w=[("ps", bk)])
                    self.stt(xt[:, mo, :], ps[bk][:, :], mods[:, 16 + mo:17 + mo], xt[:, mo, :], ALU.mult, ALU.add,
                             r=[("ps", bk), "mods", ("x", mo)], w=[("x", mo)])

        def ln_a(i):
            b = i % 2
            for c in range(KC):
                self.act(hb2[b][:, c, :], xt[:, c, :], AF.Copy, r=[("x", c)], w=[("hb", b, c)])
                self.act(ycat[:, c, :], xt[:, c, :], AF.Square, r=[("x", c)], w=[("ycat", c)])

        def ln_b1(i):
            b = i % 2
            for c in range(KC):
                self.mm(ps[0][:, :], self.ones_ln, hb2[b][:, c, :], start=(c == 0), stop=(c == KC - 1),
                        r=["ones_ln", ("hb", b, c)], w=[("ps", 0)])
            for c in range(KC):
                self.mm(ps[1][:, :], self.ones_ln, ycat[:, c, :], start=(c == 0), stop=(c == KC - 1),
                        r=["ones_ln", ("ycat", c)], w=[("ps", 1)])
            self.stats(ps[0], ("ps", 0), ps[1], ("ps", 1), st, eps1)
            for c in range(KC):
                self.tt(xt[:, c, :], xt[:, c, :], st[0], ALU.subtract, r=[("x", c), "st_mean"], w=[("x", c)])
                self.tt(xt[:, c, :], xt[:, c, :], st[2], ALU.mult, r=[("x", c), "st_rstd"], w=[("x", c)])

        def ln_b2(i):
            b = i % 2
            for c in range(KC):
                self.act(xt[:, c, :], xt[:, c, :], AF.Identity, r=[("x", c), "par_ln1"], w=[("x", c)],
                         bias=ln1[:, 8 + c:9 + c], scale=ln1[:, c:c + 1])
            self.emit_h(hb2[b], [("hb", b, c) for c in range(KC)], xt, 32, 24, mods, hdst, i * T)
            self.dma(dstv[:, :, i * T:(i + 1) * T], xt, "xout", r=xk)

        head_load(0)
        xload(0)
        inproj(0)
        conv_h()
        pool(0)
        sconv()
        for it in range(NT):
            body(it, conv_thunks(), None)
            if it + 1 < NT:
                head_load(it + 1)
            tail_wo(it)
            ln_a(it)
            ln_b1(it)
            if it + 1 < NT:
                inproj(it + 1)
                ln_b2(it)
                xload(it + 1)
                conv_h()
                pool(it + 1)
                sconv()
            else:
                ln_b2(it)

    def phase_f(self, l, src, dst, hsrc, hdst):
        A, NT = self.A, self.NT
        ps = self.ps
        din = self.din
        mods = self.mods[:, l, :]
        wg = A.bf16(KC, DFF); wu = A.bf16(KC, DFF); wd = A.bf16(FC, D)
        ln2 = A.f32(16)
        self.dma(ln2, din["ln2"][l], "c0", w=["par_ln"])
        HW = DFF // 2
        stgs = [A.f32(HW), A.f32(HW)]
        jobs = []
        for hf in range(2):
            for kc in range(KC):
                jobs.append((wg[:, kc, hf * HW:(hf + 1) * HW], din["w_gate"][l, 128 * kc:128 * kc + 128, hf * HW:(hf + 1) * HW], ("wg", kc, hf)))
                jobs.append((wu[:, kc, hf * HW:(hf + 1) * HW], din["w_up"][l, 128 * kc:128 * kc + 128, hf * HW:(hf + 1) * HW], ("wu", kc, hf)))
        for f in range(FC):
            jobs.append((wd[:, f, :], din["w_down"][l, 128 * f:128 * f + 128, :], ("wd", f)))
        xt = A.f32(KC, T); hb2 = [A.bf16(KC, T), A.bf16(KC, T)]; gb = A.bf16(FC, T)
        sgt = [A.bf16(T), A.bf16(T)]
        st_a = A.f32(T); st_b = A.f32(T)
        st = (st_a, st_b, st_b)
        srcv = src.rearrange("(c p) t -> p c t", p=128)
        dstv = dst.rearrange("(c p) t -> p c t", p=128)
        hsrcv = hsrc.rearrange("(c p) t -> p c t", p=128)
        xk = [("x", c) for c in range(KC)]
        eps2 = LN_EPS / (self.alpha ** 2)
        last = (l == self.L - 1)
        nmods = None if last else self.mods[:, l + 1, :]
        self.rot = 0
        RS = FC - KC

        def head_load(i):
            b = i % 2
            self.dma(hb2[b], hsrcv[:, :, i * T:(i + 1) * T], f"hin{b}", w=[("hb", b, c) for c in range(KC)])

        def xload(i):
            self.dma(xt, srcv[:, :, i * T:(i + 1) * T], "xin", w=xk)

        def gu(i, fs):
            b = i % 2
            for f in fs:
                bg = self.rot % 2
                bu_ = 2 + self.rot % 2
                self.rot += 1
                hf = (128 * f) // HW
                for kc in range(KC):
                    self.mm(ps[bg][:, :], wg[:, kc, 128 * f:128 * f + 128], hb2[b][:, kc, :], start=(kc == 0), stop=(kc == KC - 1),
                            r=[("wg", kc, hf), ("hb", b, kc)], w=[("ps", bg)])
                for kc in range(KC):
                    self.mm(ps[bu_][:, :], wu[:, kc, 128 * f:128 * f + 128], hb2[b][:, kc, :], start=(kc == 0), stop=(kc == KC - 1),
                            r=[("wu", kc, hf), ("hb", b, kc)], w=[("ps", bu_)])
                sk = ("sgt", f % 2)
                self.act(sgt[f % 2], ps[bg][:, :], AF.Silu, r=[("ps", bg)], w=[sk])
                self.tt(gb[:, f, :], sgt[f % 2], ps[bu_][:, :], ALU.mult, r=[sk, ("ps", bu_)], w=[("gb", f)])

        def down(i):
            for mo in range(KC):
                bk = 4 + mo % 2
                for f in range(FC):
                    self.mm(ps[bk][:, :], wd[:, f, 128 * mo:128 * mo + 128], gb[:, f, :], start=(f == 0), stop=(f == FC - 1),
                            r=[("wd", f), ("gb", f)], w=[("ps", bk)])
                self.stt(xt[:, mo, :], ps[bk][:, :], mods[:, 40 + mo:41 + mo], xt[:, mo, :], ALU.mult, ALU.add,
                         r=[("ps", bk), "mods", ("x", mo)], w=[("x", mo)])

        def ln_a(i):
            b = i % 2
            for c in range(KC):
                self.act(hb2[b][:, c, :], xt[:, c, :], AF.Copy, r=[("x", c)], w=[("hb", b, c)])
                self.act(gb[:, RS + c, :], xt[:, c, :], AF.Square, r=[("x", c)], w=[("gb", RS + c)])

        def ln_b1(i):
            b = i % 2
            for c in range(KC):
                self.mm(ps[6][:, :], self.ones_ln, hb2[b][:, c, :], start=(c == 0), stop=(c == KC - 1),
                        r=["ones_ln", ("hb", b, c)], w=[("ps", 6)])
            for c in range(KC):
                self.mm(ps[7][:, :], self.ones_ln, gb[:, RS + c, :], start=(c == 0), stop=(c == KC - 1),
                        r=["ones_ln", ("gb", RS + c)], w=[("ps", 7)])
            self.stats(ps[6], ("ps", 6), ps[7], ("ps", 7), st, eps2)
            for c in range(KC):
                self.tt(xt[:, c, :], xt[:, c, :], st[0], ALU.subtract, r=[("x", c), "st_mean"], w=[("x", c)])
                self.tt(xt[:, c, :], xt[:, c, :], st[2], ALU.mult, r=[("x", c), "st_rstd"], w=[("x", c)])

        def ln_b2(i):
            b = i % 2
            for c in range(KC):
                self.act(xt[:, c, :], xt[:, c, :], AF.Identity, r=[("x", c), "par_ln"], w=[("x", c)],
                         bias=ln2[:, 8 + c:9 + c], scale=ln2[:, c:c + 1])
            if not last:
                self.emit_h(hb2[b], [("hb", b, c) for c in range(KC)], xt, 8, 0, nmods, hdst, i * T)
            self.dma(dstv[:, :, i * T:(i + 1) * T], xt, "xout", r=xk)

        head_load(0)
        xload(0)
        self.load_cast(jobs, stgs, engs=("gpsimd",))
        gu(0, range(FC))
        for it in range(NT):
            if it + 1 < NT:
                head_load(it + 1)
            down(it)
            ln_a(it)
            ln_b1(it)
            if it + 1 < NT:
                gu(it + 1, range(0, 8))
            ln_b2(it)
            if it + 1 < NT:
                xload(it + 1)
                gu(it + 1, range(8, FC))


def _chunked(v, nch):
    return np.ascontiguousarray(v.reshape(nch, 128).T)


def prep_shared(inp, L):
    f = np.float32
    g = lambda k: np.asarray(inp[k], dtype=f)
    out = {}
    out["w_ada"] = np.ascontiguousarray(g("w_ada"))
    out["b_ada"] = np.stack([_chunked(g("b_ada")[l], 48) for l in range(L)])
    out["w_in"] = np.ascontiguousarray(g("w_in"))
    out["b_in"] = np.stack([_chunked(g("b_in")[l], ZC) for l in range(L)])
    scw = g("sc_w")
    out["sc_w"] = np.ascontiguousarray(scw.reshape(L, 3, 2, 128).transpose(0, 3, 2, 1).reshape(L, 128, 6))
    pw = g("pool_w")
    bd = np.zeros((L, 2, 128, 128), f)
    for c in range(2):
        bd[:, c, 0:64, 0:64] = pw[:, 2 * c]
        bd[:, c, 64:128, 64:128] = pw[:, 2 * c + 1]
    out["pool_bd"] = bd
    ch2 = lambda k: np.stack([_chunked(g(k)[l], 2) for l in range(L)])
    out["pool_scale"] = ch2("pool_scale")
    cw = g("cf_dw_w")
    out["cf_w"] = np.ascontiguousarray(cw.reshape(L, 31, 2, 128).transpose(0, 3, 2, 1).reshape(L, 128, 62))
    out["cf_b"] = ch2("cf_dw_b"); out["cf_g"] = ch2("cf_ln_g"); out["cf_beta"] = ch2("cf_ln_b")
    lre, lim, ldt = g("ssm_lam_re"), g("ssm_lam_im"), g("ssm_log_dt")
    def lay_i(a):
        a4 = a.reshape(L, 2, 8, 1, 64)
        a4 = np.broadcast_to(a4, (L, 2, 8, 16, 64))
        return np.ascontiguousarray(a4.transpose(0, 2, 3, 1, 4).reshape(L, 128, 128))
    ldt_b = np.broadcast_to(ldt[:, :, None], (L, 16, 64))
    out["lam_i"] = np.stack([lay_i(lre), lay_i(lim), lay_i(np.ascontiguousarray(ldt_b))], axis=1)
    bre, bim = g("ssm_b_re"), g("ssm_b_im")
    def lay_bt(b):
        b5 = b.reshape(L, 2, 8, 64, 16)
        return np.ascontiguousarray(b5.transpose(0, 2, 4, 1, 3).reshape(L, 128, 128))
    out["bt_i"] = np.stack([lay_bt(bre), lay_bt(bim)], axis=1)
    def lay_ii(a):
        t = a.transpose(0, 2, 1)
        return np.ascontiguousarray(np.concatenate([t, t], axis=1))
    rep = lambda a3: np.ascontiguousarray(np.repeat(a3, 16, axis=2))
    out["lam_ii"] = np.stack([rep(lay_ii(lre)), rep(lay_ii(lim)), rep(lay_ii(np.ascontiguousarray(ldt_b)))], axis=1)
    cre, cim = g("ssm_c_re"), g("ssm_c_im")
    cre_t, cim_t = cre.transpose(0, 3, 1, 2), cim.transpose(0, 3, 1, 2)
    out["c_a"] = np.ascontiguousarray(np.concatenate([cre_t, cim_t], axis=1).reshape(L, 128, 256))
    out["c_sw"] = np.ascontiguousarray(np.concatenate([cim_t, cre_t], axis=1).reshape(L, 128, 256))
    bre_t, bim_t = bre.transpose(0, 2, 1, 3), bim.transpose(0, 2, 1, 3)
    out["b_a"] = np.ascontiguousarray(np.concatenate([bre_t, bim_t], axis=1).reshape(L, 128, 256))
    out["b_sw"] = np.ascontiguousarray(np.concatenate([bim_t, bre_t], axis=1).reshape(L, 128, 256))
    out["ssm_d"] = ch2("ssm_d")
    out["w_glu"] = np.ascontiguousarray(g("ssm_w_glu"))
    out["b_glu"] = ch2("ssm_b_glu")
    out["w_o"] = np.ascontiguousarray(g("w_o"))
    out["ln1"] = np.stack([np.concatenate([_chunked(g("ln1_g")[l], 8), _chunked(g("ln1_b")[l], 8)], axis=1) for l in range(L)])
    out["w_gate"] = np.ascontiguousarray(g("w_gate")); out["w_up"] = np.ascontiguousarray(g("w_up"))
    out["w_down"] = np.ascontiguousarray(g("w_down"))
    out["ln2"] = np.stack([np.concatenate([_chunked(g("ln2_g")[l], 8), _chunked(g("ln2_b")[l], 8)], axis=1) for l in range(L)])
    cm = np.zeros((3, 128, 128), f)
    cm[0] = np.eye(128, dtype=f)
    for k in range(128):
        cm[1, k, (k + 64) % 128] = 1.0
    cm[2] = 1.0 / 256.0
    out["cmat"] = cm
    cvv = np.zeros((128, 40), f)
    cvv[:64, 0] = 1.0; cvv[64:, 0] = -1.0
    par = (np.arange(128) // 16) % 2
    cvv[:, 1] = (par == 0); cvv[:, 2] = (par == 1)
    wins = np.array([[2, 4], [8, 16]])
    for c in range(2):
        for half in range(2):
            w = wins[c, half]
            cvv[64 * half:64 * half + 64, 8 + 16 * c:24 + 16 * c] = 1.0 / np.minimum(np.arange(1, 17), w)
    out["cvecs"] = cvv
    return out


_CACHE = {}


def get_program(S, L):
    if (S, L) not in _CACHE:
        b = Builder(S, L)
        _CACHE[(S, L)] = b.build()
    return _CACHE[(S, L)]


def run(inputs, S, L, n_cores):
    shared = prep_shared(inputs, L)
    x = np.asarray(inputs["x"], dtype=np.float32)
    c = np.asarray(inputs["c"], dtype=np.float32)
    in_maps = []
    for b in range(n_cores):
        m = dict(shared)
        m["x_fm"] = np.ascontiguousarray(x[b].T)
        m["cvec"] = _chunked(c[b], KC)
        in_maps.append(m)
    nc = get_program(S, L)
    res = run_bass_kernel_spmd(nc, in_maps, core_ids=list(range(n_cores)))
    out = np.stack([np.ascontiguousarray(res.results[b]["y_fm"].T) for b in range(n_cores)])
    return out.astype(np.float32)


def kernel(**inputs):
    return run(inputs, 8192, 4, 8)
```

```python
import numpy as np
from contextlib import ExitStack
import concourse.bass as bass
import concourse.mybir as mybir
from concourse.bass_utils import run_bass_kernel_spmd

F32 = mybir.dt.float32
BF16 = mybir.dt.bfloat16
I32 = mybir.dt.int32
AF = mybir.ActivationFunctionType
ALU = mybir.AluOpType

D = 1024
KC = 8
INW = 1792
ZC = 14
DFF = 2816
FC = 22
T = 512
LN_EPS = 1e-5
SL = 8
NJ = T // SL
NRS = 6
PI = float(np.pi)
ENGS = ("tensor", "vector", "scalar", "gpsimd", "sync")
ARENA_WORDS = 53200


class Tracker:
    def __init__(self, nc, es):
        self.nc = nc
        self.es = es
        self.streams = {e: [] for e in ENGS}
        self.sems = {}
        self.cnt = {}
        self.last_w = {}
        self.readers = {}
        self.waited = {e: {} for e in ENGS}
        for e in ENGS:
            self._sem("E_" + e)

    def _sem(self, key):
        if key not in self.sems:
            self.sems[key] = self.es.enter_context(self.nc.semaphore("s_" + key))
            self.cnt[key] = 0
        return key

    def op(self, eng, fn, reads=(), writes=(), dma=None, sig=True):
        deps = {}

        def add(d):
            if d is not None and deps.get(d[0], 0) < d[1]:
                deps[d[0]] = d[1]

        for b in reads:
            add(self.last_w.get(b))
        for b in writes:
            add(self.last_w.get(b))
            for r in self.readers.get(b, ()):
                add(r)
        if dma is not None:
            if dma == "c0":
                self.misc_rr = (getattr(self, "misc_rr", -1) + 1) % 8
                dma = f"c{self.misc_rr}"
            skey = self._sem("D_" + dma)
            inc = 16
            if self.cnt[skey] > 0:
                add((skey, self.cnt[skey]))
        else:
            skey = "E_" + eng
            inc = 1
        own = "E_" + eng
        waits = []
        for s, v in deps.items():
            if eng == "tensor" and s == own and dma is None:
                continue
            if self.waited[eng].get(s, 0) >= v:
                continue
            self.waited[eng][s] = v
            waits.append((s, v))
        if sig:
            self.cnt[skey] += inc
            me = (skey, self.cnt[skey])
        else:
            me = (skey, self.cnt[skey] + inc)
        self.streams[eng].append((waits, fn, skey if sig else None, inc))
        for b in reads:
            self.readers.setdefault(b, []).append(me)
        for b in writes:
            self.last_w[b] = me
            self.readers[b] = []
        return me

    def barrier(self):
        for e in ENGS:
            waits = []
            for s, v in self.cnt.items():
                if v > 0 and self.waited[e].get(s, 0) < v and not (s == "E_" + e and e in ("tensor",)):
                    self.waited[e][s] = v
                    waits.append((s, v))
            if waits:
                self.streams[e].append((waits, None, None, 0))
        self.last_w = {}
        self.readers = {}

    def emit(self):
        nc = self.nc
        sems = self.sems
        streams = self.streams
        finals = [(k, v) for k, v in self.cnt.items() if v > 0]

        def runner(ename):
            def run(eng):
                for waits, fn, skey, inc in streams[ename]:
                    for s, v in waits:
                        eng.wait_ge(sems[s], v)
                    if fn is None:
                        continue
                    ins = fn(eng)
                    if skey is not None:
                        ins.then_inc(sems[skey], inc)
                if ename == "sync":
                    for k, v in finals:
                        eng.wait_ge(sems[k], v)
            return run

        with nc.Block() as block:
            block.sync(runner("sync"))
            block.tensor(runner("tensor"))
            block.vector(runner("vector"))
            block.scalar(runner("scalar"))
            block.gpsimd(runner("gpsimd"))


class Arena:
    def __init__(self, t, words):
        self.t = t
        self.words = words
        self.ptr = 0
        self.hi = 0

    @staticmethod
    def _shape(ap, shape):
        if len(shape) == 1:
            return ap
        names = "abcd"[: len(shape)]
        kw = {names[i]: int(shape[i]) for i in range(len(shape) - 1)}
        return ap.rearrange("p (" + " ".join(names) + ") -> p " + " ".join(names), **kw)

    def f32(self, *shape):
        n = int(np.prod(shape))
        ap = self.t[:, self.ptr:self.ptr + n]
        self.ptr += n
        self.hi = max(self.hi, self.ptr)
        assert self.ptr <= self.words, ("arena overflow", self.ptr)
        return self._shape(ap, shape)

    def i32(self, *shape):
        return self.f32(*shape).bitcast(I32)

    def bf16(self, *shape):
        n = int(np.prod(shape))
        nw = (n + 1) // 2
        ap = self.t[:, self.ptr:self.ptr + nw].bitcast(BF16)
        if 2 * nw != n:
            ap = ap[:, 0:n]
        self.ptr += nw
        self.hi = max(self.hi, self.ptr)
        assert self.ptr <= self.words, ("arena overflow", self.ptr)
        return self._shape(ap, shape)


class Builder:
    def __init__(self, S, L):
        assert S % T == 0
        self.S = S
        self.L = L
        self.NT = S // T
        self.alpha = float((2 * L) ** 0.25)
        self.nc = bass.Bass("TRN2", target_bir_lowering=False)
        self.es = ExitStack()
        self.tr = Tracker(self.nc, self.es)
        self.din = {}
        self.uid = 0

    def dram_in(self, name, shape):
        self.din[name] = self.nc.dram_tensor(name, list(shape), F32, kind="ExternalInput").ap()
        return self.din[name]

    def mm(self, out, lhsT, rhs, start=True, stop=True, r=(), w=(), tp=None):
        if tp is None:
            fn = lambda e: e.matmul(out, lhsT=lhsT, rhs=rhs, start=start, stop=stop)
        else:
            fn = lambda e: e.matmul(out, lhsT=lhsT, rhs=rhs, start=start, stop=stop, tile_position=tp)
        self.tr.op("tensor", fn, reads=r, writes=w, sig=stop)

    def act(self, out, in_, func, r=(), w=(), bias=0.0, scale=1.0):
        self.tr.op("scalar", lambda e: e.activation(out=out, in_=in_, func=func, bias=bias, scale=scale),
                   reads=r, writes=w)

    def tt(self, out, in0, in1, op, r=(), w=(), eng="vector"):
        self.tr.op(eng, lambda e: e.tensor_tensor(out=out, in0=in0, in1=in1, op=op), reads=r, writes=w)

    def ts(self, out, in0, s1, op0, s2=None, op1=None, r=(), w=(), eng="vector"):
        if op1 is None:
            fn = lambda e: e.tensor_scalar(out=out, in0=in0, scalar1=s1, scalar2=None, op0=op0)
        else:
            fn = lambda e: e.tensor_scalar(out=out, in0=in0, scalar1=s1, scalar2=s2, op0=op0, op1=op1)
        self.tr.op(eng, fn, reads=r, writes=w)

    def stt(self, out, in0, scalar, in1, op0, op1, r=(), w=(), eng="vector"):
        self.tr.op(eng, lambda e: e.scalar_tensor_tensor(out=out, in0=in0, scalar=scalar, in1=in1, op0=op0, op1=op1),
                   reads=r, writes=w)

    def cp(self, out, in_, r=(), w=(), eng="vector"):
        self.tr.op(eng, lambda e: e.tensor_copy(out=out, in_=in_), reads=r, writes=w)

    def memset(self, ap, val, w=(), eng="gpsimd"):
        self.tr.op(eng, lambda e: e.memset(ap, val), writes=w)

    def recip(self, out, in_, r=(), w=()):
        self.tr.op("vector", lambda e: e.reciprocal(out=out, in_=in_), reads=r, writes=w)

    def dma(self, out, in_, stream, r=(), w=(), q="sync"):
        self.tr.op(q, lambda e: e.dma_start(out=out, in_=in_), reads=r, writes=w, dma=stream)

    def key(self, base):
        self.uid += 1
        return f"{base}#{self.uid}"

    def build(self):
        nc, es, S, L = self.nc, self.es, self.S, self.L
        d = self.dram_in
        d("x_fm", [D, S]); d("cvec", [128, KC])
        d("w_ada", [L, D, 6 * D]); d("b_ada", [L, 128, 48])
        d("w_in", [L, D, INW]); d("b_in", [L, 128, ZC])
        d("sc_w", [L, 128, 6]); d("pool_bd", [L, 2, 128, 128]); d("pool_scale", [L, 128, 2])
        d("cf_w", [L, 128, 62]); d("cf_b", [L, 128, 2]); d("cf_g", [L, 128, 2]); d("cf_beta", [L, 128, 2])
        d("lam_i", [L, 3, 128, 128]); d("bt_i", [L, 2, 128, 128])
        d("lam_ii", [L, 3, 128, 256]); d("c_a", [L, 128, 256]); d("c_sw", [L, 128, 256]); d("b_a", [L, 128, 256]); d("b_sw", [L, 128, 256])
        d("ssm_d", [L, 128, 2]); d("w_glu", [L, 256, 256]); d("b_glu", [L, 128, 2])
        d("w_o", [L, D, D]); d("ln1", [L, 128, 16])
        d("w_gate", [L, D, DFF]); d("w_up", [L, D, DFF]); d("w_down", [L, DFF, D]); d("ln2", [L, 128, 16])
        d("cmat", [3, 128, 128]); d("cvecs", [128, 40])
        self.y = nc.dram_tensor("y_fm", [D, S], F32, kind="ExternalOutput").ap()
        self.xa = nc.dram_tensor("xa_scr", [D, S], F32, kind="Internal").ap()
        self.xb = nc.dram_tensor("xb_scr", [D, S], F32, kind="Internal").ap()

        arena_t = es.enter_context(nc.sbuf_tensor("arena", [128, ARENA_WORDS], F32))
        self.A = A = Arena(arena_t, ARENA_WORDS)
        self.ps = [es.enter_context(nc.psum_tensor(f"psb{i}", [128, 512], F32)) for i in range(8)]

        self.ident = A.f32(128); self.jmat = A.f32(128); self.ones_cf = A.f32(128)
        self.ones_ln = A.bf16(128); self.ones_cfb = A.bf16(128)
        self.cv = A.f32(40)
        self.cond = A.f32(KC)
        self.mods = A.f32(L, 48)
        self.dma(self.ident, self.din["cmat"][0], "c0", w=["ident"])
        self.dma(self.jmat, self.din["cmat"][1], "c0", w=["jmat"])
        self.dma(self.ones_cf, self.din["cmat"][2], "c0", w=["ones_cf"])
        self.dma(self.cv, self.din["cvecs"], "c0", w=["cv"])
        self.dma(self.cond, self.din["cvec"], "c0", w=["cond"])
        self.ts(self.ones_ln, self.ones_cf, 256.0 / D, ALU.mult, r=["ones_cf"], w=["ones_ln"])
        self.cp(self.ones_cfb, self.ones_cf, r=["ones_cf"], w=["ones_cfb"])
        self.act(self.cond, self.cond, AF.Silu, r=["cond"], w=["cond"])
        mark = A.ptr
        self.mods_alloc()
        for j6 in range(6):
            self.mods_chunk(0, j6)
        self.mods_finish(0)
        self.tr.barrier()
        self.ha = nc.dram_tensor("ha_scr", [D, S], BF16, kind="Internal").ap()
        self.hbs = nc.dram_tensor("hb_scr", [D, S], BF16, kind="Internal").ap()
        pre = self.prepass_thunks()
        later = [(l, j6) for l in range(1, L) for j6 in range(6)]
        for i in range(max(len(pre), len(later))):
            if i < len(later):
                l_, j6_ = later[i]
                self.mods_chunk(l_, j6_)
                if j6_ == 5:
                    self.mods_finish(l_)
            if i < len(pre):
                pre[i]()
        self.tr.barrier()
        for l in range(L):
            A.ptr = mark
            src = self.din["x_fm"] if l == 0 else self.xb
            self.phase_m(l, src, self.xa, self.ha, self.hbs)
            self.tr.barrier()
            A.ptr = mark
            dst = self.y if l == L - 1 else self.xb
            self.phase_f(l, self.xa, dst, self.hbs, self.ha)
            self.tr.barrier()
        self.tr.emit()
        es.close()
        return nc

    def emit_h(self, hn, hkeys, xt, sc_col, sh_col, mods, hdst, t0):
        for c in range(KC):
            self.act(hn[:, c, :], xt[:, c, :], AF.Identity, r=[("x", c), "mods"], w=[hkeys[c]],
                     bias=mods[:, sh_col + c:sh_col + c + 1], scale=mods[:, sc_col + c:sc_col + c + 1])
        self.dma(hdst.rearrange("(c p) t -> p c t", p=128)[:, :, t0:t0 + T], hn, "hout", r=list(hkeys))

    def prepass_thunks(self):
        A = self.A
        xt2 = [A.f32(KC, T), A.f32(KC, T)]
        hn2 = [A.bf16(KC, T), A.bf16(KC, T)]
        mods = self.mods[:, 0, :]
        srcv = self.din["x_fm"].rearrange("(c p) t -> p c t", p=128)
        dstv = self.ha.rearrange("(c p) t -> p c t", p=128)

        def tile(it):
            b = it % 2
            t0 = it * T
            self.dma(xt2[b], srcv[:, :, t0:t0 + T], f"pin{b}", w=[("px", b)])
            for c in range(KC):
                self.act(hn2[b][:, c, :], xt2[b][:, c, :], AF.Identity, r=[("px", b), "mods0"], w=[("ph", b)],
                         bias=mods[:, c:c + 1], scale=mods[:, 8 + c:9 + c])
            self.dma(dstv[:, :, t0:t0 + T], hn2[b], f"pout{b}", r=[("ph", b)])
        return [lambda it=it: tile(it) for it in range(self.NT)]

    def mods_alloc(self):
        A, L = self.A, self.L
        self.m_stg = [A.f32(KC, 1024), A.f32(KC, 1024)]
        self.m_bada = A.f32(L, 48)
        self.m_n = 0
        self.dma(self.m_bada, self.din["b_ada"].rearrange("l p j -> p l j"), "c0", w=["bada"])

    def mods_chunk(self, l, j6):
        pm = self.ps[l % 2]
        n = self.m_n
        self.m_n += 1
        sb = self.m_stg[n % 2]
        sk = f"stg{n % 2}"
        self.dma(sb, self.din["w_ada"][l, :, 1024 * j6:1024 * (j6 + 1)].rearrange("(kc p) n -> p kc n", p=128), sk, w=[sk])
        for jj in range(8):
            col = 8 * j6 + jj
            for kc in range(KC):
                self.mm(pm[:, col:col + 1], sb[:, kc, 128 * jj:128 * (jj + 1)], self.cond[:, kc:kc + 1],
                        start=(kc == 0), stop=(kc == KC - 1), r=[sk, "cond"], w=[("pm", l % 2)])

    def mods_finish(self, l):
        pm = self.ps[l % 2]
        mk = f"mods{l}"
        self.tt(self.mods[:, l, :], pm[:, 0:48], self.m_bada[:, l, :], ALU.add, r=[("pm", l % 2), "bada"], w=[mk])
        al = 1.0 / self.alpha
        m = self.mods[:, l, :]
        self.ts(m[:, 8:16], m[:, 8:16], 1.0, ALU.add, r=[mk], w=[mk])
        self.ts(m[:, 32:40], m[:, 32:40], 1.0, ALU.add, r=[mk], w=[mk])
        self.ts(m[:, 16:24], m[:, 16:24], 1.0, ALU.add, al, ALU.mult, r=[mk], w=[mk])
        self.ts(m[:, 40:48], m[:, 40:48], 1.0, ALU.add, al, ALU.mult, r=[mk], w=[mk])

    def load_cast(self, jobs, stgs, engs=("gpsimd", "vector", "scalar")):
        for i, (dst, src, key) in enumerate(jobs):
            n = dst.shape[-1]
            k = i % len(stgs)
            sk = f"wstg{k}"
            self.dma(stgs[k][:, 0:n], src, sk, w=[sk], q="sync")
            e = engs[i % len(engs)]
            if e == "scalar":
                self.act(dst, stgs[k][:, 0:n], AF.Copy, r=[sk], w=[key])
            else:
                self.cp(dst, stgs[k][:, 0:n], r=[sk], w=[key], eng=e)

    def stats(self, pmean, kmean, pmsq, kmsq, st, eps):
        mean_sb, mm_, rstd = st
        self.act(mean_sb, pmean[:, :], AF.Copy, r=[kmean], w=["st_mean"])
        self.tt(mm_, mean_sb, mean_sb, ALU.mult, r=["st_mean"], w=["st_mm"])
        self.tt(mm_, pmsq[:, :], mm_, ALU.subtract, r=[kmsq, "st_mm"], w=["st_mm"])
        self.ts(mm_, mm_, 0.0, ALU.max, float(eps), ALU.add, r=["st_mm"], w=["st_mm"])
        self.act(mm_, mm_, AF.Sqrt, r=["st_mm"], w=["st_mm"])
        self.recip(rstd, mm_, r=["st_mm"], w=["st_rstd"])

    def sin_red(self, out, th, shift, tmp, tmpi, r, w):
        kt = self.key("sr")
        self.ts(out, th, float(shift), ALU.add, r=r, w=w)
        self.ts(tmp, out, float(1.0 / (2 * PI)), ALU.mult, r=w, w=[kt])
        self.cp(tmpi, tmp, r=[kt], w=[kt + "i"])
        self.cp(tmp, tmpi, r=[kt + "i"], w=[kt])
        self.stt(out, tmp, -2 * PI, out, ALU.mult, ALU.add, r=[kt] + list(w), w=w)
        self.ts(tmp, out, PI, ALU.is_gt, r=w, w=[kt])
        self.stt(out, tmp, -2 * PI, out, ALU.mult, ALU.add, r=[kt] + list(w), w=w)
        self.ts(tmp, out, -PI, ALU.is_lt, r=w, w=[kt])
        self.stt(out, tmp, 2 * PI, out, ALU.mult, ALU.add, r=[kt] + list(w), w=w)
        self.act(out, out, AF.Sin, r=w, w=w)

    def lam_bar(self, lam, F, pre):
        A = self.A
        k = lambda n: f"{pre}_{n}"
        dt = A.f32(F); rho = A.f32(F); th = A.f32(F); sn = A.f32(F); cs = A.f32(F)
        tmp = A.f32(F); tmpi = A.i32(F)
        lbre = A.f32(F); lbim = A.f32(F)
        self.act(dt, lam[:, 2, :], AF.Exp, r=[k("lam")], w=[k("dt")])
        self.tt(rho, lam[:, 0, :], dt, ALU.mult, r=[k("lam"), k("dt")], w=[k("rho")])
        self.tt(th, lam[:, 1, :], dt, ALU.mult, r=[k("lam"), k("dt")], w=[k("th")])
        self.act(rho, rho, AF.Exp, r=[k("rho")], w=[k("rho")])
        self.sin_red(sn, th, 0.0, tmp, tmpi, r=[k("th")], w=[k("sn")])
        self.sin_red(cs, th, PI / 2, tmp, tmpi, r=[k("th")], w=[k("cs")])
        self.tt(lbre, rho, cs, ALU.mult, r=[k("rho"), k("cs")], w=[k("lbre")])
        self.tt(lbim, rho, sn, ALU.mult, r=[k("rho"), k("sn")], w=[k("lbim")])
        lr, li = lam[:, 0, :], lam[:, 1, :]
        nr = A.f32(F); den = A.f32(F); qre = A.f32(F); qim = A.f32(F); ta = A.f32(F); tb = A.f32(F)
        self.ts(nr, lbre, -1.0, ALU.add, r=[k("lbre")], w=[k("nr")])
        self.tt(den, lr, lr, ALU.mult, r=[k("lam")], w=[k("den")])
        self.tt(ta, li, li, ALU.mult, r=[k("lam")], w=[k("ta")])
        self.tt(den, den, ta, ALU.add, r=[k("den"), k("ta")], w=[k("den")])
        self.recip(den, den, r=[k("den")], w=[k("den")])
        self.tt(qre, nr, lr, ALU.mult, r=[k("nr"), k("lam")], w=[k("qre")])
        self.tt(ta, lbim, li, ALU.mult, r=[k("lbim"), k("lam")], w=[k("ta")])
        self.tt(qre, qre, ta, ALU.add, r=[k("qre"), k("ta")], w=[k("qre")])
        self.tt(qre, qre, den, ALU.mult, r=[k("qre"), k("den")], w=[k("qre")])
        self.tt(qim, lbim, lr, ALU.mult, r=[k("lbim"), k("lam")], w=[k("qim")])
        self.tt(tb, nr, li, ALU.mult, r=[k("nr"), k("lam")], w=[k("tb")])
        self.tt(qim, qim, tb, ALU.subtract, r=[k("qim"), k("tb")], w=[k("qim")])
        self.tt(qim, qim, den, ALU.mult, r=[k("qim"), k("den")], w=[k("qim")])
        return (lbre, k("lbre")), (lbim, k("lbim")), (qre, k("qre")), (qim, k("qim"))

    def cmul(self, ore, oim, a_re, a_im, b_re, b_im, t1, t2):
        self.tt(t1[0], a_re[0], b_re[0], ALU.mult, r=[a_re[1], b_re[1]], w=[t1[1]])
        self.tt(t2[0], a_im[0], b_im[0], ALU.mult, r=[a_im[1], b_im[1]], w=[t2[1]])
        self.tt(ore[0], t1[0], t2[0], ALU.subtract, r=[t1[1], t2[1]], w=[ore[1]])
        self.tt(t1[0], a_re[0], b_im[0], ALU.mult, r=[a_re[1], b_im[1]], w=[t1[1]])
        self.tt(t2[0], a_im[0], b_re[0], ALU.mult, r=[a_im[1], b_re[1]], w=[t2[1]])
        self.tt(oim[0], t1[0], t2[0], ALU.add, r=[t1[1], t2[1]], w=[oim[1]])

    def ssm_prep(self, l, BinT, CoutT, AT, ToepT, ssd):
        A = self.A
        cv = self.cv
        ps = self.ps
        lam = A.f32(3, 128); bt = A.f32(2, 128)
        self.dma(lam, self.din["lam_i"][l].rearrange("k p f -> p k f"), "c0", w=["i_lam"])
        self.dma(bt, self.din["bt_i"][l].rearrange("k p f -> p k f"), "c0", w=["bt"])
        lbre, lbim, qre, qim = self.lam_bar(lam, 128, "i")
        t1 = (A.f32(128), "i_t1"); t2 = (A.f32(128), "i_t2")
        bbre = (A.f32(128), "bbre"); bbim = (A.f32(128), "bbim")
        self.cmul(bbre, bbim, qre, qim, (bt[:, 0, :], "bt"), (bt[:, 1, :], "bt"), t1, t2)
        bbp = []
        for par in range(2):
            pr = (A.f32(128), f"bbpr{par}"); pi_ = (A.f32(128), f"bbpi{par}")
            self.ts(pr[0], bbre[0], cv[:, 1 + par:2 + par], ALU.mult, r=["bbre", "cv"], w=[pr[1]])
            self.ts(pi_[0], bbim[0], cv[:, 1 + par:2 + par], ALU.mult, r=["bbim", "cv"], w=[pi_[1]])
            bbp.append((pr, pi_))
        v3 = lambda ap: ap.rearrange("p (c f) -> p c f", c=2)
        pw_re, pw_im = lbre, lbim
        for m in range(SL):
            sp = SL - 1 - m
            for par in range(2):
                o_re = (BinT[:, :, par, sp, 0:64], "BinT"); o_im = (BinT[:, :, par, sp, 64:128], "BinT")
                if m == 0:
                    self.cp(o_re[0], v3(bbp[par][0][0]), r=[bbp[par][0][1]], w=["BinT"])
                    self.cp(o_im[0], v3(bbp[par][1][0]), r=[bbp[par][1][1]], w=["BinT"])
                else:
                    a_re, a_im = pw_re, pw_im
                    b_re, b_im = bbp[par]
                    self.tt(t1[0], a_re[0], b_re[0], ALU.mult, r=[a_re[1], b_re[1]], w=[t1[1]])
                    self.tt(t2[0], a_im[0], b_im[0], ALU.mult, r=[a_im[1], b_im[1]], w=[t2[1]])
                    self.tt(o_re[0], v3(t1[0]), v3(t2[0]), ALU.subtract, r=[t1[1], t2[1]], w=["BinT"])
                    self.tt(t1[0], a_re[0], b_im[0], ALU.mult, r=[a_re[1], b_im[1]], w=[t1[1]])
                    self.tt(t2[0], a_im[0], b_re[0], ALU.mult, r=[a_im[1], b_re[1]], w=[t2[1]])
                    self.tt(o_im[0], v3(t1[0]), v3(t2[0]), ALU.add, r=[t1[1], t2[1]], w=["BinT"])
            if 1 <= m < SL - 1:
                n_re = (A.f32(128), f"ipw_re{m + 1}"); n_im = (A.f32(128), f"ipw_im{m + 1}")
                self.cmul(n_re, n_im, pw_re, pw_im, lbre, lbim, t1, t2)
                pw_re, pw_im = n_re, n_im
        FW = 256
        lam2 = A.f32(3, FW); ca = A.f32(FW); csw = A.f32(FW); ba = A.f32(FW); bsw = A.f32(FW)
        self.dma(lam2, self.din["lam_ii"][l].rearrange("k p f -> p k f"), "c0", w=["ii_lam"])
        self.dma(ca, self.din["c_a"][l], "c0", w=["ca"])
        self.dma(csw, self.din["c_sw"][l], "c0", w=["csw"])
        self.dma(ba, self.din["b_a"][l], "c0", w=["ba"])
        self.dma(bsw, self.din["b_sw"][l], "c0", w=["bsw"])
        l2re, l2im, q2re, q2im = self.lam_bar(lam2, FW, "ii")
        u1 = (A.f32(FW), "ii_t1"); u2 = (A.f32(FW), "ii_t2")
        bb2 = A.f32(FW)
        self.ts(u1[0], q2im[0], cv[:, 0:1], ALU.mult, -1.0, ALU.mult, r=[q2im[1], "cv"], w=[u1[1]])
        self.tt(u2[0], bsw, u1[0], ALU.mult, r=["bsw", u1[1]], w=[u2[1]])
        self.tt(bb2, ba, q2re[0], ALU.mult, r=["ba", q2re[1]], w=["bb2"])
        self.tt(bb2, bb2, u2[0], ALU.add, r=["bb2", u2[1]], w=["bb2"])
        bbz = A.f32(16, 128)
        self.memset(bbz, 0.0, w=["bbz"])
        bb2v = bb2.rearrange("p (g h) -> p g h", g=16)
        for g in range(16):
            gl = g % 8
            self.cp(bbz[:, g, 16 * gl:16 * gl + 16], bb2v[:, g, :], r=["bb2"], w=["bbz"], eng=("gpsimd" if g % 2 else "vector"))
        x1 = A.f32(FW); x2 = A.f32(FW)
        self.ts(x1, ca, cv[:, 0:1], ALU.mult, r=["ca", "cv"], w=["x1"])
        self.ts(x2, csw, -1.0, ALU.mult, r=["csw"], w=["x2"])
        cl = A.f32(SL + 1, FW)
        self.cp(cl[:, 0, :], x1, r=["x1"], w=[("cl", 0)])
        q_re, q_im = l2re, l2im
        for m in range(1, SL + 1):
            if m > 1:
                n_re = (A.f32(FW), f"q_re{m}"); n_im = (A.f32(FW), f"q_im{m}")
                self.cmul(n_re, n_im, q_re, q_im, l2re, l2im, u1, u2)
                q_re, q_im = n_re, n_im
            self.tt(u1[0], x1, q_re[0], ALU.mult, r=["x1", q_re[1]], w=[u1[1]])
            self.tt(u2[0], x2, q_im[0], ALU.mult, r=["x2", q_im[1]], w=[u2[1]])
            self.tt(cl[:, m, :], u1[0], u2[0], ALU.add, r=[u1[1], u2[1]], w=[("cl", m)])
        are = A.f32(NRS, 16); aim = A.f32(NRS, 16); aims = A.f32(NRS, 16)
        self.cp(are[:, 0, :], q_re[0].rearrange("p (g h) -> p g h", h=16)[:, :, 0], r=[q_re[1]], w=[("are", 0)])
        self.cp(aim[:, 0, :], q_im[0].rearrange("p (g h) -> p g h", h=16)[:, :, 0], r=[q_im[1]], w=[("aim", 0)])
        for k in range(1, NRS):
            pr, pi_ = are[:, k - 1, :], aim[:, k - 1, :]
            self.tt(u1[0][:, 0:16], pr, pr, ALU.mult, r=[("are", k - 1)], w=[u1[1]])
            self.tt(u2[0][:, 0:16], pi_, pi_, ALU.mult, r=[("aim", k - 1)], w=[u2[1]])
            self.tt(are[:, k, :], u1[0][:, 0:16], u2[0][:, 0:16], ALU.subtract, r=[u1[1], u2[1]], w=[("are", k)])
            self.stt(aim[:, k, :], pr, 2.0, pi_, ALU.mult, ALU.mult, r=[("are", k - 1), ("aim", k - 1)], w=[("aim", k)])
        for k in range(NRS):
            self.ts(aims[:, k, :], aim[:, k, :], cv[:, 0:1], ALU.mult, r=[("aim", k), "cv"], w=[("aims", k)])
        for k in range(NRS):
            for g in range(16):
                kk = ("AT", k, g)
                self.act(AT[:, k, g, :], self.ident, AF.Identity, r=["ident", ("are", k)], w=[kk],
                         scale=are[:, k, g:g + 1])
                self.stt(AT[:, k, g, :], self.jmat, aims[:, k, g:g + 1], AT[:, k, g, :], ALU.mult, ALU.add,
                         r=["jmat", ("aims", k), kk], w=[kk])
        cov = CoutT.rearrange("p (q t) s m -> p q t s m", t=2)
        for s_ in range(SL):
            clv = cl[:, s_ + 1, :].rearrange("p (q t h) -> p q t h", t=2, h=16)
            for par in range(2):
                self.cp(cov[:, :, par, s_, 16 * par:16 * par + 16], clv[:, :, par, :], r=[("cl", s_ + 1)], w=["CoutT"],
                        eng=("gpsimd" if par else "vector"))
        clg = cl.rearrange("p m (g h) -> p m g h", g=16)
        for bank in range(2 * SL // 4):
            for i4 in range(4):
                b_ = bank * 4 + i4
                c, tau = b_ // SL, b_ % SL
                col = i4 * 128
                for gl in range(8):
                    g = 8 * c + gl
                    self.mm(ps[bank][:, col + 16 * gl:col + 16 * gl + 16], bbz[:, g, :], clg[:, tau, g, :],
                            r=["bbz", ("cl", tau)], w=[("ps", bank)])
            for i4 in range(4):
                b_ = bank * 4 + i4
                c, tau = b_ // SL, b_ % SL
                col = i4 * 128
                if tau == 0:
                    self.stt(ToepT[:, c, 0, :], self.ident, ssd[:, c:c + 1], ps[bank][:, col:col + 128], ALU.mult, ALU.add,
                             r=["ident", "par_ssm_d", ("ps", bank)], w=["ToepT"])
                else:
                    self.act(ToepT[:, c, tau, :], ps[bank][:, col:col + 128], AF.Copy, r=[("ps", bank)], w=["ToepT"])

    def phase_m(self, l, src, dst, hsrc, hdst):
        A, NT = self.A, self.NT
        ps = self.ps
        din = self.din
        mods = self.mods[:, l, :]
        w_in = A.bf16(KC, INW); w_o = A.bf16(KC, D); glu = A.bf16(2, 256); poolbd = A.bf16(2, 128)
        BinT = A.bf16(2, 2, SL, 128); CoutT = A.bf16(16, SL, 32); AT = A.bf16(NRS, 16, 128); ToepT = A.bf16(2, SL, 128)
        cdiag = A.bf16(2, 31, 128)
        b_in = A.f32(ZC); scw = A.f32(2, 3); pscale = A.f32(2); cfw = A.f32(2, 31)
        cfb = A.f32(2); cfg = A.f32(2); cfbeta = A.f32(2); ssd = A.f32(2); bglu = A.f32(2); ln1 = A.f32(16)
        for (ap, nm, re) in ((b_in, "b_in", None), (scw, "sc_w", "p (c k) -> p c k"), (pscale, "pool_scale", None),
                             (cfw, "cf_w", "p (c k) -> p c k"), (cfb, "cf_b", None), (cfg, "cf_g", None),
                             (cfbeta, "cf_beta", None), (ssd, "ssm_d", None), (bglu, "b_glu", None), (ln1, "ln1", None)):
            s_ = din[nm][l]
            if re is not None:
                s_ = s_.rearrange(re, c=2)
            self.dma(ap, s_, "c0", w=["par_" + nm])
        self.memset(CoutT, 0.0, w=["CoutT"])
        for j in range(2):
            for k in range(31):
                self.act(cdiag[:, j, k, :], self.ident, AF.Identity, r=["ident", "par_cf_w"], w=["cdiag"], scale=cfw[:, j, k:k + 1])
        m0 = A.ptr
        stgs = [A.f32(INW), A.f32(INW)]
        self.ssm_prep(l, BinT, CoutT, AT, ToepT, ssd)
        jobs = []
        for kc in range(KC):
            jobs.append((w_in[:, kc, :], din["w_in"][l, 128 * kc:128 * kc + 128, :], "w_in"))
        for kc in range(KC):
            jobs.append((w_o[:, kc, :], din["w_o"][l, 128 * kc:128 * kc + 128, :], "w_o"))
        for kc in range(2):
            jobs.append((glu[:, kc, :], din["w_glu"][l, 128 * kc:128 * kc + 128, :], "glu"))
            jobs.append((poolbd[:, kc, :], din["pool_bd"][l, kc], "poolbd"))
        self.load_cast(jobs, stgs, engs=("gpsimd",))
        self.tr.barrier()
        A.ptr = m0
        xt = A.f32(KC, T); hb2 = [A.bf16(KC, T), A.bf16(KC, T)]; ycat = A.bf16(KC, T)
        zh = A.f32(2, T); zb = A.f32(2, T); zc_ = A.f32(2, T); zv = A.f32(2, T + 16); sg = A.f32(2, T + 16)
        zp = A.f32(2, T + 15); pbuf = A.f32(2, T + 2); hbuf = A.bf16(2, T + 30)
        u = A.bf16(2, T); mb = A.bf16(2, T); yg = A.bf16(2, T); sig = A.bf16(T)
        scr0 = A.f32(2, T + 16); accb = A.bf16(2, T); sqb = A.bf16(2, T); scr2 = A.f32(2, T)
        Sx = A.bf16(16, NJ + 2)
        st = (A.f32(T), A.f32(T), A.f32(T))
        self.memset(zp, 0.0, w=[("zp", 0), ("zp", 1), ("z", 6), ("z", 7)])
        self.memset(pbuf, 0.0, w=[("pbuf", 0), ("pbuf", 1)])
        self.memset(hbuf, 0.0, w=[("hbuf", 0), ("hbuf", 1)])
        self.memset(Sx, 0.0, w=[("Sx", q_) for q_ in range(4)])
        srcv = src.rearrange("(c p) t -> p c t", p=128)
        dstv = dst.rearrange("(c p) t -> p c t", p=128)
        hsrcv = hsrc.rearrange("(c p) t -> p c t", p=128)
        xk = [("x", c) for c in range(KC)]
        eps1 = LN_EPS / (self.alpha ** 2)
        zdest = {0: (zh, 0, 0), 1: (zh, 1, 0), 2: (zb, 0, 0), 3: (zb, 1, 0), 4: (zc_, 0, 0), 5: (zc_, 1, 0),
                 6: (zp, 0, 15), 7: (zp, 1, 15), 8: (zv, 0, 0), 9: (zv, 1, 0), 10: (sg, 0, 0), 11: (sg, 1, 0)}
        zorder = [8, 10, 9, 11, 12, 13, 4, 0, 2, 5, 1, 3, 6, 7]
        uv = [u[:, c, :].rearrange("p (s j) -> p s j", s=SL) for c in range(2)]
        u_tok = [u[:, c, :].rearrange("p (s j) -> p j s", s=SL) for c in range(2)]
        ygv = [yg[:, c, :].rearrange("p (j s) -> p s j", s=SL) for c in range(2)]
        VB = [3, 4, 6, 7]
        psVq = [ps[VB[q]][:, 0:4 * NJ].rearrange("p (c t j) -> p c t j", c=2, t=2) for q in range(4)]
        Sxv = Sx.rearrange("p (c q t) j -> p c q t j", c=2, q=4)
        psY2 = [ps[5][:, :].rearrange("p (s j) -> p s j", s=SL), ps[2][:, :].rearrange("p (s j) -> p s j", s=SL)]
        YB = [5, 2]
        self.zrot = 0

        def head_load(i):
            b = i % 2
            self.dma(hb2[b], hsrcv[:, :, i * T:(i + 1) * T], f"hin{b}", w=[("hb", b, c) for c in range(KC)])

        def xload(i):
            self.dma(xt, srcv[:, :, i * T:(i + 1) * T], "xin", w=xk)

        def inproj(i):
            b = i % 2
            for zi in zorder:
                bk = self.zrot % 2
                self.zrot += 1
                for kc in range(KC):
                    self.mm(ps[bk][:, :], w_in[:, kc, 128 * zi:128 * zi + 128], hb2[b][:, kc, :],
                            start=(kc == 0), stop=(kc == KC - 1), r=["w_in", ("hb", b, kc)], w=[("ps", bk)])
                if zi >= 12:
                    c = zi - 12
                    self.act(u_tok[c], ps[bk][:, :].rearrange("p (j s) -> p j s", s=SL), AF.Identity,
                             r=[("ps", bk), "par_b_in"], w=[("z", zi)], bias=b_in[:, zi:zi + 1])
                    continue
                tl, j, off = zdest[zi]
                fn = AF.Sigmoid if zi in (10, 11) else AF.Identity
                self.act(tl[:, j, off:off + T], ps[bk][:, :], fn, r=[("ps", bk), "par_b_in"], w=[("z", zi)],
                         bias=b_in[:, zi:zi + 1])

        def conv_h():
            for j in range(2):
                self.tt(hbuf[:, j, 30:T + 30], zv[:, j, 0:T], sg[:, j, 0:T], ALU.mult, r=[("z", 8 + j), ("z", 10 + j)], w=[("hbuf", j)])

        def conv_thunks():
            th = []
            for j in range(2):
                hk, ak = ("hbuf", j), ("scr0", j)
                for k in range(31):
                    th.append(lambda j=j, k=k, hk=hk: self.mm(ps[2][:, :], cdiag[:, j, k, :], hbuf[:, j, k:k + T], start=(k == 0), stop=(k == 30),
                                                               r=["cdiag", hk], w=[("ps", 2)]))

                def fin(j=j, hk=hk, ak=ak):
                    self.act(scr0[:, j, 0:T], ps[2][:, :], AF.Identity, r=[("ps", 2), "par_cf_b"], w=[ak], bias=cfb[:, j:j + 1])
                    self.cp(hbuf[:, j, 0:30], hbuf[:, j, T:T + 30], r=[hk], w=[hk], eng="gpsimd")
                    self.act(sqb[:, j, :], scr0[:, j, 0:T], AF.Square, r=[ak], w=[("sqb", j)])
                    self.act(accb[:, j, :], scr0[:, j, 0:T], AF.Copy, r=[ak], w=[("accb", j)])
                th.append(fin)
            return th

        def pool(it):
            for j in range(2):
                zk, ka, kb = ("z", 6 + j), ("z", 8 + j), ("z", 10 + j)
                sa, sb_, zz = zv[:, j, :], sg[:, j, :], zp[:, j, :]
                self.tt(sa[:, 0:T + 14], zz[:, 1:T + 15], zz[:, 0:T + 14], ALU.add, r=[zk, ("zp", j)], w=[ka])
                if j == 0:
                    self.tt(sb_[64:128, 0:T + 12], sa[64:128, 2:T + 14], sa[64:128, 0:T + 12], ALU.add, r=[ka], w=[kb])
                    lo, lo_off, hi, hi_off = sa, 14, sb_, 12
                    wl, wh = 2.0, 4.0
                else:
                    self.tt(sb_[:, 0:T + 12], sa[:, 2:T + 14], sa[:, 0:T + 12], ALU.add, r=[ka], w=[kb])
                    self.tt(sa[:, 0:T + 8], sb_[:, 4:T + 12], sb_[:, 0:T + 8], ALU.add, r=[kb, ka], w=[ka])
                    self.tt(sb_[64:128, 0:T], sa[64:128, 8:T + 8], sa[64:128, 0:T], ALU.add, r=[ka, kb], w=[kb])
                    lo, lo_off, hi, hi_off = sa, 8, sb_, 0
                    wl, wh = 8.0, 16.0
                self.stt(mb[0:64, j, :], lo[0:64, lo_off:lo_off + T], 1.0 / wl, zz[0:64, 15:T + 15], ALU.mult, ALU.subtract,
                         r=[ka, kb, zk], w=[("mb", j)])
                self.stt(mb[64:128, j, :], hi[64:128, hi_off:hi_off + T], 1.0 / wh, zz[64:128, 15:T + 15], ALU.mult, ALU.subtract,
                         r=[ka, kb, zk], w=[("mb", j)])
                if it == 0:
                    rc = self.cv[:, 8 + 16 * j:24 + 16 * j]
                    tmp = scr2[:, j, 0:16]
                    for (pl, ph, sbuf_, off) in ((0, 64, lo, lo_off), (64, 128, hi, hi_off)):
                        self.tt(tmp[pl:ph, :], sbuf_[pl:ph, off:off + 16], rc[pl:ph, :], ALU.mult, r=[ka, kb, "cv"], w=[("scr2", j)])
                        self.tt(mb[pl:ph, j, 0:16], tmp[pl:ph, :], zz[pl:ph, 15:31], ALU.subtract,
                                r=[("scr2", j), zk], w=[("mb", j)])
                self.cp(zz[:, 0:15], zz[:, T:T + 15], r=[zk, ka, kb, ("mb", j)], w=[("zp", j)], eng="gpsimd")

        def pool_mm():
            for j in range(2):
                bk = self.zrot % 2
                self.zrot += 1
                self.mm(ps[bk][:, :], poolbd[:, j, :], mb[:, j, :], r=["poolbd", ("mb", j)], w=[("ps", bk)])
                self.act(ycat[:, 2 + j, :], ps[bk][:, :], AF.Identity, r=[("ps", bk), "par_pool_scale"], w=[("ycat", 2 + j)],
                         scale=pscale[:, j:j + 1])

        def sconv():
            for j in range(2):
                pk_, ak = ("pbuf", j), ("scr2", j)
                acc = scr2[:, j, :]
                self.tt(pbuf[:, j, 2:T + 2], zc_[:, j, :], zh[:, j, :], ALU.mult, r=[("z", 4 + j), ("z", j)], w=[pk_])
                self.ts(acc, pbuf[:, j, 2:T + 2], scw[:, j, 2:3], ALU.mult, r=[pk_, "par_sc_w"], w=[ak])
                self.stt(acc, pbuf[:, j, 1:T + 1], scw[:, j, 1:2], acc, ALU.mult, ALU.add, r=[pk_, "par_sc_w", ak], w=[ak])
                self.stt(acc, pbuf[:, j, 0:T], scw[:, j, 0:1], acc, ALU.mult, ALU.add, r=[pk_, "par_sc_w", ak], w=[ak])
                self.tt(ycat[:, j, :], acc, zb[:, j, :], ALU.mult, r=[ak, ("z", 2 + j)], w=[("ycat", j)])
                self.cp(pbuf[:, j, 0:2], pbuf[:, j, T:T + 2], r=[pk_], w=[pk_], eng="gpsimd")

        def body(it, conv, prev):
            ssm = []

            def st_v():
                for q in range(4):
                    for c in range(2):
                        for par in range(2):
                            g = 8 * c + 2 * q + par
                            for sp in range(SL):
                                self.mm(psVq[q][:, c, par, :], BinT[32 * q:32 * q + 32, c, par, sp, :], uv[c][32 * q:32 * q + 32, sp, :],
                                        start=(sp == 0), stop=(sp == SL - 1 and it == 0),
                                        r=["BinT", ("z", 12 + c)], w=[("ps", VB[q])], tp=(32 * q, 0))
                            if it > 0:
                                self.mm(psVq[q][:, c, par, 0:1], AT[:, 0, g, :], Sx[:, g, 0:1], start=False, stop=True,
                                        r=[("AT", 0, g), ("Sx", q)], w=[("ps", VB[q])])
                    self.cp(Sxv[:, :, q, :, 1:NJ + 1], psVq[q], r=[("ps", VB[q])], w=[("Sx", q)])
            ssm.append(st_v)

            def st_round(k):
                sh = 1 << k
                for q in range(4):
                    for c in range(2):
                        for par in range(2):
                            g = 8 * c + 2 * q + par
                            self.mm(psVq[q][:, c, par, sh:NJ], AT[:, k, g, :], Sx[:, g, 1:NJ + 1 - sh],
                                    r=[("AT", k, g), ("Sx", q)], w=[("ps", VB[q])])
                    self.tt(Sxv[:, :, q, :, 1 + sh:NJ + 1], psVq[q][:, :, :, sh:NJ], Sxv[:, :, q, :, 1 + sh:NJ + 1],
                            ALU.add, r=[("ps", VB[q]), ("Sx", q)], w=[("Sx", q)])
            for k in range(NRS):
                ssm.append(lambda k=k: st_round(k))

            def st_out(c):
                psY, yk = psY2[c], ("ps", YB[c])
                for s_ in range(SL):
                    for sp in range(s_ + 1):
                        self.mm(psY[:, s_, :], ToepT[:, c, s_ - sp, :], uv[c][:, sp, :], start=(sp == 0), stop=False,
                                r=["ToepT", ("z", 12 + c)], w=[yk])
                    for q in range(4):
                        for par in range(2):
                            g = 8 * c + 2 * q + par
                            self.mm(psY[32 * q:32 * q + 32, s_, :], CoutT[:, g, s_, :], Sx[:, g, 0:NJ], start=False,
                                    stop=(par == 1), r=["CoutT", ("Sx", q)], w=[yk], tp=(0, 32 * q))
                self.act(ygv[c], psY, AF.Gelu_apprx_tanh, r=[yk], w=[("yg", c)])
                if c == 1:
                    for q in range(4):
                        self.cp(Sxv[:, :, q, :, 0:1], Sxv[:, :, q, :, NJ:NJ + 1], r=[("Sx", q)], w=[("Sx", q)], eng="gpsimd")
            ssm.append(lambda: st_out(0))
            ssm.append(lambda: st_out(1))

            def st_glu():
                for mo in range(2):
                    bk = 6 + mo
                    for kc in range(2):
                        self.mm(ps[bk][:, :], glu[:, kc, 128 * mo:128 * mo + 128], yg[:, kc, :], start=(kc == 0), stop=(kc == 1),
                                r=["glu", ("yg", kc)], w=[("ps", bk)])
                    self.act(sig, ps[bk][:, :], AF.Sigmoid, r=[("ps", bk), "par_b_glu"], w=["sig"], bias=bglu[:, mo:mo + 1])
                    self.tt(ycat[:, 6 + mo, :], yg[:, mo, :], sig, ALU.mult, r=[("yg", mo), "sig"], w=[("ycat", 6 + mo)])
            ssm.append(st_glu)
            st_v_, rounds, out0_, out1_, glu_ = ssm[0], ssm[1:1 + NRS], ssm[1 + NRS], ssm[2 + NRS], ssm[3 + NRS]
            st_v_()
            pool_mm()
            for _ in range(32):
                conv.pop(0)()
            for k in range(3):
                rounds[k]()
                for _ in range(11):
                    if conv:
                        conv.pop(0)()
            while conv:
                conv.pop(0)()
            conf_stats_pe()
            for k in range(3, NRS):
                rounds[k]()
            conf_norm_a()
            out0_()
            out1_()
            conf_norm_b()
            glu_()

        def conf_stats_pe():
            for j in range(2):
                self.mm(ps[0][:, :], self.ones_cfb, accb[:, j, :], start=(j == 0), stop=(j == 1),
                        r=["ones_cfb", ("accb", j)], w=[("ps", 0)])
            for j in range(2):
                self.mm(ps[1][:, :], self.ones_cfb, sqb[:, j, :], start=(j == 0), stop=(j == 1),
                        r=["ones_cfb", ("sqb", j)], w=[("ps", 1)])

        def conf_norm_a():
            self.stats(ps[0], ("ps", 0), ps[1], ("ps", 1), st, LN_EPS)
            for j in range(2):
                ak = ("scr0", j)
                self.tt(scr0[:, j, 0:T], scr0[:, j, 0:T], st[0], ALU.subtract, r=[ak, "st_mean"], w=[ak])
                self.tt(scr0[:, j, 0:T], scr0[:, j, 0:T], st[2], ALU.mult, r=[ak, "st_rstd"], w=[ak])

        def conf_norm_b():
            for j in range(2):
                ak = ("scr0", j)
                self.act(ycat[:, 4 + j, :], scr0[:, j, 0:T], AF.Silu, r=[ak, "par_cf_g", "par_cf_beta"], w=[("ycat", 4 + j)],
                         bias=cfbeta[:, j:j + 1], scale=cfg[:, j:j + 1])

        def tail_wo(i):
            for mp in range(0, KC, 2):
                for mo in (mp, mp + 1):
                    bk = mo % 2
                    for kc in range(6):
                        self.mm(ps[bk][:, :], w_o[:, kc, 128 * mo:128 * mo + 128], ycat[:, kc, :],
                                start=(kc == 0), stop=False, r=["w_o", ("ycat", kc)], w=[("ps", bk)])
                for mo in (mp, mp + 1):
                    bk = mo % 2
                    for kc in (6, 7):
                        self.mm(ps[bk][:, :], w_o[:, kc, 128 * mo:128 * mo + 128], ycat[:, kc, :],
                                start=False, stop=(kc == 7), r=["w_o", ("ycat", kc)], w=[("ps", bk)])
                    self.stt(xt[:, mo, :], ps[bk][:, :], mods[:, 16 + mo:17 + mo], xt[:, mo, :], ALU.mult, ALU.add,
                             r=[("ps", bk), "mods", ("x", mo)], w=[("x", mo)])

        def ln_a(i):
            b = i % 2
            for c in range(KC):
                self.act(hb2[b][:, c, :], xt[:, c, :], AF.Copy, r=[("x", c)], w=[("hb", b, c)])
                self.act(ycat[:, c, :], xt[:, c, :], AF.Square, r=[("x", c)], w=[("ycat", c)])

        def ln_b1(i):
            b = i % 2
            for c in range(KC):
                self.mm(ps[0][:, :], self.ones_ln, hb2[b][:, c, :], start=(c == 0), stop=(c == KC - 1),
                        r=["ones_ln", ("hb", b, c)], w=[("ps", 0)])
            for c in range(KC):
                self.mm(ps[1][:, :], self.ones_ln, ycat[:, c, :], start=(c == 0), stop=(c == KC - 1),
                        r=["ones_ln", ("ycat", c)], w=[("ps", 1)])
            self.stats(ps[0], ("ps", 0), ps[1], ("ps", 1), st, eps1)
            for c in range(KC):
                self.tt(xt[:, c, :], xt[:, c, :], st[0], ALU.subtract, r=[("x", c), "st_mean"], w=[("x", c)])
                self.tt(xt[:, c, :], xt[:, c, :], st[2], ALU.mult, r=[("x", c), "st_rstd"], w=[("x", c)])

        def ln_b2(i):
            b = i % 2
            for c in range(KC):
                self.act(xt[:, c, :], xt[:, c, :], AF.Identity, r=[("x", c), "par_ln1"], w=[("x", c)],
                         bias=ln1[:, 8 + c:9 + c], scale=ln1[:, c:c + 1])
            self.emit_h(hb2[b], [("hb", b, c) for c in range(KC)], xt, 32, 24, mods, hdst, i * T)
            self.dma(dstv[:, :, i * T:(i + 1) * T], xt, "xout", r=xk)

        head_load(0)
        xload(0)
        inproj(0)
        conv_h()
        pool(0)
        sconv()
        for it in range(NT):
            body(it, conv_thunks(), None)
            if it + 1 < NT:
                head_load(it + 1)
            tail_wo(it)
            ln_a(it)
            ln_b1(it)
            if it + 1 < NT:
                inproj(it + 1)
                ln_b2(it)
                xload(it + 1)
                conv_h()
                pool(it + 1)
                sconv()
            else:
                ln_b2(it)

    def phase_f(self, l, src, dst, hsrc, hdst):
        A, NT = self.A, self.NT
        ps = self.ps
        din = self.din
        mods = self.mods[:, l, :]
        wg = A.bf16(KC, DFF); wu = A.bf16(KC, DFF); wd = A.bf16(FC, D)
        ln2 = A.f32(16)
        self.dma(ln2, din["ln2"][l], "c0", w=["par_ln"])
        HW = DFF // 2
        stgs = [A.f32(HW), A.f32(HW)]
        jobs = []
        for hf in range(2):
            for kc in range(KC):
                jobs.append((wg[:, kc, hf * HW:(hf + 1) * HW], din["w_gate"][l, 128 * kc:128 * kc + 128, hf * HW:(hf + 1) * HW], ("wg", kc, hf)))
                jobs.append((wu[:, kc, hf * HW:(hf + 1) * HW], din["w_up"][l, 128 * kc:128 * kc + 128, hf * HW:(hf + 1) * HW], ("wu", kc, hf)))
        for f in range(FC):
            jobs.append((wd[:, f, :], din["w_down"][l, 128 * f:128 * f + 128, :], ("wd", f)))
        xt = A.f32(KC, T); hb2 = [A.bf16(KC, T), A.bf16(KC, T)]; gb = A.bf16(FC, T)
        sgt = [A.bf16(T), A.bf16(T)]
        st_a = A.f32(T); st_b = A.f32(T)
        st = (st_a, st_b, st_b)
        srcv = src.rearrange("(c p) t -> p c t", p=128)
        dstv = dst.rearrange("(c p) t -> p c t", p=128)
        hsrcv = hsrc.rearrange("(c p) t -> p c t", p=128)
        xk = [("x", c) for c in range(KC)]
        eps2 = LN_EPS / (self.alpha ** 2)
        last = (l == self.L - 1)
        nmods = None if last else self.mods[:, l + 1, :]
        self.rot = 0
        RS = FC - KC

        def head_load(i):
            b = i % 2
            self.dma(hb2[b], hsrcv[:, :, i * T:(i + 1) * T], f"hin{b}", w=[("hb", b, c) for c in range(KC)])

        def xload(i):
            self.dma(xt, srcv[:, :, i * T:(i + 1) * T], "xin", w=xk)

        def gu(i, fs):
            b = i % 2
            for f in fs:
                bg = self.rot % 2
                bu_ = 2 + self.rot % 2
                self.rot += 1
                hf = (128 * f) // HW
                for kc in range(KC):
                    self.mm(ps[bg][:, :], wg[:, kc, 128 * f:128 * f + 128], hb2[b][:, kc, :], start=(kc == 0), stop=(kc == KC - 1),
                            r=[("wg", kc, hf), ("hb", b, kc)], w=[("ps", bg)])
                for kc in range(KC):
                    self.mm(ps[bu_][:, :], wu[:, kc, 128 * f:128 * f + 128], hb2[b][:, kc, :], start=(kc == 0), stop=(kc == KC - 1),
                            r=[("wu", kc, hf), ("hb", b, kc)], w=[("ps", bu_)])
                sk = ("sgt", f % 2)
                self.act(sgt[f % 2], ps[bg][:, :], AF.Silu, r=[("ps", bg)], w=[sk])
                self.tt(gb[:, f, :], sgt[f % 2], ps[bu_][:, :], ALU.mult, r=[sk, ("ps", bu_)], w=[("gb", f)])

        def down(i):
            for mo in range(KC):
                bk = 4 + mo % 2
                for f in range(FC):
                    self.mm(ps[bk][:, :], wd[:, f, 128 * mo:128 * mo + 128], gb[:, f, :], start=(f == 0), stop=(f == FC - 1),
                            r=[("wd", f), ("gb", f)], w=[("ps", bk)])
                self.stt(xt[:, mo, :], ps[bk][:, :], mods[:, 40 + mo:41 + mo], xt[:, mo, :], ALU.mult, ALU.add,
                         r=[("ps", bk), "mods", ("x", mo)], w=[("x", mo)])

        def ln_a(i):
            b = i % 2
            for c in range(KC):
                self.act(hb2[b][:, c, :], xt[:, c, :], AF.Copy, r=[("x", c)], w=[("hb", b, c)])
                self.act(gb[:, RS + c, :], xt[:, c, :], AF.Square, r=[("x", c)], w=[("gb", RS + c)])

        def ln_b1(i):
            b = i % 2
            for c in range(KC):
                self.mm(ps[6][:, :], self.ones_ln, hb2[b][:, c, :], start=(c == 0), stop=(c == KC - 1),
                        r=["ones_ln", ("hb", b, c)], w=[("ps", 6)])
            for c in range(KC):
                self.mm(ps[7][:, :], self.ones_ln, gb[:, RS + c, :], start=(c == 0), stop=(c == KC - 1),
                        r=["ones_ln", ("gb", RS + c)], w=[("ps", 7)])
            self.stats(ps[6], ("ps", 6), ps[7], ("ps", 7), st, eps2)
            for c in range(KC):
                self.tt(xt[:, c, :], xt[:, c, :], st[0], ALU.subtract, r=[("x", c), "st_mean"], w=[("x", c)])
                self.tt(xt[:, c, :], xt[:, c, :], st[2], ALU.mult, r=[("x", c), "st_rstd"], w=[("x", c)])

        def ln_b2(i):
            b = i % 2
            for c in range(KC):
                self.act(xt[:, c, :], xt[:, c, :], AF.Identity, r=[("x", c), "par_ln"], w=[("x", c)],
                         bias=ln2[:, 8 + c:9 + c], scale=ln2[:, c:c + 1])
            if not last:
                self.emit_h(hb2[b], [("hb", b, c) for c in range(KC)], xt, 8, 0, nmods, hdst, i * T)
            self.dma(dstv[:, :, i * T:(i + 1) * T], xt, "xout", r=xk)

        head_load(0)
        xload(0)
        self.load_cast(jobs, stgs, engs=("gpsimd",))
        gu(0, range(FC))
        for it in range(NT):
            if it + 1 < NT:
                head_load(it + 1)
            down(it)
            ln_a(it)
            if it + 1 < NT:
                gu(it + 1, range(0, 2))
            ln_b1(it)
            if it + 1 < NT:
                gu(it + 1, range(2, 8))
            ln_b2(it)
            if it + 1 < NT:
                xload(it + 1)
                gu(it + 1, range(8, FC))


def _chunked(v, nch):
    return np.ascontiguousarray(v.reshape(nch, 128).T)


def prep_shared(inp, L):
    f = np.float32
    g = lambda k: np.asarray(inp[k], dtype=f)
    out = {}
    out["w_ada"] = np.ascontiguousarray(g("w_ada"))
    out["b_ada"] = np.stack([_chunked(g("b_ada")[l], 48) for l in range(L)])
    out["w_in"] = np.ascontiguousarray(g("w_in"))
    out["b_in"] = np.stack([_chunked(g("b_in")[l], ZC) for l in range(L)])
    scw = g("sc_w")
    out["sc_w"] = np.ascontiguousarray(scw.reshape(L, 3, 2, 128).transpose(0, 3, 2, 1).reshape(L, 128, 6))
    pw = g("pool_w")
    bd = np.zeros((L, 2, 128, 128), f)
    for c in range(2):
        bd[:, c, 0:64, 0:64] = pw[:, 2 * c]
        bd[:, c, 64:128, 64:128] = pw[:, 2 * c + 1]
    out["pool_bd"] = bd
    ch2 = lambda k: np.stack([_chunked(g(k)[l], 2) for l in range(L)])
    out["pool_scale"] = ch2("pool_scale")
    cw = g("cf_dw_w")
    out["cf_w"] = np.ascontiguousarray(cw.reshape(L, 31, 2, 128).transpose(0, 3, 2, 1).reshape(L, 128, 62))
    out["cf_b"] = ch2("cf_dw_b"); out["cf_g"] = ch2("cf_ln_g"); out["cf_beta"] = ch2("cf_ln_b")
    lre, lim, ldt = g("ssm_lam_re"), g("ssm_lam_im"), g("ssm_log_dt")
    def lay_i(a):
        a4 = a.reshape(L, 2, 8, 1, 64)
        a4 = np.broadcast_to(a4, (L, 2, 8, 16, 64))
        return np.ascontiguousarray(a4.transpose(0, 2, 3, 1, 4).reshape(L, 128, 128))
    ldt_b = np.broadcast_to(ldt[:, :, None], (L, 16, 64))
    out["lam_i"] = np.stack([lay_i(lre), lay_i(lim), lay_i(np.ascontiguousarray(ldt_b))], axis=1)
    bre, bim = g("ssm_b_re"), g("ssm_b_im")
    def lay_bt(b):
        b5 = b.reshape(L, 2, 8, 64, 16)
        return np.ascontiguousarray(b5.transpose(0, 2, 4, 1, 3).reshape(L, 128, 128))
    out["bt_i"] = np.stack([lay_bt(bre), lay_bt(bim)], axis=1)
    def lay_ii(a):
        t = a.transpose(0, 2, 1)
        return np.ascontiguousarray(np.concatenate([t, t], axis=1))
    rep = lambda a3: np.ascontiguousarray(np.repeat(a3, 16, axis=2))
    out["lam_ii"] = np.stack([rep(lay_ii(lre)), rep(lay_ii(lim)), rep(lay_ii(np.ascontiguousarray(ldt_b)))], axis=1)
    cre, cim = g("ssm_c_re"), g("ssm_c_im")
    cre_t, cim_t = cre.transpose(0, 3, 1, 2), cim.transpose(0, 3, 1, 2)
    out["c_a"] = np.ascontiguousarray(np.concatenate([cre_t, cim_t], axis=1).reshape(L, 128, 256))
    out["c_sw"] = np.ascontiguousarray(np.concatenate([cim_t, cre_t], axis=1).reshape(L, 128, 256))
    bre_t, bim_t = bre.transpose(0, 2, 1, 3), bim.transpose(0, 2, 1, 3)
    out["b_a"] = np.ascontiguousarray(np.concatenate([bre_t, bim_t], axis=1).reshape(L, 128, 256))
    out["b_sw"] = np.ascontiguousarray(np.concatenate([bim_t, bre_t], axis=1).reshape(L, 128, 256))
    out["ssm_d"] = ch2("ssm_d")
    out["w_glu"] = np.ascontiguousarray(g("ssm_w_glu"))
    out["b_glu"] = ch2("ssm_b_glu")
    out["w_o"] = np.ascontiguousarray(g("w_o"))
    out["ln1"] = np.stack([np.concatenate([_chunked(g("ln1_g")[l], 8), _chunked(g("ln1_b")[l], 8)], axis=1) for l in range(L)])
    out["w_gate"] = np.ascontiguousarray(g("w_gate")); out["w_up"] = np.ascontiguousarray(g("w_up"))
    out["w_down"] = np.ascontiguousarray(g("w_down"))
    out["ln2"] = np.stack([np.concatenate([_chunked(g("ln2_g")[l], 8), _chunked(g("ln2_b")[l], 8)], axis=1) for l in range(L)])
    cm = np.zeros((3, 128, 128), f)
    cm[0] = np.eye(128, dtype=f)
    for k in range(128):
        cm[1, k, (k + 64) % 128] = 1.0
    cm[2] = 1.0 / 256.0
    out["cmat"] = cm
    cvv = np.zeros((128, 40), f)
    cvv[:64, 0] = 1.0; cvv[64:, 0] = -1.0
    par = (np.arange(128) // 16) % 2
    cvv[:, 1] = (par == 0); cvv[:, 2] = (par == 1)
    wins = np.array([[2, 4], [8, 16]])
    for c in range(2):
        for half in range(2):
            w = wins[c, half]
            cvv[64 * half:64 * half + 64, 8 + 16 * c:24 + 16 * c] = 1.0 / np.minimum(np.arange(1, 17), w)
    out["cvecs"] = cvv
    return out


_CACHE = {}


def get_program(S, L):
    if (S, L) not in _CACHE:
        b = Builder(S, L)
        _CACHE[(S, L)] = b.build()
    return _CACHE[(S, L)]


def run(inputs, S, L, n_cores):
    shared = prep_shared(inputs, L)
    x = np.asarray(inputs["x"], dtype=np.float32)
    c = np.asarray(inputs["c"], dtype=np.float32)
    in_maps = []
    for b in range(n_cores):
        m = dict(shared)
        m["x_fm"] = np.ascontiguousarray(x[b].T)
        m["cvec"] = _chunked(c[b], KC)
        in_maps.append(m)
    nc = get_program(S, L)
    res = run_bass_kernel_spmd(nc, in_maps, core_ids=list(range(n_cores)))
    out = np.stack([np.ascontiguousarray(res.results[b]["y_fm"].T) for b in range(n_cores)])
    return out.astype(np.float32)


def kernel(**inputs):
    return run(inputs, 8192, 4, 8)
```

```python
import numpy as np
from contextlib import ExitStack
import concourse.bass as bass
import concourse.mybir as mybir
from concourse.bass_utils import run_bass_kernel_spmd

F32 = mybir.dt.float32
BF16 = mybir.dt.bfloat16
I32 = mybir.dt.int32
AF = mybir.ActivationFunctionType
ALU = mybir.AluOpType

D = 1024
KC = 8
INW = 1792
ZC = 14
DFF = 2816
FC = 22
T = 512
LN_EPS = 1e-5
SL = 8
NJ = T // SL
NRS = 6
PI = float(np.pi)
ENGS = ("tensor", "vector", "scalar", "gpsimd", "sync")
ARENA_WORDS = 53200


class Tracker:
    def __init__(self, nc, es):
        self.nc = nc
        self.es = es
        self.streams = {e: [] for e in ENGS}
        self.sems = {}
        self.cnt = {}
        self.last_w = {}
        self.readers = {}
        self.waited = {e: {} for e in ENGS}
        for e in ENGS:
            self._sem("E_" + e)

    def _sem(self, key):
        if key not in self.sems:
            self.sems[key] = self.es.enter_context(self.nc.semaphore("s_" + key))
            self.cnt[key] = 0
        return key

    def op(self, eng, fn, reads=(), writes=(), dma=None, sig=True):
        deps = {}

        def add(d):
            if d is not None and deps.get(d[0], 0) < d[1]:
                deps[d[0]] = d[1]

        for b in reads:
            add(self.last_w.get(b))
        for b in writes:
            add(self.last_w.get(b))
            for r in self.readers.get(b, ()):
                add(r)
        if dma is not None:
            if dma == "c0":
                self.misc_rr = (getattr(self, "misc_rr", -1) + 1) % 8
                dma = f"c{self.misc_rr}"
            skey = self._sem("D_" + dma)
            inc = 16
            if self.cnt[skey] > 0:
                add((skey, self.cnt[skey]))
        else:
            skey = "E_" + eng
            inc = 1
        own = "E_" + eng
        waits = []
        for s, v in deps.items():
            if eng == "tensor" and s == own and dma is None:
                continue
            if self.waited[eng].get(s, 0) >= v:
                continue
            self.waited[eng][s] = v
            waits.append((s, v))
        if sig:
            self.cnt[skey] += inc
            me = (skey, self.cnt[skey])
        else:
            me = (skey, self.cnt[skey] + inc)
        self.streams[eng].append((waits, fn, skey if sig else None, inc))
        for b in reads:
            self.readers.setdefault(b, []).append(me)
        for b in writes:
            self.last_w[b] = me
            self.readers[b] = []
        return me

    def barrier(self):
        for e in ENGS:
            waits = []
            for s, v in self.cnt.items():
                if v > 0 and self.waited[e].get(s, 0) < v and not (s == "E_" + e and e in ("tensor",)):
                    self.waited[e][s] = v
                    waits.append((s, v))
            if waits:
                self.streams[e].append((waits, None, None, 0))
        self.last_w = {}
        self.readers = {}

    def emit(self):
        nc = self.nc
        sems = self.sems
        streams = self.streams
        finals = [(k, v) for k, v in self.cnt.items() if v > 0]

        def runner(ename):
            def run(eng):
                for waits, fn, skey, inc in streams[ename]:
                    for s, v in waits:
                        eng.wait_ge(sems[s], v)
                    if fn is None:
                        continue
                    ins = fn(eng)
                    if skey is not None:
                        ins.then_inc(sems[skey], inc)
                if ename == "sync":
                    for k, v in finals:
                        eng.wait_ge(sems[k], v)
            return run

        with nc.Block() as block:
            block.sync(runner("sync"))
            block.tensor(runner("tensor"))
            block.vector(runner("vector"))
            block.scalar(runner("scalar"))
            block.gpsimd(runner("gpsimd"))


class Arena:
    def __init__(self, t, words):
        self.t = t
        self.words = words
        self.ptr = 0
        self.hi = 0

    @staticmethod
    def _shape(ap, shape):
        if len(shape) == 1:
            return ap
        names = "abcd"[: len(shape)]
        kw = {names[i]: int(shape[i]) for i in range(len(shape) - 1)}
        return ap.rearrange("p (" + " ".join(names) + ") -> p " + " ".join(names), **kw)

    def f32(self, *shape):
        n = int(np.prod(shape))
        ap = self.t[:, self.ptr:self.ptr + n]
        self.ptr += n
        self.hi = max(self.hi, self.ptr)
        assert self.ptr <= self.words, ("arena overflow", self.ptr)
        return self._shape(ap, shape)

    def i32(self, *shape):
        return self.f32(*shape).bitcast(I32)

    def bf16(self, *shape):
        n = int(np.prod(shape))
        nw = (n + 1) // 2
        ap = self.t[:, self.ptr:self.ptr + nw].bitcast(BF16)
        if 2 * nw != n:
            ap = ap[:, 0:n]
        self.ptr += nw
        self.hi = max(self.hi, self.ptr)
        assert self.ptr <= self.words, ("arena overflow", self.ptr)
        return self._shape(ap, shape)


class Builder:
    def __init__(self, S, L):
        assert S % T == 0
        self.S = S
        self.L = L
        self.NT = S // T
        self.alpha = float((2 * L) ** 0.25)
        self.nc = bass.Bass("TRN2", target_bir_lowering=False)
        self.es = ExitStack()
        self.tr = Tracker(self.nc, self.es)
        self.din = {}
        self.uid = 0

    def dram_in(self, name, shape):
        self.din[name] = self.nc.dram_tensor(name, list(shape), F32, kind="ExternalInput").ap()
        return self.din[name]

    def mm(self, out, lhsT, rhs, start=True, stop=True, r=(), w=(), tp=None):
        if tp is None:
            fn = lambda e: e.matmul(out, lhsT=lhsT, rhs=rhs, start=start, stop=stop)
        else:
            fn = lambda e: e.matmul(out, lhsT=lhsT, rhs=rhs, start=start, stop=stop, tile_position=tp)
        self.tr.op("tensor", fn, reads=r, writes=w, sig=stop)

    def act(self, out, in_, func, r=(), w=(), bias=0.0, scale=1.0):
        self.tr.op("scalar", lambda e: e.activation(out=out, in_=in_, func=func, bias=bias, scale=scale),
                   reads=r, writes=w)

    def tt(self, out, in0, in1, op, r=(), w=(), eng="vector"):
        self.tr.op(eng, lambda e: e.tensor_tensor(out=out, in0=in0, in1=in1, op=op), reads=r, writes=w)

    def ts(self, out, in0, s1, op0, s2=None, op1=None, r=(), w=(), eng="vector"):
        if op1 is None:
            fn = lambda e: e.tensor_scalar(out=out, in0=in0, scalar1=s1, scalar2=None, op0=op0)
        else:
            fn = lambda e: e.tensor_scalar(out=out, in0=in0, scalar1=s1, scalar2=s2, op0=op0, op1=op1)
        self.tr.op(eng, fn, reads=r, writes=w)

    def stt(self, out, in0, scalar, in1, op0, op1, r=(), w=(), eng="vector"):
        self.tr.op(eng, lambda e: e.scalar_tensor_tensor(out=out, in0=in0, scalar=scalar, in1=in1, op0=op0, op1=op1),
                   reads=r, writes=w)

    def cp(self, out, in_, r=(), w=(), eng="vector"):
        self.tr.op(eng, lambda e: e.tensor_copy(out=out, in_=in_), reads=r, writes=w)

    def memset(self, ap, val, w=(), eng="gpsimd"):
        self.tr.op(eng, lambda e: e.memset(ap, val), writes=w)

    def recip(self, out, in_, r=(), w=()):
        self.tr.op("vector", lambda e: e.reciprocal(out=out, in_=in_), reads=r, writes=w)

    def dma(self, out, in_, stream, r=(), w=(), q="sync"):
        self.tr.op(q, lambda e: e.dma_start(out=out, in_=in_), reads=r, writes=w, dma=stream)

    def key(self, base):
        self.uid += 1
        return f"{base}#{self.uid}"

    def build(self):
        nc, es, S, L = self.nc, self.es, self.S, self.L
        d = self.dram_in
        d("x_fm", [D, S]); d("cvec", [128, KC])
        d("w_ada", [L, D, 6 * D]); d("b_ada", [L, 128, 48])
        d("w_in", [L, D, INW]); d("b_in", [L, 128, ZC])
        d("sc_w", [L, 128, 6]); d("pool_bd", [L, 2, 128, 128]); d("pool_scale", [L, 128, 2])
        d("cf_w", [L, 128, 62]); d("cf_b", [L, 128, 2]); d("cf_g", [L, 128, 2]); d("cf_beta", [L, 128, 2])
        d("lam_i", [L, 3, 128, 128]); d("bt_i", [L, 2, 128, 128])
        d("lam_ii", [L, 3, 128, 256]); d("c_a", [L, 128, 256]); d("c_sw", [L, 128, 256]); d("b_a", [L, 128, 256]); d("b_sw", [L, 128, 256])
        d("ssm_d", [L, 128, 2]); d("w_glu", [L, 256, 256]); d("b_glu", [L, 128, 2])
        d("w_o", [L, D, D]); d("ln1", [L, 128, 16])
        d("w_gate", [L, D, DFF]); d("w_up", [L, D, DFF]); d("w_down", [L, DFF, D]); d("ln2", [L, 128, 16])
        d("cmat", [3, 128, 128]); d("cvecs", [128, 40])
        self.y = nc.dram_tensor("y_fm", [D, S], F32, kind="ExternalOutput").ap()
        self.xa = nc.dram_tensor("xa_scr", [D, S], F32, kind="Internal").ap()
        self.xb = nc.dram_tensor("xb_scr", [D, S], F32, kind="Internal").ap()

        arena_t = es.enter_context(nc.sbuf_tensor("arena", [128, ARENA_WORDS], F32))
        self.A = A = Arena(arena_t, ARENA_WORDS)
        self.ps = [es.enter_context(nc.psum_tensor(f"psb{i}", [128, 512], F32)) for i in range(8)]

        self.ident = A.f32(128); self.jmat = A.f32(128); self.ones_cf = A.f32(128)
        self.ones_ln = A.bf16(128); self.ones_cfb = A.bf16(128)
        self.cv = A.f32(40)
        self.cond = A.f32(KC)
        self.mods = A.f32(L, 48)
        self.dma(self.ident, self.din["cmat"][0], "c0", w=["ident"])
        self.dma(self.jmat, self.din["cmat"][1], "c0", w=["jmat"])
        self.dma(self.ones_cf, self.din["cmat"][2], "c0", w=["ones_cf"])
        self.dma(self.cv, self.din["cvecs"], "c0", w=["cv"])
        self.dma(self.cond, self.din["cvec"], "c0", w=["cond"])
        self.ts(self.ones_ln, self.ones_cf, 256.0 / D, ALU.mult, r=["ones_cf"], w=["ones_ln"])
        self.cp(self.ones_cfb, self.ones_cf, r=["ones_cf"], w=["ones_cfb"])
        self.act(self.cond, self.cond, AF.Silu, r=["cond"], w=["cond"])
        mark = A.ptr
        self.mods_alloc()
        for j6 in range(6):
            self.mods_chunk(0, j6)
        self.mods_finish(0)
        self.tr.barrier()
        self.ha = nc.dram_tensor("ha_scr", [D, S], BF16, kind="Internal").ap()
        self.hbs = nc.dram_tensor("hb_scr", [D, S], BF16, kind="Internal").ap()
        pre = self.prepass_thunks()
        later = [(l, j6) for l in range(1, L) for j6 in range(6)]
        for i in range(max(len(pre), len(later))):
            if i < len(later):
                l_, j6_ = later[i]
                self.mods_chunk(l_, j6_)
                if j6_ == 5:
                    self.mods_finish(l_)
            if i < len(pre):
                pre[i]()
        self.tr.barrier()
        for l in range(L):
            A.ptr = mark
            src = self.din["x_fm"] if l == 0 else self.xb
            self.phase_m(l, src, self.xa, self.ha, self.hbs)
            self.tr.barrier()
            A.ptr = mark
            dst = self.y if l == L - 1 else self.xb
            self.phase_f(l, self.xa, dst, self.hbs, self.ha)
            self.tr.barrier()
        self.tr.emit()
        es.close()
        return nc

    def emit_h(self, hn, hkeys, xt, sc_col, sh_col, mods, hdst, t0):
        for c in range(KC):
            self.act(hn[:, c, :], xt[:, c, :], AF.Identity, r=[("x", c), "mods"], w=[hkeys[c]],
                     bias=mods[:, sh_col + c:sh_col + c + 1], scale=mods[:, sc_col + c:sc_col + c + 1])
        self.dma(hdst.rearrange("(c p) t -> p c t", p=128)[:, :, t0:t0 + T], hn, "hout", r=list(hkeys))

    def prepass_thunks(self):
        A = self.A
        xt2 = [A.f32(KC, T), A.f32(KC, T)]
        hn2 = [A.bf16(KC, T), A.bf16(KC, T)]
        mods = self.mods[:, 0, :]
        srcv = self.din["x_fm"].rearrange("(c p) t -> p c t", p=128)
        dstv = self.ha.rearrange("(c p) t -> p c t", p=128)

        def tile(it):
            b = it % 2
            t0 = it * T
            self.dma(xt2[b], srcv[:, :, t0:t0 + T], f"pin{b}", w=[("px", b)])
            for c in range(KC):
                self.act(hn2[b][:, c, :], xt2[b][:, c, :], AF.Identity, r=[("px", b), "mods0"], w=[("ph", b)],
                         bias=mods[:, c:c + 1], scale=mods[:, 8 + c:9 + c])
            self.dma(dstv[:, :, t0:t0 + T], hn2[b], f"pout{b}", r=[("ph", b)])
        return [lambda it=it: tile(it) for it in range(self.NT)]

    def mods_alloc(self):
        A, L = self.A, self.L
        self.m_stg = [A.f32(KC, 1024), A.f32(KC, 1024)]
        self.m_bada = A.f32(L, 48)
        self.m_n = 0
        self.dma(self.m_bada, self.din["b_ada"].rearrange("l p j -> p l j"), "c0", w=["bada"])

    def mods_chunk(self, l, j6):
        pm = self.ps[l % 2]
        n = self.m_n
        self.m_n += 1
        sb = self.m_stg[n % 2]
        sk = f"stg{n % 2}"
        self.dma(sb, self.din["w_ada"][l, :, 1024 * j6:1024 * (j6 + 1)].rearrange("(kc p) n -> p kc n", p=128), sk, w=[sk])
        for jj in range(8):
            col = 8 * j6 + jj
            for kc in range(KC):
                self.mm(pm[:, col:col + 1], sb[:, kc, 128 * jj:128 * (jj + 1)], self.cond[:, kc:kc + 1],
                        start=(kc == 0), stop=(kc == KC - 1), r=[sk, "cond"], w=[("pm", l % 2)])

    def mods_finish(self, l):
        pm = self.ps[l % 2]
        mk = f"mods{l}"
        self.tt(self.mods[:, l, :], pm[:, 0:48], self.m_bada[:, l, :], ALU.add, r=[("pm", l % 2), "bada"], w=[mk])
        al = 1.0 / self.alpha
        m = self.mods[:, l, :]
        self.ts(m[:, 8:16], m[:, 8:16], 1.0, ALU.add, r=[mk], w=[mk])
        self.ts(m[:, 32:40], m[:, 32:40], 1.0, ALU.add, r=[mk], w=[mk])
        self.ts(m[:, 16:24], m[:, 16:24], 1.0, ALU.add, al, ALU.mult, r=[mk], w=[mk])
        self.ts(m[:, 40:48], m[:, 40:48], 1.0, ALU.add, al, ALU.mult, r=[mk], w=[mk])

    def load_cast(self, jobs, stgs, engs=("gpsimd", "vector", "scalar")):
        for i, (dst, src, key) in enumerate(jobs):
            n = dst.shape[-1]
            k = i % len(stgs)
            sk = f"wstg{k}"
            self.dma(stgs[k][:, 0:n], src, sk, w=[sk], q="sync")
            e = engs[i % len(engs)]
            if e == "scalar":
                self.act(dst, stgs[k][:, 0:n], AF.Copy, r=[sk], w=[key])
            else:
                self.cp(dst, stgs[k][:, 0:n], r=[sk], w=[key], eng=e)

    def stats(self, pmean, kmean, pmsq, kmsq, st, eps):
        mean_sb, mm_, rstd = st
        self.act(mean_sb, pmean[:, :], AF.Copy, r=[kmean], w=["st_mean"])
        self.tt(mm_, mean_sb, mean_sb, ALU.mult, r=["st_mean"], w=["st_mm"])
        self.tt(mm_, pmsq[:, :], mm_, ALU.subtract, r=[kmsq, "st_mm"], w=["st_mm"])
        self.ts(mm_, mm_, 0.0, ALU.max, float(eps), ALU.add, r=["st_mm"], w=["st_mm"])
        self.act(mm_, mm_, AF.Sqrt, r=["st_mm"], w=["st_mm"])
        self.recip(rstd, mm_, r=["st_mm"], w=["st_rstd"])

    def sin_red(self, out, th, shift, tmp, tmpi, r, w):
        kt = self.key("sr")
        self.ts(out, th, float(shift), ALU.add, r=r, w=w)
        self.ts(tmp, out, float(1.0 / (2 * PI)), ALU.mult, r=w, w=[kt])
        self.cp(tmpi, tmp, r=[kt], w=[kt + "i"])
        self.cp(tmp, tmpi, r=[kt + "i"], w=[kt])
        self.stt(out, tmp, -2 * PI, out, ALU.mult, ALU.add, r=[kt] + list(w), w=w)
        self.ts(tmp, out, PI, ALU.is_gt, r=w, w=[kt])
        self.stt(out, tmp, -2 * PI, out, ALU.mult, ALU.add, r=[kt] + list(w), w=w)
        self.ts(tmp, out, -PI, ALU.is_lt, r=w, w=[kt])
        self.stt(out, tmp, 2 * PI, out, ALU.mult, ALU.add, r=[kt] + list(w), w=w)
        self.act(out, out, AF.Sin, r=w, w=w)

    def lam_bar(self, lam, F, pre):
        A = self.A
        k = lambda n: f"{pre}_{n}"
        dt = A.f32(F); rho = A.f32(F); th = A.f32(F); sn = A.f32(F); cs = A.f32(F)
        tmp = A.f32(F); tmpi = A.i32(F)
        lbre = A.f32(F); lbim = A.f32(F)
        self.act(dt, lam[:, 2, :], AF.Exp, r=[k("lam")], w=[k("dt")])
        self.tt(rho, lam[:, 0, :], dt, ALU.mult, r=[k("lam"), k("dt")], w=[k("rho")])
        self.tt(th, lam[:, 1, :], dt, ALU.mult, r=[k("lam"), k("dt")], w=[k("th")])
        self.act(rho, rho, AF.Exp, r=[k("rho")], w=[k("rho")])
        self.sin_red(sn, th, 0.0, tmp, tmpi, r=[k("th")], w=[k("sn")])
        self.sin_red(cs, th, PI / 2, tmp, tmpi, r=[k("th")], w=[k("cs")])
        self.tt(lbre, rho, cs, ALU.mult, r=[k("rho"), k("cs")], w=[k("lbre")])
        self.tt(lbim, rho, sn, ALU.mult, r=[k("rho"), k("sn")], w=[k("lbim")])
        lr, li = lam[:, 0, :], lam[:, 1, :]
        nr = A.f32(F); den = A.f32(F); qre = A.f32(F); qim = A.f32(F); ta = A.f32(F); tb = A.f32(F)
        self.ts(nr, lbre, -1.0, ALU.add, r=[k("lbre")], w=[k("nr")])
        self.tt(den, lr, lr, ALU.mult, r=[k("lam")], w=[k("den")])
        self.tt(ta, li, li, ALU.mult, r=[k("lam")], w=[k("ta")])
        self.tt(den, den, ta, ALU.add, r=[k("den"), k("ta")], w=[k("den")])
        self.recip(den, den, r=[k("den")], w=[k("den")])
        self.tt(qre, nr, lr, ALU.mult, r=[k("nr"), k("lam")], w=[k("qre")])
        self.tt(ta, lbim, li, ALU.mult, r=[k("lbim"), k("lam")], w=[k("ta")])
        self.tt(qre, qre, ta, ALU.add, r=[k("qre"), k("ta")], w=[k("qre")])
        self.tt(qre, qre, den, ALU.mult, r=[k("qre"), k("den")], w=[k("qre")])
        self.tt(qim, lbim, lr, ALU.mult, r=[k("lbim"), k("lam")], w=[k("qim")])
        self.tt(tb, nr, li, ALU.mult, r=[k("nr"), k("lam")], w=[k("tb")])
        self.tt(qim, qim, tb, ALU.subtract, r=[k("qim"), k("tb")], w=[k("qim")])
        self.tt(qim, qim, den, ALU.mult, r=[k("qim"), k("den")], w=[k("qim")])
        return (lbre, k("lbre")), (lbim, k("lbim")), (qre, k("qre")), (qim, k("qim"))

    def cmul(self, ore, oim, a_re, a_im, b_re, b_im, t1, t2):
        self.tt(t1[0], a_re[0], b_re[0], ALU.mult, r=[a_re[1], b_re[1]], w=[t1[1]])
        self.tt(t2[0], a_im[0], b_im[0], ALU.mult, r=[a_im[1], b_im[1]], w=[t2[1]])
        self.tt(ore[0], t1[0], t2[0], ALU.subtract, r=[t1[1], t2[1]], w=[ore[1]])
        self.tt(t1[0], a_re[0], b_im[0], ALU.mult, r=[a_re[1], b_im[1]], w=[t1[1]])
        self.tt(t2[0], a_im[0], b_re[0], ALU.mult, r=[a_im[1], b_re[1]], w=[t2[1]])
        self.tt(oim[0], t1[0], t2[0], ALU.add, r=[t1[1], t2[1]], w=[oim[1]])

    def ssm_prep(self, l, BinT, CoutT, AT, ToepT, ssd):
        A = self.A
        cv = self.cv
        ps = self.ps
        lam = A.f32(3, 128); bt = A.f32(2, 128)
        self.dma(lam, self.din["lam_i"][l].rearrange("k p f -> p k f"), "c0", w=["i_lam"])
        self.dma(bt, self.din["bt_i"][l].rearrange("k p f -> p k f"), "c0", w=["bt"])
        lbre, lbim, qre, qim = self.lam_bar(lam, 128, "i")
        t1 = (A.f32(128), "i_t1"); t2 = (A.f32(128), "i_t2")
        bbre = (A.f32(128), "bbre"); bbim = (A.f32(128), "bbim")
        self.cmul(bbre, bbim, qre, qim, (bt[:, 0, :], "bt"), (bt[:, 1, :], "bt"), t1, t2)
        bbp = []
        for par in range(2):
            pr = (A.f32(128), f"bbpr{par}"); pi_ = (A.f32(128), f"bbpi{par}")
            self.ts(pr[0], bbre[0], cv[:, 1 + par:2 + par], ALU.mult, r=["bbre", "cv"], w=[pr[1]])
            self.ts(pi_[0], bbim[0], cv[:, 1 + par:2 + par], ALU.mult, r=["bbim", "cv"], w=[pi_[1]])
            bbp.append((pr, pi_))
        v3 = lambda ap: ap.rearrange("p (c f) -> p c f", c=2)
        pw_re, pw_im = lbre, lbim
        for m in range(SL):
            sp = SL - 1 - m
            for par in range(2):
                o_re = (BinT[:, :, par, sp, 0:64], "BinT"); o_im = (BinT[:, :, par, sp, 64:128], "BinT")
                if m == 0:
                    self.cp(o_re[0], v3(bbp[par][0][0]), r=[bbp[par][0][1]], w=["BinT"])
                    self.cp(o_im[0], v3(bbp[par][1][0]), r=[bbp[par][1][1]], w=["BinT"])
                else:
                    a_re, a_im = pw_re, pw_im
                    b_re, b_im = bbp[par]
                    self.tt(t1[0], a_re[0], b_re[0], ALU.mult, r=[a_re[1], b_re[1]], w=[t1[1]])
                    self.tt(t2[0], a_im[0], b_im[0], ALU.mult, r=[a_im[1], b_im[1]], w=[t2[1]])
                    self.tt(o_re[0], v3(t1[0]), v3(t2[0]), ALU.subtract, r=[t1[1], t2[1]], w=["BinT"])
                    self.tt(t1[0], a_re[0], b_im[0], ALU.mult, r=[a_re[1], b_im[1]], w=[t1[1]])
                    self.tt(t2[0], a_im[0], b_re[0], ALU.mult, r=[a_im[1], b_re[1]], w=[t2[1]])
                    self.tt(o_im[0], v3(t1[0]), v3(t2[0]), ALU.add, r=[t1[1], t2[1]], w=["BinT"])
            if 1 <= m < SL - 1:
                n_re = (A.f32(128), f"ipw_re{m + 1}"); n_im = (A.f32(128), f"ipw_im{m + 1}")
                self.cmul(n_re, n_im, pw_re, pw_im, lbre, lbim, t1, t2)
                pw_re, pw_im = n_re, n_im
        FW = 256
        lam2 = A.f32(3, FW); ca = A.f32(FW); csw = A.f32(FW); ba = A.f32(FW); bsw = A.f32(FW)
        self.dma(lam2, self.din["lam_ii"][l].rearrange("k p f -> p k f"), "c0", w=["ii_lam"])
        self.dma(ca, self.din["c_a"][l], "c0", w=["ca"])
        self.dma(csw, self.din["c_sw"][l], "c0", w=["csw"])
        self.dma(ba, self.din["b_a"][l], "c0", w=["ba"])
        self.dma(bsw, self.din["b_sw"][l], "c0", w=["bsw"])
        l2re, l2im, q2re, q2im = self.lam_bar(lam2, FW, "ii")
        u1 = (A.f32(FW), "ii_t1"); u2 = (A.f32(FW), "ii_t2")
        bb2 = A.f32(FW)
        self.ts(u1[0], q2im[0], cv[:, 0:1], ALU.mult, -1.0, ALU.mult, r=[q2im[1], "cv"], w=[u1[1]])
        self.tt(u2[0], bsw, u1[0], ALU.mult, r=["bsw", u1[1]], w=[u2[1]])
        self.tt(bb2, ba, q2re[0], ALU.mult, r=["ba", q2re[1]], w=["bb2"])
        self.tt(bb2, bb2, u2[0], ALU.add, r=["bb2", u2[1]], w=["bb2"])
        bbz = A.f32(16, 128)
        self.memset(bbz, 0.0, w=["bbz"])
        bb2v = bb2.rearrange("p (g h) -> p g h", g=16)
        for g in range(16):
            gl = g % 8
            self.cp(bbz[:, g, 16 * gl:16 * gl + 16], bb2v[:, g, :], r=["bb2"], w=["bbz"], eng=("gpsimd" if g % 2 else "vector"))
        x1 = A.f32(FW); x2 = A.f32(FW)
        self.ts(x1, ca, cv[:, 0:1], ALU.mult, r=["ca", "cv"], w=["x1"])
        self.ts(x2, csw, -1.0, ALU.mult, r=["csw"], w=["x2"])
        cl = A.f32(SL + 1, FW)
        self.cp(cl[:, 0, :], x1, r=["x1"], w=[("cl", 0)])
        q_re, q_im = l2re, l2im
        for m in range(1, SL + 1):
            if m > 1:
                n_re = (A.f32(FW), f"q_re{m}"); n_im = (A.f32(FW), f"q_im{m}")
                self.cmul(n_re, n_im, q_re, q_im, l2re, l2im, u1, u2)
                q_re, q_im = n_re, n_im
            self.tt(u1[0], x1, q_re[0], ALU.mult, r=["x1", q_re[1]], w=[u1[1]])
            self.tt(u2[0], x2, q_im[0], ALU.mult, r=["x2", q_im[1]], w=[u2[1]])
            self.tt(cl[:, m, :], u1[0], u2[0], ALU.add, r=[u1[1], u2[1]], w=[("cl", m)])
        are = A.f32(NRS, 16); aim = A.f32(NRS, 16); aims = A.f32(NRS, 16)
        self.cp(are[:, 0, :], q_re[0].rearrange("p (g h) -> p g h", h=16)[:, :, 0], r=[q_re[1]], w=[("are", 0)])
        self.cp(aim[:, 0, :], q_im[0].rearrange("p (g h) -> p g h", h=16)[:, :, 0], r=[q_im[1]], w=[("aim", 0)])
        for k in range(1, NRS):
            pr, pi_ = are[:, k - 1, :], aim[:, k - 1, :]
            self.tt(u1[0][:, 0:16], pr, pr, ALU.mult, r=[("are", k - 1)], w=[u1[1]])
            self.tt(u2[0][:, 0:16], pi_, pi_, ALU.mult, r=[("aim", k - 1)], w=[u2[1]])
            self.tt(are[:, k, :], u1[0][:, 0:16], u2[0][:, 0:16], ALU.subtract, r=[u1[1], u2[1]], w=[("are", k)])
            self.stt(aim[:, k, :], pr, 2.0, pi_, ALU.mult, ALU.mult, r=[("are", k - 1), ("aim", k - 1)], w=[("aim", k)])
        for k in range(NRS):
            self.ts(aims[:, k, :], aim[:, k, :], cv[:, 0:1], ALU.mult, r=[("aim", k), "cv"], w=[("aims", k)])
        for k in range(NRS):
            for g in range(16):
                kk = ("AT", k, g)
                self.act(AT[:, k, g, :], self.ident, AF.Identity, r=["ident", ("are", k)], w=[kk],
                         scale=are[:, k, g:g + 1])
                self.stt(AT[:, k, g, :], self.jmat, aims[:, k, g:g + 1], AT[:, k, g, :], ALU.mult, ALU.add,
                         r=["jmat", ("aims", k), kk], w=[kk])
        cov = CoutT.rearrange("p (q t) s m -> p q t s m", t=2)
        for s_ in range(SL):
            clv = cl[:, s_ + 1, :].rearrange("p (q t h) -> p q t h", t=2, h=16)
            for par in range(2):
                self.cp(cov[:, :, par, s_, 16 * par:16 * par + 16], clv[:, :, par, :], r=[("cl", s_ + 1)], w=["CoutT"],
                        eng=("gpsimd" if par else "vector"))
        clg = cl.rearrange("p m (g h) -> p m g h", g=16)
        for bank in range(2 * SL // 4):
            for i4 in range(4):
                b_ = bank * 4 + i4
                c, tau = b_ // SL, b_ % SL
                col = i4 * 128
                for gl in range(8):
                    g = 8 * c + gl
                    self.mm(ps[bank][:, col + 16 * gl:col + 16 * gl + 16], bbz[:, g, :], clg[:, tau, g, :],
                            r=["bbz", ("cl", tau)], w=[("ps", bank)])
            for i4 in range(4):
                b_ = bank * 4 + i4
                c, tau = b_ // SL, b_ % SL
                col = i4 * 128
                if tau == 0:
                    self.stt(ToepT[:, c, 0, :], self.ident, ssd[:, c:c + 1], ps[bank][:, col:col + 128], ALU.mult, ALU.add,
                             r=["ident", "par_ssm_d", ("ps", bank)], w=["ToepT"])
                else:
                    self.act(ToepT[:, c, tau, :], ps[bank][:, col:col + 128], AF.Copy, r=[("ps", bank)], w=["ToepT"])

    def phase_m(self, l, src, dst, hsrc, hdst):
        A, NT = self.A, self.NT
        ps = self.ps
        din = self.din
        mods = self.mods[:, l, :]
        w_in = A.bf16(KC, INW); w_o = A.bf16(KC, D); glu = A.bf16(2, 256); poolbd = A.bf16(2, 128)
        BinT = A.bf16(2, 2, SL, 128); CoutT = A.bf16(16, SL, 32); AT = A.bf16(NRS, 16, 128); ToepT = A.bf16(2, SL, 128)
        cdiag = A.bf16(2, 31, 128)
        b_in = A.f32(ZC); scw = A.f32(2, 3); pscale = A.f32(2); cfw = A.f32(2, 31)
        cfb = A.f32(2); cfg = A.f32(2); cfbeta = A.f32(2); ssd = A.f32(2); bglu = A.f32(2); ln1 = A.f32(16)
        for (ap, nm, re) in ((b_in, "b_in", None), (scw, "sc_w", "p (c k) -> p c k"), (pscale, "pool_scale", None),
                             (cfw, "cf_w", "p (c k) -> p c k"), (cfb, "cf_b", None), (cfg, "cf_g", None),
                             (cfbeta, "cf_beta", None), (ssd, "ssm_d", None), (bglu, "b_glu", None), (ln1, "ln1", None)):
            s_ = din[nm][l]
            if re is not None:
                s_ = s_.rearrange(re, c=2)
            self.dma(ap, s_, "c0", w=["par_" + nm])
        self.memset(CoutT, 0.0, w=["CoutT"])
        for j in range(2):
            for k in range(31):
                self.act(cdiag[:, j, k, :], self.ident, AF.Identity, r=["ident", "par_cf_w"], w=["cdiag"], scale=cfw[:, j, k:k + 1])
        m0 = A.ptr
        stgs = [A.f32(INW), A.f32(INW)]
        self.ssm_prep(l, BinT, CoutT, AT, ToepT, ssd)
        jobs = []
        for kc in range(KC):
            jobs.append((w_in[:, kc, :], din["w_in"][l, 128 * kc:128 * kc + 128, :], "w_in"))
        for kc in range(KC):
            jobs.append((w_o[:, kc, :], din["w_o"][l, 128 * kc:128 * kc + 128, :], "w_o"))
        for kc in range(2):
            jobs.append((glu[:, kc, :], din["w_glu"][l, 128 * kc:128 * kc + 128, :], "glu"))
            jobs.append((poolbd[:, kc, :], din["pool_bd"][l, kc], "poolbd"))
        self.load_cast(jobs, stgs, engs=("gpsimd",))
        self.tr.barrier()
        A.ptr = m0
        xt = A.f32(KC, T); hb2 = [A.bf16(KC, T), A.bf16(KC, T)]; ycat = A.bf16(KC, T)
        zh = A.f32(2, T); zb = A.f32(2, T); zc_ = A.f32(2, T); zv = A.f32(2, T + 16); sg = A.f32(2, T + 16)
        zp = A.f32(2, T + 15); pbuf = A.f32(2, T + 2); hbuf = A.bf16(2, T + 30)
        u = A.bf16(2, T); mb = A.bf16(2, T); yg = A.bf16(2, T); sig = A.bf16(T)
        scr0 = A.f32(2, T + 16); accb = A.bf16(2, T); sqb = A.bf16(2, T); scr2 = A.f32(2, T)
        Sx = A.bf16(16, NJ + 2)
        st = (A.f32(T), A.f32(T), A.f32(T))
        self.memset(zp, 0.0, w=[("zp", 0), ("zp", 1), ("z", 6), ("z", 7)])
        self.memset(pbuf, 0.0, w=[("pbuf", 0), ("pbuf", 1)])
        self.memset(hbuf, 0.0, w=[("hbuf", 0), ("hbuf", 1)])
        self.memset(Sx, 0.0, w=[("Sx", q_) for q_ in range(4)])
        srcv = src.rearrange("(c p) t -> p c t", p=128)
        dstv = dst.rearrange("(c p) t -> p c t", p=128)
        hsrcv = hsrc.rearrange("(c p) t -> p c t", p=128)
        xk = [("x", c) for c in range(KC)]
        eps1 = LN_EPS / (self.alpha ** 2)
        zdest = {0: (zh, 0, 0), 1: (zh, 1, 0), 2: (zb, 0, 0), 3: (zb, 1, 0), 4: (zc_, 0, 0), 5: (zc_, 1, 0),
                 6: (zp, 0, 15), 7: (zp, 1, 15), 8: (zv, 0, 0), 9: (zv, 1, 0), 10: (sg, 0, 0), 11: (sg, 1, 0)}
        zorder = [8, 10, 9, 11, 12, 13, 4, 0, 2, 5, 1, 3, 6, 7]
        uv = [u[:, c, :].rearrange("p (s j) -> p s j", s=SL) for c in range(2)]
        u_tok = [u[:, c, :].rearrange("p (s j) -> p j s", s=SL) for c in range(2)]
        ygv = [yg[:, c, :].rearrange("p (j s) -> p s j", s=SL) for c in range(2)]
        VB = [3, 4, 6, 7]
        psVq = [ps[VB[q]][:, 0:4 * NJ].rearrange("p (c t j) -> p c t j", c=2, t=2) for q in range(4)]
        Sxv = Sx.rearrange("p (c q t) j -> p c q t j", c=2, q=4)
        psY2 = [ps[5][:, :].rearrange("p (s j) -> p s j", s=SL), ps[2][:, :].rearrange("p (s j) -> p s j", s=SL)]
        YB = [5, 2]
        self.zrot = 0

        def head_load(i):
            b = i % 2
            self.dma(hb2[b], hsrcv[:, :, i * T:(i + 1) * T], f"hin{b}", w=[("hb", b, c) for c in range(KC)])

        def xload(i):
            self.dma(xt, srcv[:, :, i * T:(i + 1) * T], "xin", w=xk)

        def inproj(i, zis=None):
            b = i % 2
            for zi in (zorder if zis is None else zis):
                bk = self.zrot % 2
                self.zrot += 1
                for kc in range(KC):
                    self.mm(ps[bk][:, :], w_in[:, kc, 128 * zi:128 * zi + 128], hb2[b][:, kc, :],
                            start=(kc == 0), stop=(kc == KC - 1), r=["w_in", ("hb", b, kc)], w=[("ps", bk)])
                if zi >= 12:
                    c = zi - 12
                    self.act(u_tok[c], ps[bk][:, :].rearrange("p (j s) -> p j s", s=SL), AF.Identity,
                             r=[("ps", bk), "par_b_in"], w=[("z", zi)], bias=b_in[:, zi:zi + 1])
                    continue
                tl, j, off = zdest[zi]
                fn = AF.Sigmoid if zi in (10, 11) else AF.Identity
                self.act(tl[:, j, off:off + T], ps[bk][:, :], fn, r=[("ps", bk), "par_b_in"], w=[("z", zi)],
                         bias=b_in[:, zi:zi + 1])

        def conv_h():
            for j in range(2):
                self.tt(hbuf[:, j, 30:T + 30], zv[:, j, 0:T], sg[:, j, 0:T], ALU.mult, r=[("z", 8 + j), ("z", 10 + j)], w=[("hbuf", j)])

        def conv_thunks():
            th = []
            for j in range(2):
                hk, ak = ("hbuf", j), ("scr0", j)
                for k in range(31):
                    th.append(lambda j=j, k=k, hk=hk: self.mm(ps[2][:, :], cdiag[:, j, k, :], hbuf[:, j, k:k + T], start=(k == 0), stop=(k == 30),
                                                               r=["cdiag", hk], w=[("ps", 2)]))

                def fin(j=j, hk=hk, ak=ak):
                    self.act(scr0[:, j, 0:T], ps[2][:, :], AF.Identity, r=[("ps", 2), "par_cf_b"], w=[ak], bias=cfb[:, j:j + 1])
                    self.cp(hbuf[:, j, 0:30], hbuf[:, j, T:T + 30], r=[hk], w=[hk], eng="gpsimd")
                    self.act(sqb[:, j, :], scr0[:, j, 0:T], AF.Square, r=[ak], w=[("sqb", j)])
                    self.act(accb[:, j, :], scr0[:, j, 0:T], AF.Copy, r=[ak], w=[("accb", j)])
                th.append(fin)
            return th

        def pool(it):
            for j in range(2):
                zk, ka, kb = ("z", 6 + j), ("z", 8 + j), ("z", 10 + j)
                sa, sb_, zz = zv[:, j, :], sg[:, j, :], zp[:, j, :]
                self.tt(sa[:, 0:T + 14], zz[:, 1:T + 15], zz[:, 0:T + 14], ALU.add, r=[zk, ("zp", j)], w=[ka])
                if j == 0:
                    self.tt(sb_[64:128, 0:T + 12], sa[64:128, 2:T + 14], sa[64:128, 0:T + 12], ALU.add, r=[ka], w=[kb])
                    lo, lo_off, hi, hi_off = sa, 14, sb_, 12
                    wl, wh = 2.0, 4.0
                else:
                    self.tt(sb_[:, 0:T + 12], sa[:, 2:T + 14], sa[:, 0:T + 12], ALU.add, r=[ka], w=[kb])
                    self.tt(sa[:, 0:T + 8], sb_[:, 4:T + 12], sb_[:, 0:T + 8], ALU.add, r=[kb, ka], w=[ka])
                    self.tt(sb_[64:128, 0:T], sa[64:128, 8:T + 8], sa[64:128, 0:T], ALU.add, r=[ka, kb], w=[kb])
                    lo, lo_off, hi, hi_off = sa, 8, sb_, 0
                    wl, wh = 8.0, 16.0
                self.stt(mb[0:64, j, :], lo[0:64, lo_off:lo_off + T], 1.0 / wl, zz[0:64, 15:T + 15], ALU.mult, ALU.subtract,
                         r=[ka, kb, zk], w=[("mb", j)])
                self.stt(mb[64:128, j, :], hi[64:128, hi_off:hi_off + T], 1.0 / wh, zz[64:128, 15:T + 15], ALU.mult, ALU.subtract,
                         r=[ka, kb, zk], w=[("mb", j)])
                if it == 0:
                    rc = self.cv[:, 8 + 16 * j:24 + 16 * j]
                    tmp = scr2[:, j, 0:16]
                    for (pl, ph, sbuf_, off) in ((0, 64, lo, lo_off), (64, 128, hi, hi_off)):
                        self.tt(tmp[pl:ph, :], sbuf_[pl:ph, off:off + 16], rc[pl:ph, :], ALU.mult, r=[ka, kb, "cv"], w=[("scr2", j)])
                        self.tt(mb[pl:ph, j, 0:16], tmp[pl:ph, :], zz[pl:ph, 15:31], ALU.subtract,
                                r=[("scr2", j), zk], w=[("mb", j)])
                self.cp(zz[:, 0:15], zz[:, T:T + 15], r=[zk, ka, kb, ("mb", j)], w=[("zp", j)], eng="gpsimd")

        def pool_mm():
            for j in range(2):
                bk = self.zrot % 2
                self.zrot += 1
                self.mm(ps[bk][:, :], poolbd[:, j, :], mb[:, j, :], r=["poolbd", ("mb", j)], w=[("ps", bk)])
                self.act(ycat[:, 2 + j, :], ps[bk][:, :], AF.Identity, r=[("ps", bk), "par_pool_scale"], w=[("ycat", 2 + j)],
                         scale=pscale[:, j:j + 1])

        def sconv():
            for j in range(2):
                pk_, ak = ("pbuf", j), ("scr2", j)
                acc = scr2[:, j, :]
                self.tt(pbuf[:, j, 2:T + 2], zc_[:, j, :], zh[:, j, :], ALU.mult, r=[("z", 4 + j), ("z", j)], w=[pk_])
                self.ts(acc, pbuf[:, j, 2:T + 2], scw[:, j, 2:3], ALU.mult, r=[pk_, "par_sc_w"], w=[ak])
                self.stt(acc, pbuf[:, j, 1:T + 1], scw[:, j, 1:2], acc, ALU.mult, ALU.add, r=[pk_, "par_sc_w", ak], w=[ak])
                self.stt(acc, pbuf[:, j, 0:T], scw[:, j, 0:1], acc, ALU.mult, ALU.add, r=[pk_, "par_sc_w", ak], w=[ak])
                self.tt(ycat[:, j, :], acc, zb[:, j, :], ALU.mult, r=[ak, ("z", 2 + j)], w=[("ycat", j)])
                self.cp(pbuf[:, j, 0:2], pbuf[:, j, T:T + 2], r=[pk_], w=[pk_], eng="gpsimd")

        def body(it, conv, prev):
            ssm = []

            def st_v():
                for q in range(4):
                    for c in range(2):
                        for par in range(2):
                            g = 8 * c + 2 * q + par
                            for sp in range(SL):
                                self.mm(psVq[q][:, c, par, :], BinT[32 * q:32 * q + 32, c, par, sp, :], uv[c][32 * q:32 * q + 32, sp, :],
                                        start=(sp == 0), stop=(sp == SL - 1 and it == 0),
                                        r=["BinT", ("z", 12 + c)], w=[("ps", VB[q])], tp=(32 * q, 0))
                            if it > 0:
                                self.mm(psVq[q][:, c, par, 0:1], AT[:, 0, g, :], Sx[:, g, 0:1], start=False, stop=True,
                                        r=[("AT", 0, g), ("Sx", q)], w=[("ps", VB[q])])
                    self.cp(Sxv[:, :, q, :, 1:NJ + 1], psVq[q], r=[("ps", VB[q])], w=[("Sx", q)])
            ssm.append(st_v)

            def st_round(k):
                sh = 1 << k
                for q in range(4):
                    for c in range(2):
                        for par in range(2):
                            g = 8 * c + 2 * q + par
                            self.mm(psVq[q][:, c, par, sh:NJ], AT[:, k, g, :], Sx[:, g, 1:NJ + 1 - sh],
                                    r=[("AT", k, g), ("Sx", q)], w=[("ps", VB[q])])
                    self.tt(Sxv[:, :, q, :, 1 + sh:NJ + 1], psVq[q][:, :, :, sh:NJ], Sxv[:, :, q, :, 1 + sh:NJ + 1],
                            ALU.add, r=[("ps", VB[q]), ("Sx", q)], w=[("Sx", q)])
            for k in range(NRS):
                ssm.append(lambda k=k: st_round(k))

            def st_out(c):
                psY, yk = psY2[c], ("ps", YB[c])
                for s_ in range(SL):
                    for sp in range(s_ + 1):
                        self.mm(psY[:, s_, :], ToepT[:, c, s_ - sp, :], uv[c][:, sp, :], start=(sp == 0), stop=False,
                                r=["ToepT", ("z", 12 + c)], w=[yk])
                    for q in range(4):
                        for par in range(2):
                            g = 8 * c + 2 * q + par
                            self.mm(psY[32 * q:32 * q + 32, s_, :], CoutT[:, g, s_, :], Sx[:, g, 0:NJ], start=False,
                                    stop=(par == 1), r=["CoutT", ("Sx", q)], w=[yk], tp=(0, 32 * q))
                self.act(ygv[c], psY, AF.Gelu_apprx_tanh, r=[yk], w=[("yg", c)])
                if c == 1:
                    for q in range(4):
                        self.cp(Sxv[:, :, q, :, 0:1], Sxv[:, :, q, :, NJ:NJ + 1], r=[("Sx", q)], w=[("Sx", q)], eng="gpsimd")
            ssm.append(lambda: st_out(0))
            ssm.append(lambda: st_out(1))

            def st_glu():
                for mo in range(2):
                    bk = 6 + mo
                    for kc in range(2):
                        self.mm(ps[bk][:, :], glu[:, kc, 128 * mo:128 * mo + 128], yg[:, kc, :], start=(kc == 0), stop=(kc == 1),
                                r=["glu", ("yg", kc)], w=[("ps", bk)])
                    self.act(sig, ps[bk][:, :], AF.Sigmoid, r=[("ps", bk), "par_b_glu"], w=["sig"], bias=bglu[:, mo:mo + 1])
                    self.tt(ycat[:, 6 + mo, :], yg[:, mo, :], sig, ALU.mult, r=[("yg", mo), "sig"], w=[("ycat", 6 + mo)])
            ssm.append(st_glu)
            st_v_, rounds, out0_, out1_, glu_ = ssm[0], ssm[1:1 + NRS], ssm[1 + NRS], ssm[2 + NRS], ssm[3 + NRS]
            st_v_()
            pool_mm()
            for _ in range(32):
                conv.pop(0)()
            for k in range(3):
                rounds[k]()
                for _ in range(11):
                    if conv:
                        conv.pop(0)()
            while conv:
                conv.pop(0)()
            conf_stats_pe()
            for k in range(3, NRS):
                rounds[k]()
            conf_norm_a()
            out0_()
            out1_()
            conf_norm_b()
            glu_()

        def conf_stats_pe():
            for j in range(2):
                self.mm(ps[0][:, :], self.ones_cfb, accb[:, j, :], start=(j == 0), stop=(j == 1),
                        r=["ones_cfb", ("accb", j)], w=[("ps", 0)])
            for j in range(2):
                self.mm(ps[1][:, :], self.ones_cfb, sqb[:, j, :], start=(j == 0), stop=(j == 1),
                        r=["ones_cfb", ("sqb", j)], w=[("ps", 1)])

        def conf_norm_a():
            self.stats(ps[0], ("ps", 0), ps[1], ("ps", 1), st, LN_EPS)
            for j in range(2):
                ak = ("scr0", j)
                self.tt(scr0[:, j, 0:T], scr0[:, j, 0:T], st[0], ALU.subtract, r=[ak, "st_mean"], w=[ak])
                self.tt(scr0[:, j, 0:T], scr0[:, j, 0:T], st[2], ALU.mult, r=[ak, "st_rstd"], w=[ak])

        def conf_norm_b():
            for j in range(2):
                ak = ("scr0", j)
                self.act(ycat[:, 4 + j, :], scr0[:, j, 0:T], AF.Silu, r=[ak, "par_cf_g", "par_cf_beta"], w=[("ycat", 4 + j)],
                         bias=cfbeta[:, j:j + 1], scale=cfg[:, j:j + 1])

        def tail_wo(i):
            for mp in range(0, KC, 2):
                for mo in (mp, mp + 1):
                    bk = mo % 2
                    for kc in range(6):
                        self.mm(ps[bk][:, :], w_o[:, kc, 128 * mo:128 * mo + 128], ycat[:, kc, :],
                                start=(kc == 0), stop=False, r=["w_o", ("ycat", kc)], w=[("ps", bk)])
                for mo in (mp, mp + 1):
                    bk = mo % 2
                    for kc in (6, 7):
                        self.mm(ps[bk][:, :], w_o[:, kc, 128 * mo:128 * mo + 128], ycat[:, kc, :],
                                start=False, stop=(kc == 7), r=["w_o", ("ycat", kc)], w=[("ps", bk)])
                    self.stt(xt[:, mo, :], ps[bk][:, :], mods[:, 16 + mo:17 + mo], xt[:, mo, :], ALU.mult, ALU.add,
                             r=[("ps", bk), "mods", ("x", mo)], w=[("x", mo)])

        def ln_a(i):
            b = i % 2
            for c in range(KC):
                self.act(hb2[b][:, c, :], xt[:, c, :], AF.Copy, r=[("x", c)], w=[("hb", b, c)])
                self.act(ycat[:, c, :], xt[:, c, :], AF.Square, r=[("x", c)], w=[("ycat", c)])

        def ln_b1(i):
            b = i % 2
            for c in range(KC):
                self.mm(ps[0][:, :], self.ones_ln, hb2[b][:, c, :], start=(c == 0), stop=(c == KC - 1),
                        r=["ones_ln", ("hb", b, c)], w=[("ps", 0)])
            for c in range(KC):
                self.mm(ps[1][:, :], self.ones_ln, ycat[:, c, :], start=(c == 0), stop=(c == KC - 1),
                        r=["ones_ln", ("ycat", c)], w=[("ps", 1)])
            self.stats(ps[0], ("ps", 0), ps[1], ("ps", 1), st, eps1)
            for c in range(KC):
                self.tt(xt[:, c, :], xt[:, c, :], st[0], ALU.subtract, r=[("x", c), "st_mean"], w=[("x", c)])
                self.tt(xt[:, c, :], xt[:, c, :], st[2], ALU.mult, r=[("x", c), "st_rstd"], w=[("x", c)])

        def ln_b2(i):
            b = i % 2
            for c in range(KC):
                self.act(xt[:, c, :], xt[:, c, :], AF.Identity, r=[("x", c), "par_ln1"], w=[("x", c)],
                         bias=ln1[:, 8 + c:9 + c], scale=ln1[:, c:c + 1])
            self.emit_h(hb2[b], [("hb", b, c) for c in range(KC)], xt, 32, 24, mods, hdst, i * T)
            self.dma(dstv[:, :, i * T:(i + 1) * T], xt, "xout", r=xk)

        head_load(0)
        xload(0)
        inproj(0)
        conv_h()
        pool(0)
        sconv()
        for it in range(NT):
            body(it, conv_thunks(), None)
            if it + 1 < NT:
                head_load(it + 1)
            tail_wo(it)
            ln_a(it)
            if it + 1 < NT:
                inproj(it + 1, zorder[0:2])
            ln_b1(it)
            if it + 1 < NT:
                inproj(it + 1, zorder[2:])
                ln_b2(it)
                xload(it + 1)
                conv_h()
                pool(it + 1)
                sconv()
            else:
                ln_b2(it)

    def phase_f(self, l, src, dst, hsrc, hdst):
        A, NT = self.A, self.NT
        ps = self.ps
        din = self.din
        mods = self.mods[:, l, :]
        wg = A.bf16(KC, DFF); wu = A.bf16(KC, DFF); wd = A.bf16(FC, D)
        ln2 = A.f32(16)
        self.dma(ln2, din["ln2"][l], "c0", w=["par_ln"])
        HW = DFF // 2
        stgs = [A.f32(HW), A.f32(HW)]
        jobs = []
        for hf in range(2):
            for kc in range(KC):
                jobs.append((wg[:, kc, hf * HW:(hf + 1) * HW], din["w_gate"][l, 128 * kc:128 * kc + 128, hf * HW:(hf + 1) * HW], ("wg", kc, hf)))
                jobs.append((wu[:, kc, hf * HW:(hf + 1) * HW], din["w_up"][l, 128 * kc:128 * kc + 128, hf * HW:(hf + 1) * HW], ("wu", kc, hf)))
        for f in range(FC):
            jobs.append((wd[:, f, :], din["w_down"][l, 128 * f:128 * f + 128, :], ("wd", f)))
        xt = A.f32(KC, T); hb2 = [A.bf16(KC, T), A.bf16(KC, T)]; gb = A.bf16(FC, T)
        sgt = [A.bf16(T), A.bf16(T)]
        st_a = A.f32(T); st_b = A.f32(T)
        st = (st_a, st_b, st_b)
        srcv = src.rearrange("(c p) t -> p c t", p=128)
        dstv = dst.rearrange("(c p) t -> p c t", p=128)
        hsrcv = hsrc.rearrange("(c p) t -> p c t", p=128)
        xk = [("x", c) for c in range(KC)]
        eps2 = LN_EPS / (self.alpha ** 2)
        last = (l == self.L - 1)
        nmods = None if last else self.mods[:, l + 1, :]
        self.rot = 0
        RS = FC - KC

        def head_load(i):
            b = i % 2
            self.dma(hb2[b], hsrcv[:, :, i * T:(i + 1) * T], f"hin{b}", w=[("hb", b, c) for c in range(KC)])

        def xload(i):
            self.dma(xt, srcv[:, :, i * T:(i + 1) * T], "xin", w=xk)

        def gu(i, fs):
            b = i % 2
            for f in fs:
                bg = self.rot % 2
                bu_ = 2 + self.rot % 2
                self.rot += 1
                hf = (128 * f) // HW
                for kc in range(KC):
                    self.mm(ps[bg][:, :], wg[:, kc, 128 * f:128 * f + 128], hb2[b][:, kc, :], start=(kc == 0), stop=(kc == KC - 1),
                            r=[("wg", kc, hf), ("hb", b, kc)], w=[("ps", bg)])
                for kc in range(KC):
                    self.mm(ps[bu_][:, :], wu[:, kc, 128 * f:128 * f + 128], hb2[b][:, kc, :], start=(kc == 0), stop=(kc == KC - 1),
                            r=[("wu", kc, hf), ("hb", b, kc)], w=[("ps", bu_)])
                sk = ("sgt", f % 2)
                self.act(sgt[f % 2], ps[bg][:, :], AF.Silu, r=[("ps", bg)], w=[sk])
                self.tt(gb[:, f, :], sgt[f % 2], ps[bu_][:, :], ALU.mult, r=[sk, ("ps", bu_)], w=[("gb", f)])

        def down(i):
            for mo in range(KC):
                bk = 4 + mo % 2
                for f in range(FC):
                    self.mm(ps[bk][:, :], wd[:, f, 128 * mo:128 * mo + 128], gb[:, f, :], start=(f == 0), stop=(f == FC - 1),
                            r=[("wd", f), ("gb", f)], w=[("ps", bk)])
                self.stt(xt[:, mo, :], ps[bk][:, :], mods[:, 40 + mo:41 + mo], xt[:, mo, :], ALU.mult, ALU.add,
                         r=[("ps", bk), "mods", ("x", mo)], w=[("x", mo)])

        def ln_a(i):
            b = i % 2
            for c in range(KC):
                self.act(hb2[b][:, c, :], xt[:, c, :], AF.Copy, r=[("x", c)], w=[("hb", b, c)])
                self.act(gb[:, RS + c, :], xt[:, c, :], AF.Square, r=[("x", c)], w=[("gb", RS + c)])

        def ln_b1(i):
            b = i % 2
            for c in range(KC):
                self.mm(ps[6][:, :], self.ones_ln, hb2[b][:, c, :], start=(c == 0), stop=(c == KC - 1),
                        r=["ones_ln", ("hb", b, c)], w=[("ps", 6)])
            for c in range(KC):
                self.mm(ps[7][:, :], self.ones_ln, gb[:, RS + c, :], start=(c == 0), stop=(c == KC - 1),
                        r=["ones_ln", ("gb", RS + c)], w=[("ps", 7)])
            self.stats(ps[6], ("ps", 6), ps[7], ("ps", 7), st, eps2)
            for c in range(KC):
                self.tt(xt[:, c, :], xt[:, c, :], st[0], ALU.subtract, r=[("x", c), "st_mean"], w=[("x", c)])
                self.tt(xt[:, c, :], xt[:, c, :], st[2], ALU.mult, r=[("x", c), "st_rstd"], w=[("x", c)])

        def ln_b2(i):
            b = i % 2
            for c in range(KC):
                self.act(xt[:, c, :], xt[:, c, :], AF.Identity, r=[("x", c), "par_ln"], w=[("x", c)],
                         bias=ln2[:, 8 + c:9 + c], scale=ln2[:, c:c + 1])
            if not last:
                self.emit_h(hb2[b], [("hb", b, c) for c in range(KC)], xt, 8, 0, nmods, hdst, i * T)
            self.dma(dstv[:, :, i * T:(i + 1) * T], xt, "xout", r=xk)

        head_load(0)
        xload(0)
        self.load_cast(jobs, stgs, engs=("gpsimd",))
        gu(0, range(FC))
        for it in range(NT):
            if it + 1 < NT:
                head_load(it + 1)
            down(it)
            ln_a(it)
            if it + 1 < NT:
                gu(it + 1, range(0, 2))
            ln_b1(it)
            if it + 1 < NT:
                gu(it + 1, range(2, 8))
            ln_b2(it)
            if it + 1 < NT:
                xload(it + 1)
                gu(it + 1, range(8, FC))


def _chunked(v, nch):
    return np.ascontiguousarray(v.reshape(nch, 128).T)


def prep_shared(inp, L):
    f = np.float32
    g = lambda k: np.asarray(inp[k], dtype=f)
    out = {}
    out["w_ada"] = np.ascontiguousarray(g("w_ada"))
    out["b_ada"] = np.stack([_chunked(g("b_ada")[l], 48) for l in range(L)])
    out["w_in"] = np.ascontiguousarray(g("w_in"))
    out["b_in"] = np.stack([_chunked(g("b_in")[l], ZC) for l in range(L)])
    scw = g("sc_w")
    out["sc_w"] = np.ascontiguousarray(scw.reshape(L, 3, 2, 128).transpose(0, 3, 2, 1).reshape(L, 128, 6))
    pw = g("pool_w")
    bd = np.zeros((L, 2, 128, 128), f)
    for c in range(2):
        bd[:, c, 0:64, 0:64] = pw[:, 2 * c]
        bd[:, c, 64:128, 64:128] = pw[:, 2 * c + 1]
    out["pool_bd"] = bd
    ch2 = lambda k: np.stack([_chunked(g(k)[l], 2) for l in range(L)])
    out["pool_scale"] = ch2("pool_scale")
    cw = g("cf_dw_w")
    out["cf_w"] = np.ascontiguousarray(cw.reshape(L, 31, 2, 128).transpose(0, 3, 2, 1).reshape(L, 128, 62))
    out["cf_b"] = ch2("cf_dw_b"); out["cf_g"] = ch2("cf_ln_g"); out["cf_beta"] = ch2("cf_ln_b")
    lre, lim, ldt = g("ssm_lam_re"), g("ssm_lam_im"), g("ssm_log_dt")
    def lay_i(a):
        a4 = a.reshape(L, 2, 8, 1, 64)
        a4 = np.broadcast_to(a4, (L, 2, 8, 16, 64))
        return np.ascontiguousarray(a4.transpose(0, 2, 3, 1, 4).reshape(L, 128, 128))
    ldt_b = np.broadcast_to(ldt[:, :, None], (L, 16, 64))
    out["lam_i"] = np.stack([lay_i(lre), lay_i(lim), lay_i(np.ascontiguousarray(ldt_b))], axis=1)
    bre, bim = g("ssm_b_re"), g("ssm_b_im")
    def lay_bt(b):
        b5 = b.reshape(L, 2, 8, 64, 16)
        return np.ascontiguousarray(b5.transpose(0, 2, 4, 1, 3).reshape(L, 128, 128))
    out["bt_i"] = np.stack([lay_bt(bre), lay_bt(bim)], axis=1)
    def lay_ii(a):
        t = a.transpose(0, 2, 1)
        return np.ascontiguousarray(np.concatenate([t, t], axis=1))
    rep = lambda a3: np.ascontiguousarray(np.repeat(a3, 16, axis=2))
    out["lam_ii"] = np.stack([rep(lay_ii(lre)), rep(lay_ii(lim)), rep(lay_ii(np.ascontiguousarray(ldt_b)))], axis=1)
    cre, cim = g("ssm_c_re"), g("ssm_c_im")
    cre_t, cim_t = cre.transpose(0, 3, 1, 2), cim.transpose(0, 3, 1, 2)
    out["c_a"] = np.ascontiguousarray(np.concatenate([cre_t, cim_t], axis=1).reshape(L, 128, 256))
    out["c_sw"] = np.ascontiguousarray(np.concatenate([cim_t, cre_t], axis=1).reshape(L, 128, 256))
    bre_t, bim_t = bre.transpose(0, 2, 1, 3), bim.transpose(0, 2, 1, 3)
    out["b_a"] = np.ascontiguousarray(np.concatenate([bre_t, bim_t], axis=1).reshape(L, 128, 256))
    out["b_sw"] = np.ascontiguousarray(np.concatenate([bim_t, bre_t], axis=1).reshape(L, 128, 256))
    out["ssm_d"] = ch2("ssm_d")
    out["w_glu"] = np.ascontiguousarray(g("ssm_w_glu"))
    out["b_glu"] = ch2("ssm_b_glu")
    out["w_o"] = np.ascontiguousarray(g("w_o"))
    out["ln1"] = np.stack([np.concatenate([_chunked(g("ln1_g")[l], 8), _chunked(g("ln1_b")[l], 8)], axis=1) for l in range(L)])
    out["w_gate"] = np.ascontiguousarray(g("w_gate")); out["w_up"] = np.ascontiguousarray(g("w_up"))
    out["w_down"] = np.ascontiguousarray(g("w_down"))
    out["ln2"] = np.stack([np.concatenate([_chunked(g("ln2_g")[l], 8), _chunked(g("ln2_b")[l], 8)], axis=1) for l in range(L)])
    cm = np.zeros((3, 128, 128), f)
    cm[0] = np.eye(128, dtype=f)
    for k in range(128):
        cm[1, k, (k + 64) % 128] = 1.0
    cm[2] = 1.0 / 256.0
    out["cmat"] = cm
    cvv = np.zeros((128, 40), f)
    cvv[:64, 0] = 1.0; cvv[64:, 0] = -1.0
    par = (np.arange(128) // 16) % 2
    cvv[:, 1] = (par == 0); cvv[:, 2] = (par == 1)
    wins = np.array([[2, 4], [8, 16]])
    for c in range(2):
        for half in range(2):
            w = wins[c, half]
            cvv[64 * half:64 * half + 64, 8 + 16 * c:24 + 16 * c] = 1.0 / np.minimum(np.arange(1, 17), w)
    out["cvecs"] = cvv
    return out


_CACHE = {}


def get_program(S, L):
    if (S, L) not in _CACHE:
        b = Builder(S, L)
        _CACHE[(S, L)] = b.build()
    return _CACHE[(S, L)]


def run(inputs, S, L, n_cores):
    shared = prep_shared(inputs, L)
    x = np.asarray(inputs["x"], dtype=np.float32)
    c = np.asarray(inputs["c"], dtype=np.float32)
    in_maps = []
    for b in range(n_cores):
        m = dict(shared)
        m["x_fm"] = np.ascontiguousarray(x[b].T)
        m["cvec"] = _chunked(c[b], KC)
        in_maps.append(m)
    nc = get_program(S, L)
    res = run_bass_kernel_spmd(nc, in_maps, core_ids=list(range(n_cores)))
    out = np.stack([np.ascontiguousarray(res.results[b]["y_fm"].T) for b in range(n_cores)])
    return out.astype(np.float32)


def kernel(**inputs):
    return run(inputs, 8192, 4, 8)
```
